# Optimizing a Trainium2 kernel written in Bass

```python
import math, functools
import jax, jax.numpy as jnp
from jax import lax
import numpy as np

D_MODEL = 1024
BATCH = 8
SEQ = 2048
DEPTH = 1
DEC_BATCH = 128
DEC_SEQ = 4
PAST_LEN = 16384
PAGE_SIZE = 128

N_META = 16
CHUNK = 64
H_RET = 4
DK_RET = D_MODEL // 8
DV_RET = D_MODEL // 8
H_GDN = 4
DK_GDN = D_MODEL // 8
DV_GDN = D_MODEL // 8
MIX_WIDTH = H_RET * DV_RET + H_GDN * DV_GDN
GDN_QKV = 2 * H_GDN * DK_GDN + H_GDN * DV_GDN
IN_WIDTH = 2 * H_RET * DK_RET + 2 * H_RET * DV_RET + GDN_QKV + H_GDN * DV_GDN + 2 * H_GDN
CONV_GDN = 4
CONV_FFN = 3
D_FF = ((8 * D_MODEL // 3 + 127) // 128) * 128
ROPE_THETA = 10000.0
EPS = 1e-6

kernel_name = 'hybrid_retention_gdn_convffn_step'


def rms_norm(x, w):
    xf = x.astype(jnp.float32)
    y = xf * lax.rsqrt(jnp.mean(xf * xf, axis=-1, keepdims=True) + EPS)
    return (y * w.astype(jnp.float32)).astype(x.dtype)


def head_layer_norm(o, w):
    mu = jnp.mean(o, axis=-1, keepdims=True)
    c = o - mu
    var = jnp.mean(c * c, axis=-1, keepdims=True)
    return c * lax.rsqrt(var + EPS) * w.astype(jnp.float32)


def head_rms_norm(o, w):
    return o * lax.rsqrt(jnp.mean(o * o, axis=-1, keepdims=True) + EPS) * w.astype(jnp.float32)


def l2_normalize(x):
    return x * lax.rsqrt(jnp.sum(x * x, axis=-1, keepdims=True) + EPS)


def rotary(x, pos):
    half = x.shape[-1] // 2
    inv_freq = ROPE_THETA ** (-jnp.arange(half, dtype=jnp.float32) / half)
    ang = pos.astype(jnp.float32)[:, None] * inv_freq[None, :]
    cos = jnp.cos(ang)[None, :, None, :]
    sin = jnp.sin(ang)[None, :, None, :]
    xf = x.astype(jnp.float32)
    x1, x2 = xf[..., :half], xf[..., half:]
    return jnp.concatenate([x1 * cos - x2 * sin, x1 * sin + x2 * cos], axis=-1)


def causal_depthwise_conv(x, buf, w):
    width = w.shape[0]
    length = x.shape[1]
    full = jnp.concatenate([buf.astype(x.dtype), x], axis=1)
    y = full[:, 0:length] * w[0]
    for i in range(1, width):
        y = y + full[:, i:i + length] * w[i]
    return y, full[:, full.shape[1] - (width - 1):]


def retention_log_gamma():
    return jnp.log1p(-jnp.power(2.0, -5.0 - jnp.arange(H_RET, dtype=jnp.float32)))


def to_chunks(x, chunk):
    b, h, l = x.shape[:3]
    return x.reshape((b, h, l // chunk, chunk) + x.shape[3:])


def from_chunks(x):
    b, h, n, c = x.shape[:4]
    return x.reshape((b, h, n * c) + x.shape[4:])


def run_leading_then_chunks(chunk_fn, n_lead, chunk, s0, tensors):
    outs = []
    s = s0
    if n_lead > 0:
        o, s = chunk_fn(*[to_chunks(t[:, :, :n_lead], n_lead) for t in tensors], s)
        outs.append(from_chunks(o))
    o, s = chunk_fn(*[to_chunks(t[:, :, n_lead:], chunk) for t in tensors], s)
    outs.append(from_chunks(o))
    return jnp.concatenate(outs, axis=2), s


def retention_chunked(q, k, v, s0, log_gamma):
    c = q.shape[3]
    idx = jnp.arange(c, dtype=jnp.float32)
    lg = log_gamma[:, None]
    diff = idx[:, None] - idx[None, :]
    dec_intra = jnp.exp(jnp.where(diff[None] >= 0, lg[:, :, None] * diff[None], -jnp.inf))
    q_dec = jnp.exp(lg * (idx + 1.0))[None, :, None, :, None]
    k_dec = jnp.exp(lg * (c - 1.0 - idx))[None, :, None, :, None]
    chunk_dec = jnp.exp(log_gamma * c)[None, :, None, None]
    scores = jnp.einsum('bhncd,bhnsd->bhncs', q, k) * dec_intra[None, :, None]
    o_intra = jnp.einsum('bhncs,bhnse->bhnce', scores, v)
    kv = jnp.einsum('bhncd,bhnce->bhnde', k * k_dec, v)
    qd = q * q_dec

    def step(s, inp):
        qn, kvn = inp
        o = jnp.einsum('bhcd,bhde->bhce', qn, s)
        return chunk_dec * s + kvn, o

    s_fin, o_cross = lax.scan(step, s0, (jnp.moveaxis(qd, 2, 0), jnp.moveaxis(kv, 2, 0)))
    return o_intra + jnp.moveaxis(o_cross, 0, 2), s_fin


def gdn_chunked(q, k, v, g, beta, s0):
    c = q.shape[3]
    cum = jnp.cumsum(g, axis=-1)
    ii = jnp.arange(c)
    tri = ii[:, None] >= ii[None, :]
    strict = ii[:, None] > ii[None, :]
    gdiff = cum[..., :, None] - cum[..., None, :]
    dmask = jnp.exp(jnp.where(tri, gdiff, -jnp.inf))
    kk = jnp.einsum('bhncd,bhnsd->bhncs', k, k)
    a = jnp.where(strict, beta[..., :, None] * kk * dmask, 0.0)
    eye = jnp.eye(c, dtype=q.dtype)
    rhs = jnp.concatenate([v * beta[..., None], k * (beta * jnp.exp(cum))[..., None]], axis=-1)
    sol = lax.linalg.triangular_solve(a + eye, rhs, left_side=True, lower=True, unit_diagonal=True)
    u, wk = sol[..., :v.shape[-1]], sol[..., v.shape[-1]:]
    qk = jnp.where(tri, jnp.einsum('bhncd,bhnsd->bhncs', q, k) * dmask, 0.0)
    q_dec = q * jnp.exp(cum)[..., None]
    k_dec = k * jnp.exp(cum[..., -1:] - cum)[..., None]
    chunk_dec = jnp.exp(cum[..., -1])

    def step(s, inp):
        un, wkn, qdn, qkn, kdn, cdn = inp
        w = un - jnp.einsum('bhcd,bhde->bhce', wkn, s)
        o = jnp.einsum('bhcd,bhde->bhce', qdn, s) + jnp.einsum('bhcs,bhse->bhce', qkn, w)
        s = cdn[:, :, None, None] * s + jnp.einsum('bhcd,bhce->bhde', kdn, w)
        return s, o

    xs = tuple(jnp.moveaxis(t, 2, 0) for t in (u, wk, q_dec, qk, k_dec, chunk_dec))
    s_fin, o = lax.scan(step, s0, xs)
    return jnp.moveaxis(o, 0, 2), s_fin


def decoder_layer(x, pos, n_lead, chunk, s_ret, s_gdn, buf_qkv, buf_ffn,
                  norm_mix, w_in, conv_gdn, gdn_a_log, gdn_dt_bias, norm_ret, norm_gdn, w_out,
                  norm_ffn, w_up, conv_ffn, w_down):
    b, l, _ = x.shape
    f32 = jnp.float32
    h = rms_norm(x, norm_mix)
    proj = h @ w_in
    sizes = [H_RET * DK_RET, H_RET * DK_RET, H_RET * DV_RET, H_RET * DV_RET, GDN_QKV, H_GDN * DV_GDN, H_GDN]
    offsets = [int(o) for o in np.cumsum(sizes)]
    rq, rk, rv, rg, qkv, gg, gb, ga = jnp.split(proj, offsets, axis=-1)

    def to_heads(t, nh):
        return jnp.swapaxes(t.reshape(b, l, nh, -1), 1, 2).astype(f32)

    rq_h = jnp.swapaxes(rotary(rq.reshape(b, l, H_RET, DK_RET), pos), 1, 2)
    rk_h = jnp.swapaxes(rotary(rk.reshape(b, l, H_RET, DK_RET), pos), 1, 2) * (DK_RET ** -0.5)
    rv_h = to_heads(rv, H_RET)
    ret_fn = functools.partial(retention_chunked, log_gamma=retention_log_gamma())
    o_ret, s_ret_new = run_leading_then_chunks(ret_fn, n_lead, chunk, s_ret.astype(f32), [rq_h, rk_h, rv_h])
    o_ret = head_layer_norm(jnp.swapaxes(o_ret, 1, 2), norm_ret.reshape(H_RET, DV_RET))
    o_ret = o_ret.reshape(b, l, H_RET * DV_RET).astype(x.dtype) * jax.nn.silu(rg)

    qkv, buf_qkv_new = causal_depthwise_conv(qkv, buf_qkv, conv_gdn)
    qkv = jax.nn.silu(qkv)
    gq, gk, gv = jnp.split(qkv, [H_GDN * DK_GDN, 2 * H_GDN * DK_GDN], axis=-1)
    gq_h = l2_normalize(to_heads(gq, H_GDN)) * (DK_GDN ** -0.5)
    gk_h = l2_normalize(to_heads(gk, H_GDN))
    gv_h = to_heads(gv, H_GDN)
    beta = jnp.swapaxes(jax.nn.sigmoid(gb.astype(f32)), 1, 2)
    decay = -jnp.exp(gdn_a_log.astype(f32)) * jax.nn.softplus(ga.astype(f32) + gdn_dt_bias.astype(f32))
    decay = jnp.swapaxes(decay, 1, 2)
    o_gdn, s_gdn_new = run_leading_then_chunks(gdn_chunked, n_lead, chunk, s_gdn.astype(f32), [gq_h, gk_h, gv_h, decay, beta])
    o_gdn = head_rms_norm(jnp.swapaxes(o_gdn, 1, 2), norm_gdn)
    o_gdn = o_gdn.reshape(b, l, H_GDN * DV_GDN).astype(x.dtype) * jax.nn.silu(gg)

    x = x + jnp.concatenate([o_ret, o_gdn], axis=-1) @ w_out

    u, buf_ffn_new = causal_depthwise_conv(rms_norm(x, norm_ffn) @ w_up, buf_ffn, conv_ffn)
    gate, val = jnp.split(u, 2, axis=-1)
    x = x + (jax.nn.silu(gate) * val) @ w_down
    return x, s_ret_new, s_gdn_new, buf_qkv_new, buf_ffn_new


def setup_inputs(seed: int = 0) -> dict:
    key = jax.random.key(seed)
    ks = jax.random.split(key, 24)
    f32 = jnp.float32
    n = lambda k, s, sc: jax.random.normal(k, s, f32) * sc
    dt = jnp.exp(jax.random.uniform(ks[10], (DEPTH, H_GDN), f32, math.log(0.001), math.log(0.1)))
    return {
        'x_prompt': n(ks[0], (BATCH, SEQ, D_MODEL), 1.0),
        'x_sample': n(ks[1], (DEC_BATCH, DEC_SEQ, D_MODEL), 1.0),
        'state_ret': n(ks[2], (DEPTH, DEC_BATCH, H_RET, DK_RET, DV_RET), 1.0),
        'state_gdn': n(ks[3], (DEPTH, DEC_BATCH, H_GDN, DK_GDN, DV_GDN), DK_GDN ** -0.5),
        'state_conv_qkv': n(ks[4], (DEPTH, DEC_BATCH, CONV_GDN - 1, GDN_QKV), 1.0),
        'state_ffn_conv': n(ks[5], (DEPTH, DEC_BATCH, CONV_FFN - 1, 2 * D_FF), 1.0),
        'meta_tokens': n(ks[6], (N_META, D_MODEL), 1.0),
        'norm_mix': 1.0 + n(ks[7], (DEPTH, D_MODEL), 0.02),
        'w_in': n(ks[8], (DEPTH, D_MODEL, IN_WIDTH), D_MODEL ** -0.5),
        'conv_gdn': n(ks[9], (DEPTH, CONV_GDN, GDN_QKV), CONV_GDN ** -0.5),
        'gdn_a_log': jnp.log(jax.random.uniform(ks[11], (DEPTH, H_GDN), f32, 1.0, 16.0)),
        'gdn_dt_bias': dt + jnp.log(-jnp.expm1(-dt)),
        'norm_ret': 1.0 + n(ks[12], (DEPTH, H_RET * DV_RET), 0.02),
        'norm_gdn': 1.0 + n(ks[13], (DEPTH, DV_GDN), 0.02),
        'w_out': n(ks[14], (DEPTH, MIX_WIDTH, D_MODEL), MIX_WIDTH ** -0.5),
        'norm_ffn': 1.0 + n(ks[15], (DEPTH, D_MODEL), 0.02),
        'w_up': n(ks[16], (DEPTH, D_MODEL, 2 * D_FF), D_MODEL ** -0.5),
        'conv_ffn': n(ks[17], (DEPTH, CONV_FFN, 2 * D_FF), CONV_FFN ** -0.5),
        'w_down': n(ks[18], (DEPTH, D_FF, D_MODEL), D_FF ** -0.5),
        'norm_final': 1.0 + n(ks[19], (D_MODEL,), 0.02),
    }


def reference(x_prompt, x_sample, state_ret, state_gdn, state_conv_qkv, state_ffn_conv,
              meta_tokens, norm_mix, w_in, conv_gdn, gdn_a_log, gdn_dt_bias, norm_ret, norm_gdn, w_out,
              norm_ffn, w_up, conv_ffn, w_down, norm_final):
    bp = x_prompt.shape[0]
    meta = jnp.broadcast_to(meta_tokens[None].astype(x_prompt.dtype), (bp, N_META, D_MODEL))
    xp = jnp.concatenate([meta, x_prompt], axis=1)
    xs = x_sample
    pos_p = jnp.arange(xp.shape[1], dtype=jnp.int32)
    pos_s = PAST_LEN + jnp.arange(xs.shape[1], dtype=jnp.int32)
    p_ret, p_gdn, p_cq, p_cf = [], [], [], []
    s_ret_l, s_gdn_l, s_cq, s_cf = [], [], [], []
    for layer in range(DEPTH):
        params = (norm_mix[layer], w_in[layer], conv_gdn[layer], gdn_a_log[layer], gdn_dt_bias[layer],
                  norm_ret[layer], norm_gdn[layer], w_out[layer], norm_ffn[layer], w_up[layer],
                  conv_ffn[layer], w_down[layer])
        z_ret = jnp.zeros((bp, H_RET, DK_RET, DV_RET), jnp.float32)
        z_gdn = jnp.zeros((bp, H_GDN, DK_GDN, DV_GDN), jnp.float32)
        z_cq = jnp.zeros((bp, CONV_GDN - 1, GDN_QKV), xp.dtype)
        z_cf = jnp.zeros((bp, CONV_FFN - 1, 2 * D_FF), xp.dtype)
        xp, a, bq, c, d = decoder_layer(xp, pos_p, N_META, CHUNK, z_ret, z_gdn, z_cq, z_cf, *params)
        p_ret.append(a); p_gdn.append(bq); p_cq.append(c); p_cf.append(d)
        xs, a, bq, c, d = decoder_layer(xs, pos_s, 0, xs.shape[1], state_ret[layer], state_gdn[layer],
                                        state_conv_qkv[layer], state_ffn_conv[layer], *params)
        s_ret_l.append(a); s_gdn_l.append(bq); s_cq.append(c); s_cf.append(d)
    y_prompt = rms_norm(xp[:, N_META:], norm_final)
    y_sample = rms_norm(xs, norm_final)
    return (y_prompt, y_sample,
            jnp.stack(p_ret), jnp.stack(p_gdn), jnp.stack(p_cq), jnp.stack(p_cf),
            jnp.stack(s_ret_l), jnp.stack(s_gdn_l), jnp.stack(s_cq), jnp.stack(s_cf))
```

```python
import contextlib
import numpy as np
import concourse.bass as bass
import concourse.mybir as mybir
from concourse.bass_utils import run_bass_kernel_spmd

F32 = mybir.dt.float32
BF16 = mybir.dt.bfloat16
AF = mybir.ActivationFunctionType
ALU = mybir.AluOpType
AX = mybir.AxisListType

D = 1024
NMETA = 16
SEQ = 2048
LP = NMETA + SEQ
NS = 16
DS = 4
NROW = LP + NS * DS
H = 4
DH = 128
INW = 4104
DFF = 2816
EPS = 1e-6
PAST = 16384
GAM = [1.0 - 2.0 ** (-5 - h) for h in range(H)]
QSCALE = DH ** -0.5
NEG = -1.0e5


class _Op:
    __slots__ = ("fn", "waits", "signal", "dma", "idx")

    def __init__(self, fn, waits, dma):
        self.fn = fn
        self.waits = waits
        self.signal = False
        self.dma = dma
        self.idx = None


class Plan:
    ENGS = ("pe", "act", "dve", "pool", "sp")

    def __init__(self, same_engine_sync=True):
        self.streams = {e: [] for e in self.ENGS}
        self.last_write = {}
        self.readers = {}
        self.by_root = {}
        self.dma_count = {}
        self.waited = {e: {} for e in self.ENGS}
        self.same_engine_sync = same_engine_sync
        self.barrier_toks = []
        self.rec = None
        self.t_eng = {}
        self.t_tok = {}

    @staticmethod
    def _k(r):
        return r if isinstance(r, tuple) else (r,)

    def _conf(self, key):
        for k in self.by_root.get(key[0], ()):
            n = min(len(k), len(key))
            if k[:n] == key[:n]:
                yield k

    def _deps(self, reads, writes, eng=None):
        toks = list(self.barrier_toks)
        for r in reads:
            key = self._k(r)
            psum = isinstance(key[0], str) and key[0].startswith("ps") and key[0][2:].isdigit()
            for k in self._conf(key):
                t = self.last_write.get(k)
                if t is not None:
                    toks.append(t)
                if psum:
                    toks.extend(t2 for t2 in self.readers.get(k, ()) if not (t2[0] == "eng" and t2[1] == eng))
        for w in writes:
            key = self._k(w)
            for k in self._conf(key):
                t = self.last_write.get(k)
                if t is not None:
                    toks.append(t)
                toks.extend(self.readers.get(k, ()))
        return toks

    def _commit(self, tok, reads, writes):
        for w in writes:
            key = self._k(w)
            for k in list(self._conf(key)):
                if len(k) >= len(key):
                    self.readers.pop(k, None)
                    if k != key:
                        self.last_write.pop(k, None)
            self.by_root.setdefault(key[0], set()).add(key)
            self.last_write[key] = tok
        for r in reads:
            key = self._k(r)
            self.by_root.setdefault(key[0], set()).add(key)
            self.readers.setdefault(key, []).append(tok)

    def _filter(self, eng, toks):
        wd = self.waited[eng]
        best = {}
        for t in toks:
            kind, who, val = t
            if kind == "eng" and who == eng:
                if eng in ("pe", "sp") or not self.same_engine_sync:
                    continue
            key = (kind, who)
            if wd.get(key, -1) >= val:
                continue
            if key not in best or best[key][2] < val:
                best[key] = t
        for key, t in best.items():
            wd[key] = t[2]
        return list(best.values())

    def op(self, eng, fn, reads=(), writes=()):
        if self.rec is not None:
            self.rec.append(("op", eng, fn, None, tuple(reads), tuple(writes)))
            return None
        toks = self._filter(eng, self._deps(reads, writes, eng))
        o = _Op(fn, toks, None)
        o.idx = len(self.streams[eng])
        self.streams[eng].append(o)
        tok = ("eng", eng, o.idx)
        self._commit(tok, reads, writes)
        return tok

    def dma(self, eng, fn, semkey, reads=(), writes=()):
        if self.rec is not None:
            self.rec.append(("dma", eng, fn, semkey, tuple(reads), tuple(writes)))
            return None
        toks = self._filter(eng, self._deps(reads, writes))
        cnt = self.dma_count.get(semkey, 0) + 16
        self.dma_count[semkey] = cnt
        o = _Op(fn, toks, (semkey, cnt))
        o.idx = len(self.streams[eng])
        self.streams[eng].append(o)
        tok = ("dma", semkey, cnt)
        self._commit(tok, reads, writes)
        return tok

    def merge(self, lists):
        lists = [l for l in lists if l]
        heads = [0] * len(lists)
        rem = []
        for l in lists:
            acc, suf = 0.0, [0.0] * (len(l) + 1)
            for i_ in range(len(l) - 1, -1, -1):
                acc += getattr(l[i_][2], "cost", 0.3) + 0.1
                suf[i_] = acc
            rem.append(suf)
        while True:
            best = None
            for li, l in enumerate(lists):
                if heads[li] >= len(l):
                    continue
                kind, eng, fn, semkey, reads, writes = l[heads[li]]
                toks = self._deps(reads, writes, eng)
                ready = 0.0
                for t in toks:
                    tt_ = self.t_tok.get(t)
                    if tt_ is not None and tt_ > ready:
                        ready = tt_
                start = max(self.t_eng.get(eng, 0.0), ready)
                cand = (start, -rem[li][heads[li]], li)
                if best is None or cand < best:
                    best = cand
            if best is None:
                break
            start, _, li = best
            kind, eng, fn, semkey, reads, writes = lists[li][heads[li]]
            heads[li] += 1
            if kind == "op":
                tok = self.op(eng, fn, reads, writes)
                cost = getattr(fn, "cost", 0.3)
                self.t_eng[eng] = start + cost
                self.t_tok[tok] = start + cost + 0.10
            else:
                tok = self.dma(eng, fn, semkey, reads, writes)
                self.t_eng[eng] = start + 0.1
                self.t_tok[tok] = start + 2.5

    def fix_group(self, semkey):
        fin = self.dma_count[semkey]
        for k, t in list(self.last_write.items()):
            if t[0] == "dma" and t[1] == semkey:
                self.last_write[k] = ("dma", semkey, fin)
        for k, lst in self.readers.items():
            self.readers[k] = [("dma", semkey, fin) if (t[0] == "dma" and t[1] == semkey) else t for t in lst]

    def barrier(self):
        toks = []
        for e in ("pe", "act", "dve", "pool"):
            for o in reversed(self.streams[e]):
                if o.dma is None and o.fn is not None:
                    toks.append(("eng", e, o.idx))
                    break
        toks += [("dma", k, c) for k, c in self.dma_count.items()]
        self.barrier_toks = toks

    def finalize(self):
        toks = [("dma", k, c) for k, c in self.dma_count.items()]
        toks = self._filter("sp", toks)
        o = _Op(None, toks, None)
        o.idx = len(self.streams["sp"])
        self.streams["sp"].append(o)

    def emit(self, nc):
        for e in self.ENGS:
            for o in self.streams[e]:
                for (kind, who, val) in o.waits:
                    if kind == "eng":
                        self.streams[who][val].signal = True
        semval = {}
        for e in self.ENGS:
            c = 0
            for o in self.streams[e]:
                if o.signal:
                    c += 1
                    semval[(e, o.idx)] = c
        with contextlib.ExitStack() as st:
            esem = {e: st.enter_context(nc.semaphore("s_" + e)) for e in self.ENGS}
            dsem = {k: st.enter_context(nc.semaphore("d_%d" % i)) for i, k in enumerate(self.dma_count)}
            block = st.enter_context(nc.Block())

            def run(engname):
                def body(e):
                    for o in self.streams[engname]:
                        for (kind, who, val) in o.waits:
                            if kind == "eng":
                                e.wait_ge(esem[who], semval[(who, val)])
                            else:
                                e.wait_ge(dsem[who], val)
                        if o.fn is None:
                            continue
                        ins = o.fn(e)
                        if o.dma is not None:
                            ins.then_inc(dsem[o.dma[0]], 16)
                        elif o.signal:
                            ins.then_inc(esem[engname], 1)
                return body

            block.tensor(run("pe"))
            block.scalar(run("act"))
            block.vector(run("dve"))
            block.gpsimd(run("pool"))
            block.sync(run("sp"))
        return {e: len(self.streams[e]) for e in self.ENGS}


class Ins:
    def __init__(self, fn, cost):
        self.fn = fn
        self.cost = cost

    def __call__(self, e):
        return self.fn(e)


def _fsz(ap):
    n = 1
    for d in tuple(ap.shape)[1:]:
        n *= int(d)
    return n


def MM(out, lhsT, rhs, start=True, stop=True, skip=False):
    cost = max(_fsz(out) * (2 if lhsT.dtype == F32 else 1) / 2400.0, 0.107) + 0.01
    if skip:
        return Ins(lambda e: e.matmul(out, lhsT=lhsT, rhs=rhs, start=start, stop=stop, skip_group_check=True), cost)
    return Ins(lambda e: e.matmul(out, lhsT=lhsT, rhs=rhs, start=start, stop=stop), cost)


def TR(out, in_, idt):
    return Ins(lambda e: e.transpose(out, in_, idt), max(_fsz(out) * (2 if in_.dtype == F32 else 1) / 2400.0, 0.107) + 0.01)


def ACT(out, in_, func, **kw):
    return Ins(lambda e: e.activation(out, in_, func, **kw), _fsz(out) / 1000.0 + 0.12)


def TT(out, in0, in1, op):
    return Ins(lambda e: e.tensor_tensor(out=out, in0=in0, in1=in1, op=op), _fsz(out) / 1000.0 + 0.07)


def TS(out, in0, s1, s2, op0, op1):
    return Ins(lambda e: e.tensor_scalar(out, in0, s1, s2, op0=op0, op1=op1), _fsz(out) / 1000.0 + 0.07)


def STT(out, in0, scalar, in1, op0, op1):
    return Ins(lambda e: e.scalar_tensor_tensor(out=out, in0=in0, scalar=scalar, in1=in1, op0=op0, op1=op1), _fsz(out) / 1000.0 + 0.07)


def RED(out, in_):
    return Ins(lambda e: e.tensor_reduce(out, in_, axis=AX.X, op=ALU.add), _fsz(in_) / 1000.0 + 0.07)


def CP(out, in_):
    return Ins(lambda e: e.tensor_copy(out, in_), _fsz(out) / 1000.0 + 0.07)


def RCP(out, in_):
    return Ins(lambda e: e.reciprocal(out, in_), _fsz(out) / 1000.0 + 0.07)


def MSET(ap, v):
    return Ins(lambda e: e.memset(ap, v), _fsz(ap) / 1000.0 + 0.07)


def DMA(out, in_):
    return Ins(lambda e: e.dma_start(out=out, in_=in_), 2.0)


def _tile_types():
    return {"meta": (16, 16), "prm": (128, 128), "smp": (64, 4)}


def host_constants():
    c = {}
    c["ident"] = np.eye(128, dtype=np.float32)
    pos = np.concatenate([np.arange(LP), np.tile(PAST + np.arange(DS), NS)]).astype(np.float32)
    cl = np.concatenate([np.arange(NMETA), np.tile(np.arange(128), SEQ // 128), np.tile(np.arange(DS), NS)])
    cn = np.concatenate([np.full(NMETA, NMETA), np.full(SEQ, 128), np.full(NS * DS, DS)])
    half = DH // 2
    inv_freq = np.power(np.float32(10000.0), -np.arange(half, dtype=np.float32) / np.float32(half)).astype(np.float32)
    ang = (pos[:, None] * inv_freq[None, :]).astype(np.float32)
    cos = np.cos(ang).astype(np.float32).astype(np.float64)
    sin = np.sin(ang).astype(np.float32).astype(np.float64)
    g = np.array(GAM, dtype=np.float64)
    qdec = g[None, :] ** (cl[:, None] + 1.0)
    kdec = g[None, :] ** (cn[:, None] - 1.0 - cl[:, None])
    rope = np.zeros((NROW, 644), np.float64)
    rope[:, 0:256] = (cos[:, None, :] * qdec[:, :, None]).reshape(NROW, 256)
    rope[:, 256:512] = (sin[:, None, :] * qdec[:, :, None]).reshape(NROW, 256)
    rope[:, 512:576] = cos * QSCALE
    rope[:, 576:640] = sin * QSCALE
    rope[:, 640:644] = kdec
    c["rope"] = rope.astype(np.float32)
    for name, (n, sl) in _tile_types().items():
        idx = np.arange(n)
        seq = idx // sl
        loc = idx % sl
        same = seq[:, None] == seq[None, :]
        ok = same & (idx[None, :] >= idx[:, None])
        mr = np.zeros((n, H, n), np.float64)
        for h in range(H):
            mr[:, h, :] = np.where(ok, (g[h] ** (-(loc[:, None] + 1.0))), 0.0)
        c["mr_" + name] = mr.astype(np.float32)
        c["nma_" + name] = np.where(same & (idx[:, None] > idx[None, :]), 0.0, NEG).astype(np.float32)
        c["nmq_" + name] = np.where(same & (idx[None, :] >= idx[:, None]), 0.0, NEG).astype(np.float32)
        c["tri_" + name] = (same & (idx[:, None] <= idx[None, :])).astype(np.float32)
        c["blk_" + name] = same.astype(np.float32)
    n = NS * DS
    seq = np.arange(n) // DS
    c["mfm"] = np.broadcast_to((np.arange(NS)[:, None] == seq[None, :]).astype(np.float32)[None], (128, NS, n)).copy()
    c["mtm"] = (seq[:, None] == np.arange(NS)[None, :]).astype(np.float32)
    return c


CONST_SHAPES = None


def build_program():
    nc = bass.Bass("TRN2", target_bir_lowering=False)
    consts = host_constants()

    def din(name, shape):
        return nc.dram_tensor(name, list(shape), F32, kind="ExternalInput").ap()

    def dout(name, shape):
        return nc.dram_tensor(name, list(shape), F32, kind="ExternalOutput").ap()

    x = din("x", [NROW, D])
    sall_in = din("sall", [NS, 2 * H, DH, DH])
    scq_in = din("scq", [NS * 3, 1536])
    scf_in = din("scf", [NS * 2, 2 * DFF])
    w_in = din("w_in", [D, INW])
    w_out = din("w_out", [D, D])
    w_up = din("w_up", [D, 2 * DFF])
    w_down = din("w_down", [DFF, D])
    nm_col = din("nm_col", [128, 8])
    nf_col = din("nf_col", [128, 8])
    cwg_in = din("cwg", [128, 48])
    cwf_in = din("cwf", [128, 132])
    alog_in = din("alog", [4])
    dtb_in = din("dtb", [4])
    nret_in = din("nret", [512])
    ngdn_in = din("ngdn", [128])
    nfin_in = din("nfin", [D])
    cd = {k: din("c_" + k, v.shape) for k, v in consts.items()}

    y = dout("y", [SEQ + NS * DS, D])
    o_sret_p = dout("o_sret_p", [H, DH, DH])
    o_sgdn_p = dout("o_sgdn_p", [H, DH, DH])
    o_cq_p = dout("o_cq_p", [3, 1536])
    o_cf_p = dout("o_cf_p", [2, 2 * DFF])
    o_sall_s = dout("o_sall_s", [NS, 2 * H, DH, DH])
    o_cq_s = dout("o_cq_s", [NS, 3, 1536])
    o_cf_s = dout("o_cf_s", [NS, 2, 2 * DFF])
    x1s = nc.dram_tensor("x1s", [NROW, D], F32, kind="Internal").ap()

    P = Plan()
    pe = lambda fn, r=(), w=(): P.op("pe", fn, r, w)
    act = lambda fn, r=(), w=(): P.op("act", fn, r, w)
    dve = lambda fn, r=(), w=(): P.op("dve", fn, r, w)
    pool = lambda fn, r=(), w=(): P.op("pool", fn, r, w)

    with contextlib.ExitStack() as top:
        ps = [top.enter_context(nc.psum_tensor("ps%d" % i, [128, 512], F32)) for i in range(8)]
        psb = [p.bitcast(BF16) for p in ps]
        PK = lambda i: "ps%d" % i

        with contextlib.ExitStack() as st:
            def T(name, shape, dt=F32):
                return st.enter_context(nc.sbuf_tensor("sb_" + name, list(shape), dt))

            w_in_sb = T("w_in_sb", [128, 8, INW], BF16)
            w_out_sb = T("w_out_sb", [128, 8, D], BF16)
            ident_f = T("ident_f", [128, 128])
            ident_b = T("ident_b", [128, 128], BF16)
            negid = T("negid", [128, 128])
            ones_f = T("ones_f", [128, 128])
            negones = T("negones", [128, 128])
            msk = {}
            for name, (n, sl) in _tile_types().items():
                msk[name] = dict(
                    mr=T("mr_" + name, [n, H, n]), nma=T("nma_" + name, [n, n]), nmq=T("nmq_" + name, [n, n]),
                    tri=T("tri_" + name, [n, n]), blk=T("blk_" + name, [n, n]))
            mfm = T("mfm", [128, NS, NS * DS], BF16)
            mtm = T("mtm", [NS * DS, NS])
            nrB = T("nrB", [128, 512])
            ngB = T("ngB", [128, 128])
            nmc = T("nmc", [128, 8])
            cwg = T("cwg", [128, 12, 4])
            dtb = T("dtb", [128, 4])
            negA = T("negA", [128, 4])
            xn = T("xn", [128, D], BF16)
            hT2 = [T("hT0", [128, 8, 128], BF16), T("hT1", [128, 8, 128], BF16)]
            xt = T("xt", [128, D])
            xres = T("xres", [128, D])
            rp = T("rp", [128, 644])
            ta = T("ta", [128, 256])
            tb = T("tb", [128, 256])
            qr = T("qr", [128, H, DH], BF16)
            kT = T("kT", [128, H, 128], BF16)
            pc = T("pc", [128, 12, 131])
            carry = T("carry", [128, 12, 3])
            cv2 = [T("cv0", [128, 12, 128]), T("cv1", [128, 12, 128])]
            sqb = T("sqb", [128, 8, 128])
            kTg = T("kTg", [128, H, 128], BF16)
            qTg = T("qTg", [128, H, 128], BF16)
            sm = T("sm", [128, 64])
            g1 = T("g1", [128, H, 128])
            g2 = T("g2", [128, H, 128])
            Am = [T("Am0", [128, H, 128]), T("Am1", [128, H, 128])]
            Bm = [T("Bm0", [128, H, 128]), T("Bm1", [128, H, 128])]
            kb = T("kb", [128, H, DH])
            cst = T("cst", [128, 512])
            cen = T("cen", [128, 512])
            sqh = T("sqh", [128, 512])
            mixed = T("mixed", [128, D], BF16)
            mT = T("mT", [128, 8, 128], BF16)
            w_sb = T("w_sb", [128, H, DH], BF16)
            sret = T("sret", [128, H, DH])
            sgdn = T("sgdn", [128, H, DH])
            sretb = T("sretb", [128, H, DH], BF16)
            sgdnb = T("sgdnb", [128, H, DH], BF16)
            smy = T("smy", [128, 16])

            HSPEC = [("qdT", [128, H, 128], BF16), ("scT", [128, H, 128], BF16),
                     ("vr", [128, 512], BF16), ("vk", [128, H, DH], BF16), ("kr", [128, H, DH], BF16),
                     ("Pm", [128, H, 128], F32), ("vb", [128, H, DH], F32), ("nwkT", [128, H, 128], BF16),
                     ("qdTg", [128, H, 128], BF16), ("qkT", [128, H, 128], BF16), ("kdec", [128, H, DH], BF16),
                     ("sgr", [128, 512], F32), ("sgg", [128, 512], F32), ("CDt", [128, NS * H], F32)]

            def alloc_hset(p, TT_):
                return {nm: TT_("%s%d" % (nm, p), shp, dt) for (nm, shp, dt) in HSPEC}

            hset = {1: alloc_hset(1, T)}

            SU = "setup"
            P.dma("pool", DMA(ident_b[:], cd["ident"]), "identb", writes=["ident_b"])
            w_in_v = w_in.rearrange("(k p) f -> p k f", p=128)
            for (tag, c0, c1) in (("q", 2048, 3584), ("a", 0, 2048), ("g", 3584, INW)):
                P.dma("pool", DMA(w_in_sb[:, :, c0:c1], w_in_v[:, :, c0:c1]), "w_in_" + tag, writes=[("w_in_sb", tag)])
            P.dma("pool", DMA(w_out_sb[:], w_out.rearrange("(k p) m -> p k m", p=128)), "w_out", writes=["w_out_sb"])
            P.dma("pool", DMA(mfm[:], cd["mfm"]), "mfm", writes=["mfm"])
            sp_setup = [(ident_f[:], cd["ident"], "ident_f"), (mtm[:], cd["mtm"], "mtm"),
                        (nrB[:], nret_in.partition_broadcast(128), "nrB"), (ngB[:], ngdn_in.partition_broadcast(128), "ngB"),
                        (nmc[:], nm_col, "nmc"), (cwg[:].rearrange("p c i -> p (c i)"), cwg_in, "cwg"),
                        (dtb[:], dtb_in.partition_broadcast(128), "dtb"), (negA[:], alog_in.partition_broadcast(128), "negA")]
            for name in msk:
                for kk_ in ("mr", "nma", "nmq", "tri", "blk"):
                    sp_setup.append((msk[name][kk_][:], cd[kk_ + "_" + name], kk_ + "_" + name))
            for (o_, i_, key) in sp_setup:
                P.dma("sp", DMA(o_, i_), SU, writes=[key])
            P.fix_group(SU)
            dve(MSET(ones_f[:], 1.0), w=["ones_f"])
            dve(MSET(negones[:], -1.0), w=["negones"])
            dve(TS(negid[:], ident_f[:], -1.0, 0.0, ALU.mult, ALU.add), r=["ident_f"], w=["negid"])
            act(ACT(negA[:], negA[:], AF.Exp), r=["negA"], w=["negA"])
            dve(TS(negA[:], negA[:], -1.0, 0.0, ALU.mult, ALU.add), r=["negA"], w=["negA"])
            dve(MSET(sret[:], 0.0), w=["sret"])
            dve(MSET(sgdn[:], 0.0), w=["sgdn"])
            dve(MSET(sretb[:], 0.0), w=["sretb"])
            dve(MSET(sgdnb[:], 0.0), w=["sgdnb"])
            dve(MSET(carry[:], 0.0), w=["carry"])

            def bc_mid(ap2, n, reps):
                return ap2.unsqueeze(1).to_broadcast([n, reps, ap2.shape[1]])

            def bc_last(ap2, n, m):
                return ap2.unsqueeze(2).to_broadcast([n, ap2.shape[1], m])

            def rstd_chain(dst, src, scale, r_keys, key):
                dve(TS(dst, src, scale, EPS, ALU.mult, ALU.add), r=r_keys, w=[key])
                act(ACT(dst, dst, AF.Ln), r=[key], w=[key])
                act(ACT(dst, dst, AF.Exp, scale=-0.5), r=[key], w=[key])

            def stage_x1(r0, n, tname, sample, last_prompt, q):
                hT, cv = hT2[q], cv2[q]
                KH, KC = "hT%d" % q, "cv%d" % q
                P.dma("sp", DMA(xt[:n, :], x[r0:r0 + n, :]), "xt_ld", writes=["xt"])
                junk = pc[:].rearrange("p c n -> p (c n)")[:n, 0:D]
                dve(MSET(sm[:n, 0:1], 0.0), w=["sm0"])
                act(ACT(junk, xt[:n, :], AF.Square, accum_out=sm[:n, 0:1]), r=["xt"], w=["pc", "sm0"])
                rstd_chain(sm[:n, 0:1], sm[:n, 0:1], 1.0 / D, ["sm0"], "sm0")
                act(ACT(xn[:n, :], xt[:n, :], AF.Copy, scale=sm[:n, 0:1]), r=["xt", "sm0"], w=["xn"])
                for k in range(8):
                    pe(TR(psb[0][:, k * 128:k * 128 + n], xn[:n, k * 128:(k + 1) * 128], ident_b[:n, :n]),
                       r=["xn", "ident_b"], w=[PK(0)])
                dve(TT(hT[:, :, :n], psb[0][:].rearrange("p (k t) -> p k t", k=8)[:, :, :n],
                       bc_last(nmc[:, :], 128, n), ALU.mult), r=[PK(0), "nmc"], w=[KH])
                yield
                if not sample:
                    dve(CP(pc[:, :, 0:3], carry[:, :, :]), r=["carry"], w=["pc"])
                else:
                    scq_tm = cv[:].rearrange("p c n -> p (c n)")[:48, :]
                    P.dma("sp", DMA(scq_tm, scq_in), "scq_ld", writes=[KC])
                    pcv = pc[:, :, 0:112].rearrange("p c (s j) -> p c s j", j=7)
                    for c in range(12):
                        bank = 2 if c < 6 else 0
                        off = (c % 6) * 48
                        pe(TR(ps[bank][:, off:off + 48], scq_tm[:, c * 128:(c + 1) * 128], ident_f[:48, :48]),
                           r=[KC, "ident_f"], w=[PK(bank)])
                    for c in range(12):
                        bank = 2 if c < 6 else 0
                        off = (c % 6) * 48
                        act(ACT(pcv[:, c, :, 0:3], ps[bank][:, off:off + 48].rearrange("p (s i) -> p s i", i=3), AF.Copy),
                            r=[PK(bank)], w=["pc"])
                fm_banks = [1, 2, 1]
                for gi in range(3):
                    bank = fm_banks[gi]
                    for c in range(gi * 4, gi * 4 + 4):
                        for k in range(8):
                            pe(MM(ps[bank][:, (c % 4) * 128:(c % 4) * 128 + n], w_in_sb[:, k, 2048 + c * 128:2048 + (c + 1) * 128],
                                  hT[:, k, :n], start=(k == 0), stop=(k == 7)), r=[KH, ("w_in_sb", "q")], w=[PK(bank)])
                    if not sample:
                        act(ACT(pc[:, gi * 4:(gi + 1) * 4, 3:3 + n], ps[bank][:].rearrange("p (c t) -> p c t", c=4)[:, :, :n], AF.Copy),
                            r=[PK(bank)], w=["pc"])
                    else:
                        for c in range(gi * 4, gi * 4 + 4):
                            act(ACT(pcv[:, c, :, 3:7], ps[bank][:, (c % 4) * 128:(c % 4) * 128 + n].rearrange("p (s j) -> p s j", j=4), AF.Copy),
                                r=[PK(bank)], w=["pc"])
                    yield
                if not sample:
                    dve(CP(carry[:, :, :], pc[:, :, n:n + 3]), r=["pc"], w=["carry"])
                for c in range(12):
                    if not sample:
                        o_ = cv[:, c, :n]
                        tp = [pc[:, c, i:i + n] for i in range(4)]
                    else:
                        o_ = cv[:, c, :n].rearrange("p (s j) -> p s j", j=4)
                        tp = [pcv[:, c, :, i:i + 4] for i in range(4)]
                    act(ACT(o_, tp[0], AF.Copy, scale=cwg[:, c, 0:1]), r=["pc", "cwg"], w=[(KC, c)])
                    for i in range(1, 4):
                        dve(STT(o_, tp[i], cwg[:, c, i:i + 1], o_, ALU.mult, ALU.add), r=["pc", "cwg", (KC, c)], w=[(KC, c)])
                    if c % 3 == 2:
                        yield
                act(ACT(cv[:, :, :n], cv[:, :, :n], AF.Silu), r=[KC], w=[KC])
                if sample or last_prompt:
                    cols = slice(0, n) if sample else slice(n - 3, n)
                    m_ = n if sample else 3
                    for j in range(3):
                        for k in range(8):
                            pe(MM(ps[2][:m_, :512], hT[:, k, cols], w_in_sb[:, k, 2048 + j * 512:2048 + (j + 1) * 512],
                                  start=(k == 0), stop=(k == 7)), r=[KH, ("w_in_sb", "q")], w=[PK(2)])
                        act(ACT(cst[:m_, :], ps[2][:m_, :512], AF.Copy), r=[PK(2)], w=["cst"])
                        if sample:
                            for i in range(3):
                                P.dma("sp", DMA(o_cq_s[:, i, j * 512:(j + 1) * 512], cst[1 + i:64:4, :]), "cst_st", reads=["cst"])
                        else:
                            P.dma("sp", DMA(o_cq_p[:, j * 512:(j + 1) * 512], cst[:3, :]), "cst_st", reads=["cst"])
                yield
                act(ACT(sqb[:, :, :n], cv[:, 0:8, :n], AF.Square), r=[KC], w=["sqb"])
                for half in range(2):
                    bank = 1 + half
                    if n == 128:
                        pe(MM(ps[bank][:, :], ones_f[:, :], sqb[:, half * 4:(half + 1) * 4, :].rearrange("p c t -> p (c t)")),
                           r=["ones_f", "sqb"], w=[PK(bank)])
                    else:
                        for c4 in range(4):
                            pe(MM(ps[bank][:, c4 * 128:c4 * 128 + n], ones_f[:, :], sqb[:, half * 4 + c4, :n]),
                               r=["ones_f", "sqb"], w=[PK(bank)])
                for half in range(2):
                    bank = 1 + half
                    dve(TS(sqb[:, half * 4:(half + 1) * 4, :n], ps[bank][:].rearrange("p (c t) -> p c t", c=4)[:, :, :n],
                           1.0, EPS, ALU.mult, ALU.add), r=[PK(bank)], w=["sqb"])
                act(ACT(sqb[:, :, :n], sqb[:, :, :n], AF.Ln), r=["sqb"], w=["sqb"])
                act(ACT(sqb[:, :, :n], sqb[:, :, :n], AF.Exp, scale=-0.5), r=["sqb"], w=["sqb"])
                dve(TT(cv[:, 0:8, :n], cv[:, 0:8, :n], sqb[:, :, :n], ALU.mult), r=[KC, "sqb"], w=[KC])
                yield

            def stage_x2(r0, n, tname, sample, last_prompt, p, q):
                hb = hset[p]
                K = lambda nm: "%s%d" % (nm, p)
                hT, cv = hT2[q], cv2[q]
                KH, KC = "hT%d" % q, "cv%d" % q
                qkn = cv
                qdT, scT, vr, vk, kr = hb["qdT"], hb["scT"], hb["vr"], hb["vk"], hb["kr"]
                Pm, vb, nwkT, qdTg, qkT, kdec = hb["Pm"], hb["vb"], hb["nwkT"], hb["qdTg"], hb["qkT"], hb["kdec"]
                sgr, sgg, CDt = hb["sgr"], hb["sgg"], hb["CDt"]
                M = msk[tname]
                levels = {"meta": 3, "prm": 6, "smp": 1}[tname]
                P.dma("sp", DMA(rp[:n, :], cd["rope"][r0:r0 + n, :]), "rp_ld", writes=["rp"])

                def inproj(c0, wd, bank, dst):
                    wtag = "a" if c0 < 2048 else "g"
                    for k in range(8):
                        pe(MM(dst, hT[:, k, :n], w_in_sb[:, k, c0:c0 + wd], start=(k == 0), stop=(k == 7)),
                           r=[KH, ("w_in_sb", wtag)], w=[PK(bank)])

                def rotary(src_bank, dst, dkey, cq, sq):
                    v4 = ps[src_bank][:n, :].rearrange("p (h t d) -> p h t d", h=H, t=2)
                    x1_, x2_ = v4[:, :, 0, :], v4[:, :, 1, :]
                    ta3 = ta[:n, :].rearrange("p (h d) -> p h d", h=H)
                    tb3 = tb[:n, :].rearrange("p (h d) -> p h d", h=H)
                    dve(TT(ta3, x1_, cq, ALU.mult), r=[PK(src_bank), "rp"], w=["ta"])
                    dve(TT(tb3, x2_, sq, ALU.mult), r=[PK(src_bank), "rp"], w=["tb"])
                    dve(TT(dst[:n, :, 0:64], ta3, tb3, ALU.subtract), r=["ta", "tb"], w=[dkey])
                    dve(TT(ta3, x2_, cq, ALU.mult), r=[PK(src_bank), "rp"], w=["ta"])
                    dve(TT(tb3, x1_, sq, ALU.mult), r=[PK(src_bank), "rp"], w=["tb"])
                    dve(TT(dst[:n, :, 64:128], ta3, tb3, ALU.add), r=["ta", "tb"], w=[dkey])

                inproj(4096, 8, 6, ps[6][:n, 0:8])
                dve(CP(sm[:n, 8:16], ps[6][:n, 0:8]), r=[PK(6)], w=["smg"])
                inproj(0, 512, 3, ps[3][:n, :])
                inproj(512, 512, 4, ps[4][:n, :])
                yield
                rotary(3, qr, "qr", rp[:n, 0:256].rearrange("p (h d) -> p h d", h=H),
                       rp[:n, 256:512].rearrange("p (h d) -> p h d", h=H))
                inproj(1024, 512, 3, ps[3][:n, :])
                yield
                rotary(4, kr, K("kr"), bc_mid(rp[:n, 512:576], n, H), bc_mid(rp[:n, 576:640], n, H))
                inproj(1536, 512, 4, ps[4][:n, :])
                yield
                act(ACT(vr[:n, :], ps[3][:n, :], AF.Copy), r=[PK(3)], w=[K("vr")])
                dve(TT(vk[:n, :, :], ps[3][:n, :].rearrange("p (h d) -> p h d", h=H), bc_last(rp[:n, 640:644], n, DH), ALU.mult),
                    r=[PK(3), "rp"], w=[K("vk")])
                act(ACT(sgr[:n, :], ps[4][:n, :], AF.Silu), r=[PK(4)], w=[K("sgr")])
                inproj(3584, 512, 3, ps[3][:n, :])
                act(ACT(sgg[:n, :], ps[3][:n, :], AF.Silu), r=[PK(3)], w=[K("sgg")])
                yield
                for h in range(H):
                    pe(TR(psb[4][:, h * 128:h * 128 + n], qr[:n, h, :], ident_b[:n, :n]), r=["qr", "ident_b"], w=[PK(4)])
                    pe(TR(psb[4][:, (4 + h) * 128:(4 + h) * 128 + n], kr[:n, h, :], ident_b[:n, :n]), r=[K("kr"), "ident_b"], w=[PK(4)])
                pbv = psb[4][:].rearrange("p (k t) -> p k t", k=8)
                act(ACT(qdT[:, :, :n], pbv[:, 0:4, :n], AF.Copy), r=[PK(4)], w=[K("qdT")])
                act(ACT(kT[:, :, :n], pbv[:, 4:8, :n], AF.Copy), r=[PK(4)], w=["kT"])
                p3v = ps[3][:].rearrange("p (h t) -> p h t", h=H)
                p4v = ps[4][:].rearrange("p (h t) -> p h t", h=H)
                p6v = ps[6][:].rearrange("p (h t) -> p h t", h=H)
                for h in range(H):
                    pe(MM(ps[3][:n, h * 128:h * 128 + n], kT[:, h, :n], qdT[:, h, :n]), r=["kT", K("qdT")], w=[PK(3)])
                dve(TT(scT[:n, :, :n], p3v[:n, :, :n], M["mr"][:n, :, :n], ALU.mult), r=[PK(3), "mr_" + tname], w=[K("scT")])
                yield
                act(ACT(kTg[:, :, :n], qkn[:, 4:8, :n], AF.Copy), r=[KC], w=["kTg"])
                act(ACT(qTg[:, :, :n], qkn[:, 0:4, :n], AF.Copy, scale=QSCALE), r=[KC], w=["qTg"])
                dve(TT(sm[:n, 16:20], sm[:n, 12:16], dtb[:n, :], ALU.add), r=["smg", "dtb"], w=["smg"])
                act(ACT(sm[:n, 16:20], sm[:n, 16:20], AF.Exp), r=["smg"], w=["smg"])
                act(ACT(sm[:n, 16:20], sm[:n, 16:20], AF.Ln, bias=1.0), r=["smg"], w=["smg"])
                dve(TT(sm[:n, 16:20], sm[:n, 16:20], negA[:n, :], ALU.mult), r=["smg", "negA"], w=["smg"])
                act(ACT(sm[:n, 20:24], sm[:n, 8:12], AF.Exp, scale=-1.0), r=["smg"], w=["smg"])
                dve(TS(sm[:n, 20:24], sm[:n, 20:24], 1.0, 1.0, ALU.mult, ALU.add), r=["smg"], w=["smg"])
                act(ACT(sm[:n, 24:28], sm[:n, 20:24], AF.Ln), r=["smg"], w=["smg"])
                dve(TS(sm[:n, 24:28], sm[:n, 24:28], -1.0, 0.0, ALU.mult, ALU.add), r=["smg"], w=["smg"])
                dve(RCP(sm[:n, 20:24], sm[:n, 20:24]), r=["smg"], w=["smg"])
                pe(MM(ps[6][:n, 8:12], M["tri"][:n, :n], sm[:n, 16:20]), r=["smg", "tri_" + tname], w=[PK(6)])
                pe(MM(ps[6][:n, 12:16], M["blk"][:n, :n], sm[:n, 16:20]), r=["smg", "blk_" + tname], w=[PK(6)])
                dve(CP(sm[:n, 28:36], ps[6][:n, 8:16]), r=[PK(6)], w=["smg"])
                act(ACT(sm[:n, 36:40], sm[:n, 28:32], AF.Exp), r=["smg"], w=["smg"])
                dve(TT(sm[:n, 40:44], sm[:n, 32:36], sm[:n, 28:32], ALU.subtract), r=["smg"], w=["smg"])
                act(ACT(sm[:n, 40:44], sm[:n, 40:44], AF.Exp), r=["smg"], w=["smg"])
                dve(TT(sm[:n, 44:48], sm[:n, 20:24], sm[:n, 36:40], ALU.mult), r=["smg"], w=["smg"])
                if sample:
                    gm = ta[:n, 0:NS * H].rearrange("p (s h) -> p s h", h=H)
                    dve(TT(gm, bc_mid(sm[:n, 16:20], n, NS), bc_last(mtm[:n, :], n, H), ALU.mult), r=["smg", "mtm"], w=["ta"])
                    pe(MM(ps[6][:, 16:16 + NS * H], ones_f[:n, :], ta[:n, 0:NS * H]), r=["ta", "ones_f"], w=[PK(6)])
                    act(ACT(CDt[:, :], ps[6][:, 16:16 + NS * H], AF.Exp), r=[PK(6)], w=[K("CDt")])
                else:
                    pe(MM(ps[6][:, 16:20], ones_f[:n, :], sm[:n, 16:20]), r=["smg", "ones_f"], w=[PK(6)])
                    act(ACT(CDt[:, 0:4], ps[6][:, 16:20], AF.Exp), r=[PK(6)], w=[K("CDt")])
                yield
                dve(TT(g1[:n, :, :n], bc_mid(ones_f[:n, :n], n, H), bc_last(sm[:n, 28:32], n, n), ALU.mult), r=["ones_f", "smg"], w=["g1"])
                dve(TT(g2[:n, :, :n], bc_mid(ident_f[:n, :n], n, H), bc_last(sm[:n, 28:32], n, n), ALU.mult), r=["ident_f", "smg"], w=["g2"])
                for h in range(H):
                    sl = slice(h * 128, h * 128 + n)
                    pe(MM(ps[3][:n, sl], ident_f[:n, :n], g1[:n, h, :n], start=True, stop=False), r=["ident_f", "g1"], w=[PK(3)])
                    pe(MM(ps[3][:n, sl], negones[:n, :n], g2[:n, h, :n], start=False, stop=True), r=["negones", "g2"], w=[PK(3)])
                    pe(MM(ps[4][:n, sl], ones_f[:n, :n], g2[:n, h, :n], start=True, stop=False), r=["ones_f", "g2"], w=[PK(4)])
                    pe(MM(ps[4][:n, sl], negid[:n, :n], g1[:n, h, :n], start=False, stop=True), r=["negid", "g1"], w=[PK(4)])
                dve(STT(g1[:n, :, :n], p3v[:n, :, :n], 0.0, bc_mid(M["nma"][:n, :n], n, H), ALU.min, ALU.add), r=[PK(3), "nma_" + tname], w=["g1"])
                dve(STT(g2[:n, :, :n], p4v[:n, :, :n], 0.0, bc_mid(M["nmq"][:n, :n], n, H), ALU.min, ALU.add), r=[PK(4), "nmq_" + tname], w=["g2"])
                for h in range(H):
                    act(ACT(g1[:n, h, :n], g1[:n, h, :n], AF.Exp, bias=sm[:n, 24 + h:25 + h]), r=["g1", "smg"], w=["g1"])
                act(ACT(g2[:n, :, :n], g2[:n, :, :n], AF.Exp), r=["g2"], w=["g2"])
                yield
                for h in range(H):
                    sl = slice(h * 128, h * 128 + n)
                    pe(MM(ps[6][:n, sl], kTg[:, h, :n], kTg[:, h, :n]), r=["kTg"], w=[PK(6)])
                    pe(MM(ps[3][:n, sl], kTg[:, h, :n], qTg[:, h, :n]), r=["kTg", "qTg"], w=[PK(3)])
                dve(TT(Am[0][:n, :, :n], p6v[:n, :, :n], g1[:n, :, :n], ALU.mult), r=[PK(6), "g1"], w=["Am0"])
                dve(TT(qkT[:n, :, :n], p3v[:n, :, :n], g2[:n, :, :n], ALU.mult), r=[PK(3), "g2"], w=[K("qkT")])
                for h in range(H):
                    pe(TR(ps[4][:n, h * 128:h * 128 + n], Am[0][:n, h, :n], ident_f[:n, :n]), r=["Am0", "ident_f"], w=[PK(4)])
                act(ACT(Bm[0][:n, :, :n], p4v[:n, :, :n], AF.Copy), r=[PK(4)], w=["Bm0"])
                yield
                dve(TT(g1[:n, :, :n], bc_mid(M["tri"][:n, :n], n, H), bc_last(sm[:n, 16:20], n, n), ALU.mult), r=["tri_" + tname, "smg"], w=["g1"])
                if n == 128:
                    pe(MM(ps[3][:, :], ones_f[:n, :], g1[:n, :, :].rearrange("p h t -> p (h t)")), r=["ones_f", "g1"], w=[PK(3)])
                else:
                    for h in range(H):
                        pe(MM(ps[3][:, h * 128:h * 128 + n], ones_f[:n, :], g1[:n, h, :n]), r=["ones_f", "g1"], w=[PK(3)])
                act(ACT(g2[:, :, :n], p3v[:, :, :n], AF.Exp), r=[PK(3)], w=["g2"])
                dve(STT(qdTg[:, :, :n], qkn[:, 0:4, :n], QSCALE, g2[:, :, :n], ALU.mult, ALU.mult), r=[KC, "g2"], w=[K("qdTg")])
                for h in range(H):
                    pe(TR(ps[4][:n, h * 128:(h + 1) * 128], qkn[:, 4 + h, :n], ident_f[:, :]), r=[KC, "ident_f"], w=[PK(4)])
                    pe(TR(ps[6][:n, h * 128:(h + 1) * 128], cv[:, 8 + h, :n], ident_f[:, :]), r=[KC, "ident_f"], w=[PK(6)])
                p4d = ps[4][:].rearrange("p (h d) -> p h d", h=H)
                p6d = ps[6][:].rearrange("p (h d) -> p h d", h=H)
                dve(TT(vb[:n], p6d[:n], bc_last(sm[:n, 20:24], n, DH), ALU.mult), r=[PK(6), "smg"], w=[K("vb")])
                dve(TT(kb[:n], p4d[:n], bc_last(sm[:n, 44:48], n, DH), ALU.mult), r=[PK(4), "smg"], w=["kb"])
                dve(TT(kdec[:n], p4d[:n], bc_last(sm[:n, 40:44], n, DH), ALU.mult), r=[PK(4), "smg"], w=[K("kdec")])
                yield
                dve(TT(Pm[:n, :, :n], bc_mid(ident_f[:n, :n], n, H), Bm[0][:n, :, :n], ALU.subtract), r=["ident_f", "Bm0"], w=[K("Pm")])
                cur = 0
                for lv in range(1, levels + 1):
                    nxt = 1 - cur
                    for h in range(H):
                        sl = slice(h * 128, h * 128 + n)
                        pe(MM(ps[3][:n, sl], Bm[cur][:n, h, :n], Am[cur][:n, h, :n]), r=["Bm%d" % cur, "Am%d" % cur], w=[PK(3)])
                        if lv < levels:
                            pe(MM(ps[4][:n, sl], Am[cur][:n, h, :n], Bm[cur][:n, h, :n]), r=["Bm%d" % cur, "Am%d" % cur], w=[PK(4)])
                    act(ACT(Am[nxt][:n, :, :n], p3v[:n, :, :n], AF.Copy), r=[PK(3)], w=["Am%d" % nxt])
                    if lv < levels:
                        dve(CP(Bm[nxt][:n, :, :n], p4v[:n, :, :n]), r=[PK(4)], w=["Bm%d" % nxt])
                    for h in range(H):
                        sl = slice(h * 128, h * 128 + n)
                        pe(MM(ps[6][:n, sl], Am[nxt][:n, h, :n], Pm[:n, h, :n]), r=["Am%d" % nxt, K("Pm")], w=[PK(6)])
                    dve(TT(Pm[:n, :, :n], Pm[:n, :, :n], p6v[:n, :, :n], ALU.add), r=[K("Pm"), PK(6)], w=[K("Pm")])
                    cur = nxt
                    yield
                for h in range(H):
                    pe(MM(ps[4][:, h * 128:h * 128 + n], kb[:n, h, :], Pm[:n, h, :n]), r=["kb", K("Pm")], w=[PK(4)])
                act(ACT(nwkT[:, :, :n], p4v[:, :, :n], AF.Copy, scale=-1.0), r=[PK(4)], w=[K("nwkT")])
                yield

            def ret_norm(n, bank, p):
                hb = hset[p]
                K = lambda nm: "%s%d" % (nm, p)
                pd = ps[bank][:].rearrange("p (h d) -> p h d", h=H)
                cen3 = cen[:n, :].rearrange("p (h d) -> p h d", h=H)
                sqh3 = sqh[:n, :].rearrange("p (h d) -> p h d", h=H)
                dve(RED(smy[:n, 0:4], pd[:n]), r=[PK(bank)], w=["smy"])
                dve(TS(smy[:n, 0:4], smy[:n, 0:4], -1.0 / DH, 0.0, ALU.mult, ALU.add), r=["smy"], w=["smy"])
                dve(TT(cen3, pd[:n], bc_last(smy[:n, 0:4], n, DH), ALU.add), r=[PK(bank), "smy"], w=["cen"])
                act(ACT(sqh[:n, :], cen[:n, :], AF.Square), r=["cen"], w=["sqh"])
                dve(RED(smy[:n, 4:8], sqh3), r=["sqh"], w=["smy"])
                rstd_chain(smy[:n, 4:8], smy[:n, 4:8], 1.0 / DH, ["smy"], "smy")
                dve(TT(cen3, cen3, bc_last(smy[:n, 4:8], n, DH), ALU.mult), r=["cen", "smy"], w=["cen"])
                dve(TT(cen[:n, :], cen[:n, :], nrB[:n, :], ALU.mult), r=["cen", "nrB"], w=["cen"])
                dve(TT(mixed[:n, 0:512], cen[:n, :], hb["sgr"][:n, :], ALU.mult), r=["cen", K("sgr")], w=["mixed"])

            def gdn_norm(n, bank, p):
                hb = hset[p]
                K = lambda nm: "%s%d" % (nm, p)
                pd = ps[bank][:].rearrange("p (h d) -> p h d", h=H)
                cen3 = cen[:n, :].rearrange("p (h d) -> p h d", h=H)
                sqh3 = sqh[:n, :].rearrange("p (h d) -> p h d", h=H)
                act(ACT(sqh[:n, :], ps[bank][:n, :], AF.Square), r=[PK(bank)], w=["sqh"])
                dve(RED(smy[:n, 8:12], sqh3), r=["sqh"], w=["smy"])
                rstd_chain(smy[:n, 8:12], smy[:n, 8:12], 1.0 / DH, ["smy"], "smy")
                dve(TT(cen3, pd[:n], bc_last(smy[:n, 8:12], n, DH), ALU.mult), r=[PK(bank), "smy"], w=["cen"])
                dve(TT(cen3, cen3, bc_mid(ngB[:n, :], n, H), ALU.mult), r=["cen", "ngB"], w=["cen"])
                dve(TT(mixed[:n, 512:1024], cen[:n, :], hb["sgg"][:n, :], ALU.mult), r=["cen", K("sgg")], w=["mixed"])

            def out_proj(r0, n, p, btr, b0, b1):
                P.dma("sp", DMA(xres[:n, :], x[r0:r0 + n, :]), "xres_ld", writes=["xres"])
                for k in range(8):
                    pe(TR(psb[btr][:, k * 128:k * 128 + n], mixed[:n, k * 128:(k + 1) * 128], ident_b[:n, :n]), r=["mixed", "ident_b"], w=[PK(btr)])
                act(ACT(mT[:, :, :n], psb[btr][:].rearrange("p (k t) -> p k t", k=8)[:, :, :n], AF.Copy), r=[PK(btr)], w=["mT"])
                for half, bank in enumerate((b0, b1)):
                    for k in range(8):
                        pe(MM(ps[bank][:n, :], mT[:, k, :n], w_out_sb[:, k, half * 512:(half + 1) * 512], start=(k == 0), stop=(k == 7)),
                           r=["mT", "w_out_sb"], w=[PK(bank)])
                    dve(TT(xres[:n, half * 512:(half + 1) * 512], xres[:n, half * 512:(half + 1) * 512], ps[bank][:n, :], ALU.add),
                        r=["xres", PK(bank)], w=["xres"])
                P.dma("sp", DMA(x1s[r0:r0 + n, :], xres[:n, :]), "xres_st", reads=["xres"], writes=["x1s"])

            def stage_y(r0, n, tname, last_prompt, p):
                hb = hset[p]
                K = lambda nm: "%s%d" % (nm, p)
                qdT, scT, vr, vk, kr = hb["qdT"], hb["scT"], hb["vr"], hb["vk"], hb["kr"]
                Pm, vb, nwkT, qdTg, qkT, kdec, CDt = hb["Pm"], hb["vb"], hb["nwkT"], hb["qdTg"], hb["qkT"], hb["kdec"], hb["CDt"]
                for h in range(H):
                    hs = slice(h * 128, (h + 1) * 128)
                    pe(MM(ps[5][:n, hs], scT[:n, h, :n], vr[:n, hs], start=True, stop=False), r=[K("scT"), K("vr")], w=[PK(5)])
                    pe(MM(ps[5][:n, hs], qdT[:, h, :n], sretb[:, h, :], start=False, stop=True), r=[K("qdT"), "sretb"], w=[PK(5)])
                for h in range(H):
                    hs = slice(h * 128, (h + 1) * 128)
                    pe(MM(ps[7][:, hs], kr[:n, h, :], vk[:n, h, :]), r=[K("kr"), K("vk")], w=[PK(7)])
                yield
                for h in range(H):
                    hs = slice(h * 128, (h + 1) * 128)
                    dve(STT(sret[:, h, :], sret[:, h, :], float(GAM[h] ** n), ps[7][:, hs], ALU.mult, ALU.add),
                        r=["sret", PK(7)], w=["sret"])
                act(ACT(sretb[:], sret[:], AF.Copy), r=["sret"], w=["sretb"])
                yield
                for h in range(H):
                    hs = slice(h * 128, (h + 1) * 128)
                    pe(MM(ps[7][:n, hs], Pm[:n, h, :n], vb[:n, h, :], start=True, stop=False), r=[K("Pm"), K("vb")], w=[PK(7)])
                    pe(MM(ps[7][:n, hs], nwkT[:, h, :n], sgdnb[:, h, :], start=False, stop=True), r=[K("nwkT"), "sgdnb"], w=[PK(7)])
                act(ACT(w_sb[:n], ps[7][:n, :].rearrange("p (h d) -> p h d", h=H), AF.Copy), r=[PK(7)], w=["w_sb"])
                yield
                ret_norm(n, 5, p)
                yield
                for h in range(H):
                    hs = slice(h * 128, (h + 1) * 128)
                    pe(MM(ps[5][:n, hs], qdTg[:, h, :n], sgdnb[:, h, :], start=True, stop=False), r=[K("qdTg"), "sgdnb"], w=[PK(5)])
                    pe(MM(ps[5][:n, hs], qkT[:n, h, :n], w_sb[:n, h, :], start=False, stop=True), r=[K("qkT"), "w_sb"], w=[PK(5)])
                for h in range(H):
                    hs = slice(h * 128, (h + 1) * 128)
                    pe(MM(ps[7][:, hs], kdec[:n, h, :], w_sb[:n, h, :]), r=[K("kdec"), "w_sb"], w=[PK(7)])
                yield
                for h in range(H):
                    hs = slice(h * 128, (h + 1) * 128)
                    dve(STT(sgdn[:, h, :], sgdn[:, h, :], CDt[:, h:h + 1], ps[7][:, hs], ALU.mult, ALU.add),
                        r=["sgdn", K("CDt"), PK(7)], w=["sgdn"])
                act(ACT(sgdnb[:], sgdn[:], AF.Copy), r=["sgdn"], w=["sgdnb"])
                if last_prompt:
                    P.dma("sp", DMA(o_sret_p.rearrange("h d e -> d h e"), sret[:]), "sret_st", reads=["sret"])
                    P.dma("sp", DMA(o_sgdn_p.rearrange("h d e -> d h e"), sgdn[:]), "sgdn_st", reads=["sgdn"])
                yield
                gdn_norm(n, 5, p)
                yield
                out_proj(r0, n, p, 7, 5, 7)
                yield

            def run_all(g):
                for _ in g:
                    pass

            def interleave(gens):
                live = [[g, w] for (g, w) in gens if g is not None]
                while live:
                    for item in list(live):
                        for _ in range(item[1]):
                            try:
                                next(item[0])
                            except StopIteration:
                                live.remove(item)
                                break

            ptiles = [(0, NMETA, "meta", False)] + [(NMETA + 128 * i, 128, "prm", i == SEQ // 128 - 1) for i in range(SEQ // 128)]
            NT = len(ptiles)

            xtiles = ptiles + [(LP, NS * DS, "smp", False)]

            def gx1(i):
                if i >= len(xtiles):
                    return None
                r_, n_, t_, l_ = xtiles[i]
                return stage_x1(r_, n_, t_, t_ == "smp", l_, i % 2)

            def gx2(i):
                if i >= len(xtiles):
                    return None
                r_, n_, t_, l_ = xtiles[i]
                return stage_x2(r_, n_, t_, t_ == "smp", l_, i % 2, i % 2)

            def gy(i):
                r_, n_, t_, l_ = ptiles[i]
                return stage_y(r_, n_, t_, l_, i % 2)

            with contextlib.ExitStack() as st1:
                T1 = lambda name, shape, dt=F32: st1.enter_context(nc.sbuf_tensor("sb_" + name, list(shape), dt))
                hset[0] = alloc_hset(0, T1)
                def rec(g):
                    if g is None:
                        return []
                    P.rec = []
                    run_all(g)
                    r_ = P.rec
                    P.rec = None
                    return r_

                run_all(gx1(0))
                P.merge([rec(gx1(1)), rec(gx2(0))])
                for i in range(NT):
                    P.merge([rec(gx1(i + 2)), rec(gx2(i + 1)), rec(gy(i))])
                P.barrier()

            with contextlib.ExitStack() as st2:
                T2 = lambda name, shape, dt=F32: st2.enter_context(nc.sbuf_tensor("sb_" + name, list(shape), dt))
                Sb = [T2("Sb0", [128, 2 * H, DH], BF16), T2("Sb1", [128, 2 * H, DH], BF16)]
                Sf = [T2("Sf0", [128, 2 * H, DH]), T2("Sf1", [128, 2 * H, DH])]
                mq = [T2("mq0", [128, H, 64], BF16), T2("mq1", [128, H, 64], BF16)]
                mw = [T2("mw0", [128, H, 64], BF16), T2("mw1", [128, H, 64], BF16)]
                mg = [T2("mg0", [128, H, 64], BF16), T2("mg1", [128, H, 64], BF16)]
                mk = [T2("mk0", [64, H, DH], BF16), T2("mk1", [64, H, DH], BF16)]
                md = [T2("md0", [64, H, DH], BF16), T2("md1", [64, H, DH], BF16)]
                n = NS * DS
                r0 = LP
                assert NT % 2 == 1
                hb = hset[1]
                K = lambda nm: "%s1" % nm
                qdT, scT, vr, vk, kr = hb["qdT"], hb["scT"], hb["vr"], hb["vk"], hb["kr"]
                Pm, vb, nwkT, qdTg, qkT, kdec, CDt = hb["Pm"], hb["vb"], hb["nwkT"], hb["qdTg"], hb["qkT"], hb["kdec"], hb["CDt"]
                for s in range(NS):
                    b = s % 2
                    P.dma("pool", DMA(Sb[b][:, :, :], sall_in[s].rearrange("h d e -> d h e")), "Sb%d" % b, writes=["Sb%d" % b])
                    dve(TT(mq[b][:, :, :n], qdT[:, :, :n], bc_mid(mfm[:, s, :], 128, H), ALU.mult), r=[K("qdT"), "mfm"], w=["mq%d" % b])
                    dve(TT(mw[b][:, :, :n], nwkT[:, :, :n], bc_mid(mfm[:, s, :], 128, H), ALU.mult), r=[K("nwkT"), "mfm"], w=["mw%d" % b])
                    dve(TT(mg[b][:, :, :n], qdTg[:, :, :n], bc_mid(mfm[:, s, :], 128, H), ALU.mult), r=[K("qdTg"), "mfm"], w=["mg%d" % b])
                    for h in range(H):
                        hs = slice(h * 128, (h + 1) * 128)
                        first = (s == 0)
                        if first:
                            pe(MM(ps[5][:n, hs], scT[:n, h, :n], vr[:n, hs], start=(h == 0), stop=False, skip=True), r=[K("scT"), K("vr")], w=[PK(5)])
                            pe(MM(ps[1][:n, hs], Pm[:n, h, :n], vb[:n, h, :], start=(h == 0), stop=False, skip=True), r=[K("Pm"), K("vb")], w=[PK(1)])
                        pe(MM(ps[5][:n, hs], mq[b][:, h, :n], Sb[b][:, h, :], start=False, stop=(s == NS - 1), skip=True),
                           r=["mq%d" % b, "Sb%d" % b], w=[PK(5)])
                        pe(MM(ps[1][:n, hs], mw[b][:, h, :n], Sb[b][:, H + h, :], start=False, stop=(s == NS - 1), skip=True),
                           r=["mw%d" % b, "Sb%d" % b], w=[PK(1)])
                        pe(MM(ps[2][:n, hs], mg[b][:, h, :n], Sb[b][:, H + h, :], start=(first and h == 0), stop=False, skip=True),
                           r=["mg%d" % b, "Sb%d" % b], w=[PK(2)])
                act(ACT(w_sb[:n], ps[1][:n, :].rearrange("p (h d) -> p h d", h=H), AF.Copy), r=[PK(1)], w=["w_sb"])
                for h in range(H):
                    hs = slice(h * 128, (h + 1) * 128)
                    pe(MM(ps[2][:n, hs], qkT[:n, h, :n], w_sb[:n, h, :], start=False, stop=True, skip=True), r=[K("qkT"), "w_sb"], w=[PK(2)])
                def pass2(par):
                    for s in range(par, NS, 2):
                        b = s % 2
                        kr_ = kg_ = "Sf%d" % b
                        P.dma("sp", DMA(Sf[b][:, :, :], sall_in[s].rearrange("h d e -> d h e")), "Sfl%d" % b, writes=[kr_])
                        dve(TS(mk[b][:n], kr[:n], mtm[:n, s:s + 1], 0.0, ALU.mult, ALU.add), r=[K("kr"), "mtm"], w=["mk%d" % b])
                        dve(TS(md[b][:n], kdec[:n], mtm[:n, s:s + 1], 0.0, ALU.mult, ALU.add), r=[K("kdec"), "mtm"], w=["md%d" % b])
                        bank = 3 + (s % 2)
                        bank2 = 6 if (s % 2) else 7
                        for h in range(H):
                            hs = slice(h * 128, (h + 1) * 128)
                            pe(MM(ps[bank][:, hs], mk[b][:n, h, :], vk[:n, h, :]), r=["mk%d" % b, K("vk")], w=[PK(bank)])
                            pe(MM(ps[bank2][:, hs], md[b][:n, h, :], w_sb[:n, h, :]), r=["md%d" % b, "w_sb"], w=[PK(bank2)])
                        for h in range(H):
                            hs = slice(h * 128, (h + 1) * 128)
                            dve(STT(Sf[b][:, h, :], Sf[b][:, h, :], float(GAM[h] ** DS), ps[bank][:, hs], ALU.mult, ALU.add),
                                r=[kr_, PK(bank)], w=[kr_])
                            dve(STT(Sf[b][:, H + h, :], Sf[b][:, H + h, :], CDt[:, s * H + h:s * H + h + 1], ps[bank2][:, hs], ALU.mult, ALU.add),
                                r=[kg_, K("CDt"), PK(bank2)], w=[kg_])
                        P.dma("act", DMA(o_sall_s[s].rearrange("h d e -> d h e"), Sf[b][:, :, :]), "Sfs%d" % b, reads=[kr_])

                def rec_fn(f, *a_):
                    P.rec = []
                    f(*a_)
                    r_ = P.rec
                    P.rec = None
                    return r_

                P.merge([rec_fn(pass2, 0), rec_fn(pass2, 1)])
                ret_norm(n, 5, 1)
                gdn_norm(n, 2, 1)
                out_proj(r0, n, 1, 0, 3, 4)
                P.barrier()

        with contextlib.ExitStack() as st:
            def T(name, shape, dt=F32):
                return st.enter_context(nc.sbuf_tensor("sb_" + name, list(shape), dt))

            w_up_sb = T("w_up_sb", [128, 8, 2 * DFF], BF16)
            w_dn_sb = T("w_dn_sb", [128, 22, D], BF16)
            identB_f = T("identB_f", [128, 128])
            identB_b = T("identB_b", [128, 128], BF16)
            nfc = T("nfc", [128, 8])
            cwf = T("cwf", [128, 44, 3])
            nfinB = T("nfinB", [128, D])
            carryf = T("carryf", [128, 44, 2])
            xf = [T("xf0", [128, D]), T("xf1", [128, D])]
            xr = [T("xr0", [128, D]), T("xr1", [128, D])]
            h2n2 = [T("h2n0", [128, D], BF16), T("h2n1", [128, D], BF16)]
            h2T2 = [T("h2T0", [128, 8, 256], BF16), T("h2T1", [128, 8, 256], BF16)]
            actT2 = [T("actT0", [128, 22, 256], BF16), T("actT1", [128, 22, 256], BF16)]
            ubg2 = [T("ubg0", [128, 258]), T("ubg1", [128, 258])]
            ubv2 = [T("ubv0", [128, 258]), T("ubv1", [128, 258])]
            cg2 = [T("cg0", [128, 256]), T("cg1", [128, 256])]
            cvl2 = [T("cvl0", [128, 256]), T("cvl1", [128, 256])]
            ptmp2 = [[T("pta0", [128, 256]), T("ptb0", [128, 256])], [T("pta1", [128, 256]), T("ptb1", [128, 256])]]
            yt = T("yt", [128, D])
            smb = T("smb", [128, 8])
            sct = T("sct", [32, 256])
            stg = T("stg", [64, 512])

            P.dma("pool", DMA(identB_b[:], cd["ident"]), "idBb", writes=["identB_b"])
            for blk in (0, 5, 6, 1, 7, 2, 8, 3, 9, 4, 10):
                P.dma("pool", DMA(w_up_sb[:, :, blk * 512:(blk + 1) * 512],
                                  w_up[:, blk * 512:(blk + 1) * 512].rearrange("(k p) f -> p k f", p=128)),
                      "w_up%d" % blk, writes=[("w_up_sb", blk)])
            for hf, j0 in enumerate((0, 11)):
                P.dma("pool", DMA(w_dn_sb[:, j0:j0 + 11, :], w_down[j0 * 128:(j0 + 11) * 128, :].rearrange("(j p) m -> p j m", p=128)),
                      "w_dn%d" % hf, writes=[("w_dn_sb", hf)])
            for (o_, i_, key) in [(identB_f[:], cd["ident"], "identB_f"), (nfc[:], nf_col, "nfc"),
                                  (cwf[:].rearrange("p c i -> p (c i)"), cwf_in, "cwf"),
                                  (nfinB[:], nfin_in.partition_broadcast(128), "nfinB")]:
                P.dma("sp", DMA(o_, i_), "setupB", writes=[key])
            P.fix_group("setupB")
            dve(MSET(carryf[:], 0.0), w=["carryf"])

            def tiles_of(rg0, ncol):
                return [(rg0 + o, min(128, ncol - o), o) for o in range(0, ncol, 128)]

            glist = [(256 * gi, 256, False, False) for gi in range(LP // 256)]
            glist.append((256 * (LP // 256), LP - 256 * (LP // 256), False, True))
            glist.append((LP, NS * DS, True, False))

            def front_a(g):
                rg0, ncol, sample, last_prompt = glist[g]
                for ti, (rt0, nt, off) in enumerate(tiles_of(rg0, ncol)):
                    xk = "xf%d" % ti
                    P.dma("sp", DMA(xf[ti][:nt, :], x1s[rt0:rt0 + nt, :]), xk + "_ld", reads=["x1s"], writes=[xk])
                    dve(MSET(smb[:nt, 0:1], 0.0), w=["smbf"])
                    act(ACT(h2n2[ti][:nt, :], xf[ti][:nt, :], AF.Square, accum_out=smb[:nt, 0:1]), r=[xk], w=["h2n%d" % ti, "smbf"])
                    dve(TS(smb[:nt, 0:1], smb[:nt, 0:1], 1.0 / D, EPS, ALU.mult, ALU.add), r=["smbf"], w=["smbf"])
                    act(ACT(smb[:nt, 0:1], smb[:nt, 0:1], AF.Ln), r=["smbf"], w=["smbf"])
                    act(ACT(smb[:nt, 0:1], smb[:nt, 0:1], AF.Exp, scale=-0.5), r=["smbf"], w=["smbf"])
                    act(ACT(h2n2[ti][:nt, :], xf[ti][:nt, :], AF.Copy, scale=smb[:nt, 0:1]), r=[xk, "smbf"], w=["h2n%d" % ti])

            def front_b(g):
                rg0, ncol, sample, last_prompt = glist[g]
                h2T = h2T2[g % 2]
                for ti, (rt0, nt, off) in enumerate(tiles_of(rg0, ncol)):
                    for k in range(8):
                        pe(TR(psb[0][:, k * 128:k * 128 + nt], h2n2[ti][:nt, k * 128:(k + 1) * 128], identB_b[:nt, :nt]),
                           r=["h2n%d" % ti, "identB_b"], w=[PK(0)])
                    dve(TT(h2T[:, :, off:off + nt], psb[0][:].rearrange("p (k t) -> p k t", k=8)[:, :, :nt],
                           nfc[:, :].unsqueeze(2).to_broadcast([128, 8, nt]), ALU.mult), r=[PK(0), "nfc"], w=["h2T%d" % (g % 2)])

            def down_work(g):
                rg0, ncol, sample, last_prompt = glist[g]
                actT = actT2[g % 2]
                ak = "actT%d" % (g % 2)
                work = []
                for ti, (rt0, nt, off) in enumerate(tiles_of(rg0, ncol)):
                    xk = "xr%d" % ti

                    def ld(ti=ti, rt0=rt0, nt=nt, xk=xk):
                        P.dma("sp", DMA(xr[ti][:nt, :], x1s[rt0:rt0 + nt, :]), xk + "_ld", reads=["x1s"], writes=[xk])
                    work.append(ld)
                    for half in range(2):
                        bank = 5 + half
                        for j0 in range(0, 22, 4):
                            def mm(ti=ti, nt=nt, off=off, half=half, bank=bank, j0=j0):
                                for j in range(j0, min(j0 + 4, 22)):
                                    pe(MM(ps[bank][:nt, :], actT[:, j, off:off + nt], w_dn_sb[:, j, half * 512:(half + 1) * 512],
                                          start=(j == 0), stop=(j == 21)), r=[(ak, j), ("w_dn_sb", j // 11)], w=[PK(bank)])
                            work.append(mm)

                        def add(ti=ti, nt=nt, half=half, bank=bank, xk=xk):
                            dve(TT(xr[ti][:nt, half * 512:(half + 1) * 512], xr[ti][:nt, half * 512:(half + 1) * 512], ps[bank][:nt, :], ALU.add),
                                r=[xk, PK(bank)], w=[xk])
                        work.append(add)

                    def fin(ti=ti, rt0=rt0, nt=nt, xk=xk, sample=sample):
                        dve(MSET(smb[:nt, 1:2], 0.0), w=["smbd"])
                        act(ACT(yt[:nt, :], xr[ti][:nt, :], AF.Square, accum_out=smb[:nt, 1:2]), r=[xk], w=["yt", "smbd"])
                        dve(TS(smb[:nt, 1:2], smb[:nt, 1:2], 1.0 / D, EPS, ALU.mult, ALU.add), r=["smbd"], w=["smbd"])
                        act(ACT(smb[:nt, 1:2], smb[:nt, 1:2], AF.Ln), r=["smbd"], w=["smbd"])
                        act(ACT(smb[:nt, 1:2], smb[:nt, 1:2], AF.Exp, scale=-0.5), r=["smbd"], w=["smbd"])
                        act(ACT(yt[:nt, :], xr[ti][:nt, :], AF.Copy, scale=smb[:nt, 1:2]), r=[xk, "smbd"], w=["yt"])
                        dve(TT(yt[:nt, :], yt[:nt, :], nfinB[:nt, :], ALU.mult), r=["yt", "nfinB"], w=["yt"])
                        if sample:
                            P.dma("sp", DMA(y[SEQ:SEQ + nt, :], yt[:nt, :]), "yt_st", reads=["yt"])
                        else:
                            lo = max(rt0, NMETA)
                            hi = rt0 + nt
                            if hi > lo:
                                P.dma("sp", DMA(y[lo - NMETA:hi - NMETA, :], yt[lo - rt0:hi - rt0, :]), "yt_st", reads=["yt"])
                    work.append(fin)
                return work

            def up_group(g, what):
                rg0, ncol, sample, last_prompt = glist[g]
                h2T = h2T2[g % 2]
                h2k = "h2T%d" % (g % 2)
                actT = actT2[g % 2]
                ak = "actT%d" % (g % 2)
                if (sample or last_prompt) and what in ("tm", "all"):
                    for f in range(11):
                        for k in range(8):
                            pe(MM(ps[1][:ncol, :], h2T[:, k, :ncol], w_up_sb[:, k, f * 512:(f + 1) * 512], start=(k == 0), stop=(k == 7)),
                               r=[h2k, ("w_up_sb", f)], w=[PK(1)])
                        act(ACT(stg[:ncol, :], ps[1][:ncol, :], AF.Copy), r=[PK(1)], w=["stg"])
                        if sample:
                            for i in range(2):
                                P.dma("sp", DMA(o_cf_s[:, i, f * 512:(f + 1) * 512], stg[2 + i:64:4, :]), "stg_st", reads=["stg"])
                        else:
                            P.dma("sp", DMA(o_cf_p[:, f * 512:(f + 1) * 512], stg[ncol - 2:ncol, :]), "stg_st", reads=["stg"])

                def vw(ap2):
                    return ap2 if not sample else ap2.rearrange("p (s j) -> p s j", j=4)

                def stage1(j):
                    pj = j % 2
                    for (cidx, ub, bank, cdst, ubk, cdk, isg) in ((j, ubg2[pj], (2, 4)[pj], cg2[pj], "ubg%d" % pj, "cg%d" % pj, True),
                                                                  (22 + j, ubv2[pj], (3, 7)[pj], cvl2[pj], "ubv%d" % pj, "cvl%d" % pj, False)):
                        for k in range(8):
                            pe(MM(ps[bank][:, :ncol], w_up_sb[:, k, cidx * 128:(cidx + 1) * 128], h2T[:, k, :ncol], start=(k == 0), stop=(k == 7)),
                               r=[h2k, ("w_up_sb", cidx // 4)], w=[PK(bank)])
                        if not sample:
                            pool(CP(ub[:, 0:2], carryf[:, cidx, :]), r=[("carryf", cidx)], w=[(ubk, "c")])
                            act(ACT(ub[:, 2:2 + ncol], ps[bank][:, :ncol], AF.Copy), r=[PK(bank)], w=[(ubk, "m")])
                            pool(CP(carryf[:, cidx, :], ub[:, ncol:ncol + 2]), r=[(ubk, "m")], w=[("carryf", cidx)])
                            tp = [ub[:, i:i + ncol] for i in range(3)]
                        else:
                            ubv3 = ub[:, 0:96].rearrange("p (s j) -> p s j", j=6)
                            slot = pj * 2 + (0 if isg else 1)
                            sctv = h2n2[0].bitcast(F32)
                            sl_ = sctv[:32, slot * 128:(slot + 1) * 128]
                            skey = ("h2n0", slot)
                            P.dma("sp", DMA(sl_, scf_in[:, cidx * 128:(cidx + 1) * 128]), "sct_ld%d" % slot, writes=[skey])
                            pe(TR(ps[0][:, slot * 32:(slot + 1) * 32], sl_, identB_f[:32, :32]), r=[skey, "identB_f"], w=[PK(0)])
                            act(ACT(ubv3[:, :, 0:2], ps[0][:, slot * 32:(slot + 1) * 32].rearrange("p (s i) -> p s i", i=2), AF.Copy), r=[PK(0)], w=[ubk])
                            act(ACT(ubv3[:, :, 2:6], ps[bank][:, :ncol].rearrange("p (s j) -> p s j", j=4), AF.Copy), r=[PK(bank)], w=[ubk])
                            tp = [ubv3[:, :, i:i + 4] for i in range(3)]
                        o_ = vw(cdst[:, :ncol])
                        if isg:
                            act(ACT(o_, tp[0], AF.Copy, scale=cwf[:, cidx, 0:1]), r=[ubk, "cwf"], w=[cdk])
                            dve(STT(o_, tp[1], cwf[:, cidx, 1:2], o_, ALU.mult, ALU.add), r=[ubk, "cwf", cdk], w=[cdk])
                            dve(STT(o_, tp[2], cwf[:, cidx, 2:3], o_, ALU.mult, ALU.add), r=[ubk, "cwf", cdk], w=[cdk])
                        else:
                            pa, pb = vw(ptmp2[pj][0][:, :ncol]), vw(ptmp2[pj][1][:, :ncol])
                            act(ACT(pa, tp[0], AF.Copy, scale=cwf[:, cidx, 0:1]), r=[ubk, "cwf"], w=["pta%d" % pj])
                            pool(TS(pb, tp[1], cwf[:, cidx, 1:2], 0.0, ALU.mult, ALU.add), r=[ubk, "cwf"], w=["ptb%d" % pj])
                            dve(STT(o_, tp[2], cwf[:, cidx, 2:3], pa, ALU.mult, ALU.add), r=[ubk, "cwf", "pta%d" % pj], w=[cdk])
                            dve(TT(o_, o_, pb, ALU.add), r=[cdk, "ptb%d" % pj], w=[cdk])

                def stage2(j):
                    pj = j % 2
                    cg, cvl = cg2[pj], cvl2[pj]
                    act(ACT(cg[:, :ncol], cg[:, :ncol], AF.Silu), r=["cg%d" % pj], w=["cg%d" % pj])
                    dve(TT(actT[:, j, :ncol], cg[:, :ncol], cvl[:, :ncol], ALU.mult), r=["cg%d" % pj, "cvl%d" % pj], w=[(ak, j)])

                if what == "all":
                    for j in range(23):
                        if j < 22:
                            stage1(j)
                        if j >= 1:
                            stage2(j - 1)
                elif what in ("even", "odd"):
                    for j in range(0 if what == "even" else 1, 22, 2):
                        stage1(j)
                        stage2(j)

            def rec_call(f):
                P.rec = []
                f()
                r_ = P.rec
                P.rec = None
                return r_

            front_a(0)
            front_b(0)
            G_ = len(glist)
            for g in range(G_):
                lists = []
                if glist[g][2] or glist[g][3]:
                    lists.append(rec_call(lambda: up_group(g, "tm")))
                lists.append(rec_call(lambda: up_group(g, "even")))
                lists.append(rec_call(lambda: up_group(g, "odd")))
                if g > 0:
                    lists.append(rec_call(lambda: [w_() for w_ in down_work(g - 1)]))
                if g + 1 < G_:
                    lists.append(rec_call(lambda: (front_a(g + 1), front_b(g + 1))))
                P.merge(lists)
            for w_ in down_work(G_ - 1):
                w_()
            P.finalize()

        counts = P.emit(nc)
    return nc, consts, counts


_CACHE = {}


def kernel(x_prompt, x_sample, state_ret, state_gdn, state_conv_qkv, state_ffn_conv,
           meta_tokens, norm_mix, w_in, conv_gdn, gdn_a_log, gdn_dt_bias, norm_ret, norm_gdn, w_out,
           norm_ffn, w_up, conv_ffn, w_down, norm_final):
    f = lambda a: np.ascontiguousarray(np.asarray(a, dtype=np.float32))
    x_prompt, x_sample, state_ret, state_gdn = f(x_prompt), f(x_sample), f(state_ret), f(state_gdn)
    state_conv_qkv, state_ffn_conv, meta_tokens = f(state_conv_qkv), f(state_ffn_conv), f(meta_tokens)
    if "nc" not in _CACHE:
        _CACHE["nc"] = build_program()
    nc, consts, counts = _CACHE["nc"]
    shared = {
        "w_in": f(w_in)[0], "w_out": f(w_out)[0], "w_up": f(w_up)[0], "w_down": f(w_down)[0],
        "nm_col": f(f(norm_mix)[0].reshape(8, 128).T), "nf_col": f(f(norm_ffn)[0].reshape(8, 128).T),
        "cwg": f(f(conv_gdn)[0].reshape(4, 12, 128).transpose(2, 1, 0).reshape(128, 48)),
        "cwf": f(f(conv_ffn)[0].reshape(3, 44, 128).transpose(2, 1, 0).reshape(128, 132)),
        "alog": f(gdn_a_log)[0], "dtb": f(gdn_dt_bias)[0], "nret": f(norm_ret)[0], "ngdn": f(norm_gdn)[0],
        "nfin": f(norm_final),
    }
    for k, v in consts.items():
        shared["c_" + k] = v
    in_maps = []
    for c in range(8):
        m = dict(shared)
        m["x"] = np.concatenate([meta_tokens, x_prompt[c], x_sample[c * NS:(c + 1) * NS].reshape(NS * DS, D)], axis=0)
        m["sall"] = np.concatenate([state_ret[0, c * NS:(c + 1) * NS], state_gdn[0, c * NS:(c + 1) * NS]], axis=1)
        m["scq"] = state_conv_qkv[0, c * NS:(c + 1) * NS].reshape(NS * 3, 1536)
        m["scf"] = state_ffn_conv[0, c * NS:(c + 1) * NS].reshape(NS * 2, 2 * DFF)
        in_maps.append(m)
    res = run_bass_kernel_spmd(nc, in_maps, core_ids=list(range(8)))
    R = res.results
    y_prompt = np.stack([R[c]["y"][:SEQ] for c in range(8)])
    y_sample = np.concatenate([R[c]["y"][SEQ:].reshape(NS, DS, D) for c in range(8)])
    p_ret = np.stack([R[c]["o_sret_p"] for c in range(8)])[None]
    p_gdn = np.stack([R[c]["o_sgdn_p"] for c in range(8)])[None]
    p_cq = np.stack([R[c]["o_cq_p"] for c in range(8)])[None]
    p_cf = np.stack([R[c]["o_cf_p"] for c in range(8)])[None]
    s_ret = np.concatenate([R[c]["o_sall_s"][:, :H] for c in range(8)])[None]
    s_gdn = np.concatenate([R[c]["o_sall_s"][:, H:] for c in range(8)])[None]
    s_cq = np.concatenate([R[c]["o_cq_s"] for c in range(8)])[None]
    s_cf = np.concatenate([R[c]["o_cf_s"] for c in range(8)])[None]
    outs = (y_prompt, y_sample, p_ret, p_gdn, p_cq, p_cf, s_ret, s_gdn, s_cq, s_cf)
    return tuple(np.ascontiguousarray(o, dtype=np.float32) for o in outs)
```

```python
import contextlib
import numpy as np
import concourse.bass as bass
import concourse.mybir as mybir
from concourse.bass_utils import run_bass_kernel_spmd

F32 = mybir.dt.float32
BF16 = mybir.dt.bfloat16
AF = mybir.ActivationFunctionType
ALU = mybir.AluOpType
AX = mybir.AxisListType

D = 1024
NMETA = 16
SEQ = 2048
LP = NMETA + SEQ
NS = 16
DS = 4
NROW = LP + NS * DS
H = 4
DH = 128
INW = 4104
DFF = 2816
EPS = 1e-6
PAST = 16384
GAM = [1.0 - 2.0 ** (-5 - h) for h in range(H)]
QSCALE = DH ** -0.5
NEG = -1.0e5


class _Op:
    __slots__ = ("fn", "waits", "signal", "dma", "idx")

    def __init__(self, fn, waits, dma):
        self.fn = fn
        self.waits = waits
        self.signal = False
        self.dma = dma
        self.idx = None


class Plan:
    ENGS = ("pe", "act", "dve", "pool", "sp")

    def __init__(self, same_engine_sync=True):
        self.streams = {e: [] for e in self.ENGS}
        self.last_write = {}
        self.readers = {}
        self.by_root = {}
        self.dma_count = {}
        self.waited = {e: {} for e in self.ENGS}
        self.same_engine_sync = same_engine_sync
        self.barrier_toks = []
        self.rec = None
        self.t_eng = {}
        self.t_tok = {}

    @staticmethod
    def _k(r):
        return r if isinstance(r, tuple) else (r,)

    def _conf(self, key):
        for k in self.by_root.get(key[0], ()):
            n = min(len(k), len(key))
            if k[:n] == key[:n]:
                yield k

    def _deps(self, reads, writes, eng=None):
        toks = list(self.barrier_toks)
        for r in reads:
            key = self._k(r)
            psum = isinstance(key[0], str) and key[0].startswith("ps") and key[0][2:].isdigit()
            for k in self._conf(key):
                t = self.last_write.get(k)
                if t is not None:
                    toks.append(t)
                if psum:
                    toks.extend(t2 for t2 in self.readers.get(k, ()) if not (t2[0] == "eng" and t2[1] == eng))
        for w in writes:
            key = self._k(w)
            for k in self._conf(key):
                t = self.last_write.get(k)
                if t is not None:
                    toks.append(t)
                toks.extend(self.readers.get(k, ()))
        return toks

    def _commit(self, tok, reads, writes):
        for w in writes:
            key = self._k(w)
            for k in list(self._conf(key)):
                if len(k) >= len(key):
                    self.readers.pop(k, None)
                    if k != key:
                        self.last_write.pop(k, None)
            self.by_root.setdefault(key[0], set()).add(key)
            self.last_write[key] = tok
        for r in reads:
            key = self._k(r)
            self.by_root.setdefault(key[0], set()).add(key)
            self.readers.setdefault(key, []).append(tok)

    def _filter(self, eng, toks):
        wd = self.waited[eng]
        best = {}
        for t in toks:
            kind, who, val = t
            if kind == "eng" and who == eng:
                if eng in ("pe", "sp") or not self.same_engine_sync:
                    continue
            key = (kind, who)
            if wd.get(key, -1) >= val:
                continue
            if key not in best or best[key][2] < val:
                best[key] = t
        for key, t in best.items():
            wd[key] = t[2]
        return list(best.values())

    def op(self, eng, fn, reads=(), writes=()):
        if self.rec is not None:
            self.rec.append(("op", eng, fn, None, tuple(reads), tuple(writes)))
            return None
        toks = self._filter(eng, self._deps(reads, writes, eng))
        o = _Op(fn, toks, None)
        o.idx = len(self.streams[eng])
        self.streams[eng].append(o)
        tok = ("eng", eng, o.idx)
        self._commit(tok, reads, writes)
        return tok

    def dma(self, eng, fn, semkey, reads=(), writes=()):
        if self.rec is not None:
            self.rec.append(("dma", eng, fn, semkey, tuple(reads), tuple(writes)))
            return None
        toks = self._filter(eng, self._deps(reads, writes))
        cnt = self.dma_count.get(semkey, 0) + 16
        self.dma_count[semkey] = cnt
        o = _Op(fn, toks, (semkey, cnt))
        o.idx = len(self.streams[eng])
        self.streams[eng].append(o)
        tok = ("dma", semkey, cnt)
        self._commit(tok, reads, writes)
        return tok

    def merge(self, lists):
        lists = [l for l in lists if l]
        heads = [0] * len(lists)
        rem = []
        for l in lists:
            acc, suf = 0.0, [0.0] * (len(l) + 1)
            for i_ in range(len(l) - 1, -1, -1):
                acc += getattr(l[i_][2], "cost", 0.3) + 0.1
                suf[i_] = acc
            rem.append(suf)
        while True:
            best = None
            for li, l in enumerate(lists):
                if heads[li] >= len(l):
                    continue
                kind, eng, fn, semkey, reads, writes = l[heads[li]]
                toks = self._deps(reads, writes, eng)
                ready = 0.0
                for t in toks:
                    tt_ = self.t_tok.get(t)
                    if tt_ is not None and tt_ > ready:
                        ready = tt_
                start = max(self.t_eng.get(eng, 0.0), ready)
                cand = (start, -rem[li][heads[li]], li)
                if best is None or cand < best:
                    best = cand
            if best is None:
                break
            start, _, li = best
            kind, eng, fn, semkey, reads, writes = lists[li][heads[li]]
            heads[li] += 1
            if kind == "op":
                tok = self.op(eng, fn, reads, writes)
                cost = getattr(fn, "cost", 0.3)
                self.t_eng[eng] = start + cost
                self.t_tok[tok] = start + cost + 0.06
            else:
                tok = self.dma(eng, fn, semkey, reads, writes)
                self.t_eng[eng] = start + 0.1
                self.t_tok[tok] = start + 2.5

    def fix_group(self, semkey):
        fin = self.dma_count[semkey]
        for k, t in list(self.last_write.items()):
            if t[0] == "dma" and t[1] == semkey:
                self.last_write[k] = ("dma", semkey, fin)
        for k, lst in self.readers.items():
            self.readers[k] = [("dma", semkey, fin) if (t[0] == "dma" and t[1] == semkey) else t for t in lst]

    def barrier(self):
        toks = []
        for e in ("pe", "act", "dve", "pool"):
            for o in reversed(self.streams[e]):
                if o.dma is None and o.fn is not None:
                    toks.append(("eng", e, o.idx))
                    break
        toks += [("dma", k, c) for k, c in self.dma_count.items()]
        self.barrier_toks = toks

    def finalize(self):
        toks = [("dma", k, c) for k, c in self.dma_count.items()]
        toks = self._filter("sp", toks)
        o = _Op(None, toks, None)
        o.idx = len(self.streams["sp"])
        self.streams["sp"].append(o)

    def emit(self, nc):
        for e in self.ENGS:
            for o in self.streams[e]:
                for (kind, who, val) in o.waits:
                    if kind == "eng":
                        self.streams[who][val].signal = True
        semval = {}
        for e in self.ENGS:
            c = 0
            for o in self.streams[e]:
                if o.signal:
                    c += 1
                    semval[(e, o.idx)] = c
        with contextlib.ExitStack() as st:
            esem = {e: st.enter_context(nc.semaphore("s_" + e)) for e in self.ENGS}
            dsem = {k: st.enter_context(nc.semaphore("d_%d" % i)) for i, k in enumerate(self.dma_count)}
            block = st.enter_context(nc.Block())

            def run(engname):
                def body(e):
                    for o in self.streams[engname]:
                        for (kind, who, val) in o.waits:
                            if kind == "eng":
                                e.wait_ge(esem[who], semval[(who, val)])
                            else:
                                e.wait_ge(dsem[who], val)
                        if o.fn is None:
                            continue
                        ins = o.fn(e)
                        if o.dma is not None:
                            ins.then_inc(dsem[o.dma[0]], 16)
                        elif o.signal:
                            ins.then_inc(esem[engname], 1)
                return body

            block.tensor(run("pe"))
            block.scalar(run("act"))
            block.vector(run("dve"))
            block.gpsimd(run("pool"))
            block.sync(run("sp"))
        return {e: len(self.streams[e]) for e in self.ENGS}


class Ins:
    def __init__(self, fn, cost):
        self.fn = fn
        self.cost = cost

    def __call__(self, e):
        return self.fn(e)


def _fsz(ap):
    n = 1
    for d in tuple(ap.shape)[1:]:
        n *= int(d)
    return n


def MM(out, lhsT, rhs, start=True, stop=True, skip=False):
    cost = max(_fsz(out) * (2 if lhsT.dtype == F32 else 1) / 2400.0, 0.107) + 0.01
    if skip:
        return Ins(lambda e: e.matmul(out, lhsT=lhsT, rhs=rhs, start=start, stop=stop, skip_group_check=True), cost)
    return Ins(lambda e: e.matmul(out, lhsT=lhsT, rhs=rhs, start=start, stop=stop), cost)


def TR(out, in_, idt):
    return Ins(lambda e: e.transpose(out, in_, idt), max(_fsz(out) * (2 if in_.dtype == F32 else 1) / 2400.0, 0.107) + 0.01)


def ACT(out, in_, func, **kw):
    return Ins(lambda e: e.activation(out, in_, func, **kw), _fsz(out) / 1000.0 + 0.12)


def TT(out, in0, in1, op):
    return Ins(lambda e: e.tensor_tensor(out=out, in0=in0, in1=in1, op=op), _fsz(out) / 1000.0 + 0.07)


def TS(out, in0, s1, s2, op0, op1):
    return Ins(lambda e: e.tensor_scalar(out, in0, s1, s2, op0=op0, op1=op1), _fsz(out) / 1000.0 + 0.07)


def STT(out, in0, scalar, in1, op0, op1):
    return Ins(lambda e: e.scalar_tensor_tensor(out=out, in0=in0, scalar=scalar, in1=in1, op0=op0, op1=op1), _fsz(out) / 1000.0 + 0.07)


def RED(out, in_):
    return Ins(lambda e: e.tensor_reduce(out, in_, axis=AX.X, op=ALU.add), _fsz(in_) / 1000.0 + 0.07)


def CP(out, in_):
    return Ins(lambda e: e.tensor_copy(out, in_), _fsz(out) / 1000.0 + 0.07)


def RCP(out, in_):
    return Ins(lambda e: e.reciprocal(out, in_), _fsz(out) / 1000.0 + 0.07)


def MSET(ap, v):
    return Ins(lambda e: e.memset(ap, v), _fsz(ap) / 1000.0 + 0.07)


def DMA(out, in_):
    return Ins(lambda e: e.dma_start(out=out, in_=in_), 2.0)


def _tile_types():
    return {"meta": (16, 16), "prm": (128, 128), "smp": (64, 4)}


def host_constants():
    c = {}
    c["ident"] = np.eye(128, dtype=np.float32)
    pos = np.concatenate([np.arange(LP), np.tile(PAST + np.arange(DS), NS)]).astype(np.float32)
    cl = np.concatenate([np.arange(NMETA), np.tile(np.arange(128), SEQ // 128), np.tile(np.arange(DS), NS)])
    cn = np.concatenate([np.full(NMETA, NMETA), np.full(SEQ, 128), np.full(NS * DS, DS)])
    half = DH // 2
    inv_freq = np.power(np.float32(10000.0), -np.arange(half, dtype=np.float32) / np.float32(half)).astype(np.float32)
    ang = (pos[:, None] * inv_freq[None, :]).astype(np.float32)
    cos = np.cos(ang).astype(np.float32).astype(np.float64)
    sin = np.sin(ang).astype(np.float32).astype(np.float64)
    g = np.array(GAM, dtype=np.float64)
    qdec = g[None, :] ** (cl[:, None] + 1.0)
    kdec = g[None, :] ** (cn[:, None] - 1.0 - cl[:, None])
    rope = np.zeros((NROW, 644), np.float64)
    rope[:, 0:256] = (cos[:, None, :] * qdec[:, :, None]).reshape(NROW, 256)
    rope[:, 256:512] = (sin[:, None, :] * qdec[:, :, None]).reshape(NROW, 256)
    rope[:, 512:576] = cos * QSCALE
    rope[:, 576:640] = sin * QSCALE
    rope[:, 640:644] = kdec
    c["rope"] = rope.astype(np.float32)
    for name, (n, sl) in _tile_types().items():
        idx = np.arange(n)
        seq = idx // sl
        loc = idx % sl
        same = seq[:, None] == seq[None, :]
        ok = same & (idx[None, :] >= idx[:, None])
        mr = np.zeros((n, H, n), np.float64)
        for h in range(H):
            mr[:, h, :] = np.where(ok, (g[h] ** (-(loc[:, None] + 1.0))), 0.0)
        c["mr_" + name] = mr.astype(np.float32)
        c["nma_" + name] = np.where(same & (idx[:, None] > idx[None, :]), 0.0, NEG).astype(np.float32)
        c["nmq_" + name] = np.where(same & (idx[None, :] >= idx[:, None]), 0.0, NEG).astype(np.float32)
        c["tri_" + name] = (same & (idx[:, None] <= idx[None, :])).astype(np.float32)
        c["blk_" + name] = same.astype(np.float32)
    n = NS * DS
    seq = np.arange(n) // DS
    c["mfm"] = np.broadcast_to((np.arange(NS)[:, None] == seq[None, :]).astype(np.float32)[None], (128, NS, n)).copy()
    c["mtm"] = (seq[:, None] == np.arange(NS)[None, :]).astype(np.float32)
    return c


CONST_SHAPES = None


def build_program():
    nc = bass.Bass("TRN2", target_bir_lowering=False)
    consts = host_constants()

    def din(name, shape):
        return nc.dram_tensor(name, list(shape), F32, kind="ExternalInput").ap()

    def dout(name, shape):
        return nc.dram_tensor(name, list(shape), F32, kind="ExternalOutput").ap()

    x = din("x", [NROW, D])
    sall_in = din("sall", [NS, 2 * H, DH, DH])
    scq_in = din("scq", [NS * 3, 1536])
    scf_in = din("scf", [NS * 2, 2 * DFF])
    w_in = din("w_in", [D, INW])
    w_out = din("w_out", [D, D])
    w_up = din("w_up", [D, 2 * DFF])
    w_down = din("w_down", [DFF, D])
    nm_col = din("nm_col", [128, 8])
    nf_col = din("nf_col", [128, 8])
    cwg_in = din("cwg", [128, 48])
    cwf_in = din("cwf", [128, 132])
    alog_in = din("alog", [4])
    dtb_in = din("dtb", [4])
    nret_in = din("nret", [512])
    ngdn_in = din("ngdn", [128])
    nfin_in = din("nfin", [D])
    cd = {k: din("c_" + k, v.shape) for k, v in consts.items()}

    y = dout("y", [SEQ + NS * DS, D])
    o_sret_p = dout("o_sret_p", [H, DH, DH])
    o_sgdn_p = dout("o_sgdn_p", [H, DH, DH])
    o_cq_p = dout("o_cq_p", [3, 1536])
    o_cf_p = dout("o_cf_p", [2, 2 * DFF])
    o_sall_s = dout("o_sall_s", [NS, 2 * H, DH, DH])
    o_cq_s = dout("o_cq_s", [NS, 3, 1536])
    o_cf_s = dout("o_cf_s", [NS, 2, 2 * DFF])
    x1s = nc.dram_tensor("x1s", [NROW, D], F32, kind="Internal").ap()

    P = Plan()
    pe = lambda fn, r=(), w=(): P.op("pe", fn, r, w)
    act = lambda fn, r=(), w=(): P.op("act", fn, r, w)
    dve = lambda fn, r=(), w=(): P.op("dve", fn, r, w)
    pool = lambda fn, r=(), w=(): P.op("pool", fn, r, w)

    with contextlib.ExitStack() as top:
        ps = [top.enter_context(nc.psum_tensor("ps%d" % i, [128, 512], F32)) for i in range(8)]
        psb = [p.bitcast(BF16) for p in ps]
        PK = lambda i: "ps%d" % i

        with contextlib.ExitStack() as st:
            def T(name, shape, dt=F32):
                return st.enter_context(nc.sbuf_tensor("sb_" + name, list(shape), dt))

            w_in_sb = T("w_in_sb", [128, 8, INW], BF16)
            w_out_sb = T("w_out_sb", [128, 8, D], BF16)
            ident_f = T("ident_f", [128, 128])
            ident_b = T("ident_b", [128, 128], BF16)
            negid = T("negid", [128, 128])
            ones_f = T("ones_f", [128, 128])
            negones = T("negones", [128, 128])
            msk = {}
            for name, (n, sl) in _tile_types().items():
                msk[name] = dict(
                    mr=T("mr_" + name, [n, H, n]), nma=T("nma_" + name, [n, n]), nmq=T("nmq_" + name, [n, n]),
                    tri=T("tri_" + name, [n, n]), blk=T("blk_" + name, [n, n]))
            mfm = T("mfm", [128, NS, NS * DS], BF16)
            mtm = T("mtm", [NS * DS, NS])
            nrB = T("nrB", [128, 512])
            ngB = T("ngB", [128, 128])
            nmc = T("nmc", [128, 8])
            cwg = T("cwg", [128, 12, 4])
            dtb = T("dtb", [128, 4])
            negA = T("negA", [128, 4])
            xn = T("xn", [128, D], BF16)
            hT2 = [T("hT0", [128, 8, 128], BF16), T("hT1", [128, 8, 128], BF16)]
            xt = T("xt", [128, D])
            xres = T("xres", [128, D])
            rp = T("rp", [128, 644])
            ta = T("ta", [128, 256])
            tb = T("tb", [128, 256])
            qr = T("qr", [128, H, DH], BF16)
            kT = T("kT", [128, H, 128], BF16)
            pc = T("pc", [128, 12, 131])
            carry = T("carry", [128, 12, 3])
            cv2 = [T("cv0", [128, 12, 128]), T("cv1", [128, 12, 128])]
            sqb = T("sqb", [128, 8, 128])
            kTg = T("kTg", [128, H, 128], BF16)
            qTg = T("qTg", [128, H, 128], BF16)
            sm = T("sm", [128, 64])
            g1 = T("g1", [128, H, 128])
            g2 = T("g2", [128, H, 128])
            Am = [T("Am0", [128, H, 128]), T("Am1", [128, H, 128])]
            Bm = [T("Bm0", [128, H, 128]), T("Bm1", [128, H, 128])]
            kb = T("kb", [128, H, DH])
            cst = T("cst", [128, 512])
            cen = T("cen", [128, 512])
            sqh = T("sqh", [128, 512])
            mixed = T("mixed", [128, D], BF16)
            mT = T("mT", [128, 8, 128], BF16)
            w_sb = T("w_sb", [128, H, DH], BF16)
            sret = T("sret", [128, H, DH])
            sgdn = T("sgdn", [128, H, DH])
            sretb = T("sretb", [128, H, DH], BF16)
            sgdnb = T("sgdnb", [128, H, DH], BF16)
            smy = T("smy", [128, 16])

            HSPEC = [("qdT", [128, H, 128], BF16), ("scT", [128, H, 128], BF16),
                     ("vr", [128, 512], BF16), ("vk", [128, H, DH], BF16), ("kr", [128, H, DH], BF16),
                     ("Pm", [128, H, 128], F32), ("vb", [128, H, DH], F32), ("nwkT", [128, H, 128], BF16),
                     ("qdTg", [128, H, 128], BF16), ("qkT", [128, H, 128], BF16), ("kdec", [128, H, DH], BF16),
                     ("sgr", [128, 512], F32), ("sgg", [128, 512], F32), ("CDt", [128, NS * H], F32)]

            def alloc_hset(p, TT_):
                return {nm: TT_("%s%d" % (nm, p), shp, dt) for (nm, shp, dt) in HSPEC}

            hset = {1: alloc_hset(1, T)}

            SU = "setup"
            P.dma("pool", DMA(ident_b[:], cd["ident"]), "identb", writes=["ident_b"])
            w_in_v = w_in.rearrange("(k p) f -> p k f", p=128)
            for (tag, c0, c1) in (("q", 2048, 3584), ("a", 0, 2048), ("g", 3584, INW)):
                P.dma("pool", DMA(w_in_sb[:, :, c0:c1], w_in_v[:, :, c0:c1]), "w_in_" + tag, writes=[("w_in_sb", tag)])
            P.dma("pool", DMA(w_out_sb[:], w_out.rearrange("(k p) m -> p k m", p=128)), "w_out", writes=["w_out_sb"])
            P.dma("pool", DMA(mfm[:], cd["mfm"]), "mfm", writes=["mfm"])
            sp_setup = [(ident_f[:], cd["ident"], "ident_f"), (mtm[:], cd["mtm"], "mtm"),
                        (nrB[:], nret_in.partition_broadcast(128), "nrB"), (ngB[:], ngdn_in.partition_broadcast(128), "ngB"),
                        (nmc[:], nm_col, "nmc"), (cwg[:].rearrange("p c i -> p (c i)"), cwg_in, "cwg"),
                        (dtb[:], dtb_in.partition_broadcast(128), "dtb"), (negA[:], alog_in.partition_broadcast(128), "negA")]
            for name in msk:
                for kk_ in ("mr", "nma", "nmq", "tri", "blk"):
                    sp_setup.append((msk[name][kk_][:], cd[kk_ + "_" + name], kk_ + "_" + name))
            for (o_, i_, key) in sp_setup:
                P.dma("sp", DMA(o_, i_), SU, writes=[key])
            P.fix_group(SU)
            dve(MSET(ones_f[:], 1.0), w=["ones_f"])
            dve(MSET(negones[:], -1.0), w=["negones"])
            dve(TS(negid[:], ident_f[:], -1.0, 0.0, ALU.mult, ALU.add), r=["ident_f"], w=["negid"])
            act(ACT(negA[:], negA[:], AF.Exp), r=["negA"], w=["negA"])
            dve(TS(negA[:], negA[:], -1.0, 0.0, ALU.mult, ALU.add), r=["negA"], w=["negA"])
            dve(MSET(sret[:], 0.0), w=["sret"])
            dve(MSET(sgdn[:], 0.0), w=["sgdn"])
            dve(MSET(sretb[:], 0.0), w=["sretb"])
            dve(MSET(sgdnb[:], 0.0), w=["sgdnb"])
            dve(MSET(carry[:], 0.0), w=["carry"])

            def bc_mid(ap2, n, reps):
                return ap2.unsqueeze(1).to_broadcast([n, reps, ap2.shape[1]])

            def bc_last(ap2, n, m):
                return ap2.unsqueeze(2).to_broadcast([n, ap2.shape[1], m])

            def rstd_chain(dst, src, scale, r_keys, key):
                dve(TS(dst, src, scale, EPS, ALU.mult, ALU.add), r=r_keys, w=[key])
                act(ACT(dst, dst, AF.Ln), r=[key], w=[key])
                act(ACT(dst, dst, AF.Exp, scale=-0.5), r=[key], w=[key])

            def stage_x1(r0, n, tname, sample, last_prompt, q):
                hT, cv = hT2[q], cv2[q]
                KH, KC = "hT%d" % q, "cv%d" % q
                P.dma("sp", DMA(xt[:n, :], x[r0:r0 + n, :]), "xt_ld", writes=["xt"])
                junk = pc[:].rearrange("p c n -> p (c n)")[:n, 0:D]
                dve(MSET(sm[:n, 0:1], 0.0), w=["sm0"])
                act(ACT(junk, xt[:n, :], AF.Square, accum_out=sm[:n, 0:1]), r=["xt"], w=["pc", "sm0"])
                rstd_chain(sm[:n, 0:1], sm[:n, 0:1], 1.0 / D, ["sm0"], "sm0")
                act(ACT(xn[:n, :], xt[:n, :], AF.Copy, scale=sm[:n, 0:1]), r=["xt", "sm0"], w=["xn"])
                for k in range(8):
                    pe(TR(psb[0][:, k * 128:k * 128 + n], xn[:n, k * 128:(k + 1) * 128], ident_b[:n, :n]),
                       r=["xn", "ident_b"], w=[PK(0)])
                dve(TT(hT[:, :, :n], psb[0][:].rearrange("p (k t) -> p k t", k=8)[:, :, :n],
                       bc_last(nmc[:, :], 128, n), ALU.mult), r=[PK(0), "nmc"], w=[KH])
                yield
                if not sample:
                    dve(CP(pc[:, :, 0:3], carry[:, :, :]), r=["carry"], w=["pc"])
                else:
                    scq_tm = cv[:].rearrange("p c n -> p (c n)")[:48, :]
                    P.dma("sp", DMA(scq_tm, scq_in), "scq_ld", writes=[KC])
                    pcv = pc[:, :, 0:112].rearrange("p c (s j) -> p c s j", j=7)
                    for c in range(12):
                        bank = 2 if c < 6 else 0
                        off = (c % 6) * 48
                        pe(TR(ps[bank][:, off:off + 48], scq_tm[:, c * 128:(c + 1) * 128], ident_f[:48, :48]),
                           r=[KC, "ident_f"], w=[PK(bank)])
                    for c in range(12):
                        bank = 2 if c < 6 else 0
                        off = (c % 6) * 48
                        act(ACT(pcv[:, c, :, 0:3], ps[bank][:, off:off + 48].rearrange("p (s i) -> p s i", i=3), AF.Copy),
                            r=[PK(bank)], w=["pc"])
                fm_banks = [1, 2, 1]
                for gi in range(3):
                    bank = fm_banks[gi]
                    for c in range(gi * 4, gi * 4 + 4):
                        for k in range(8):
                            pe(MM(ps[bank][:, (c % 4) * 128:(c % 4) * 128 + n], w_in_sb[:, k, 2048 + c * 128:2048 + (c + 1) * 128],
                                  hT[:, k, :n], start=(k == 0), stop=(k == 7)), r=[KH, ("w_in_sb", "q")], w=[PK(bank)])
                    if not sample:
                        act(ACT(pc[:, gi * 4:(gi + 1) * 4, 3:3 + n], ps[bank][:].rearrange("p (c t) -> p c t", c=4)[:, :, :n], AF.Copy),
                            r=[PK(bank)], w=["pc"])
                    else:
                        for c in range(gi * 4, gi * 4 + 4):
                            act(ACT(pcv[:, c, :, 3:7], ps[bank][:, (c % 4) * 128:(c % 4) * 128 + n].rearrange("p (s j) -> p s j", j=4), AF.Copy),
                                r=[PK(bank)], w=["pc"])
                    yield
                if not sample:
                    dve(CP(carry[:, :, :], pc[:, :, n:n + 3]), r=["pc"], w=["carry"])
                for c in range(12):
                    if not sample:
                        o_ = cv[:, c, :n]
                        tp = [pc[:, c, i:i + n] for i in range(4)]
                    else:
                        o_ = cv[:, c, :n].rearrange("p (s j) -> p s j", j=4)
                        tp = [pcv[:, c, :, i:i + 4] for i in range(4)]
                    act(ACT(o_, tp[0], AF.Copy, scale=cwg[:, c, 0:1]), r=["pc", "cwg"], w=[(KC, c)])
                    for i in range(1, 4):
                        dve(STT(o_, tp[i], cwg[:, c, i:i + 1], o_, ALU.mult, ALU.add), r=["pc", "cwg", (KC, c)], w=[(KC, c)])
                    if c % 3 == 2:
                        yield
                act(ACT(cv[:, :, :n], cv[:, :, :n], AF.Silu), r=[KC], w=[KC])
                if sample or last_prompt:
                    cols = slice(0, n) if sample else slice(n - 3, n)
                    m_ = n if sample else 3
                    for j in range(3):
                        for k in range(8):
                            pe(MM(ps[2][:m_, :512], hT[:, k, cols], w_in_sb[:, k, 2048 + j * 512:2048 + (j + 1) * 512],
                                  start=(k == 0), stop=(k == 7)), r=[KH, ("w_in_sb", "q")], w=[PK(2)])
                        act(ACT(cst[:m_, :], ps[2][:m_, :512], AF.Copy), r=[PK(2)], w=["cst"])
                        if sample:
                            for i in range(3):
                                P.dma("sp", DMA(o_cq_s[:, i, j * 512:(j + 1) * 512], cst[1 + i:64:4, :]), "cst_st", reads=["cst"])
                        else:
                            P.dma("sp", DMA(o_cq_p[:, j * 512:(j + 1) * 512], cst[:3, :]), "cst_st", reads=["cst"])
                yield
                act(ACT(sqb[:, :, :n], cv[:, 0:8, :n], AF.Square), r=[KC], w=["sqb"])
                for half in range(2):
                    bank = 1 + half
                    if n == 128:
                        pe(MM(ps[bank][:, :], ones_f[:, :], sqb[:, half * 4:(half + 1) * 4, :].rearrange("p c t -> p (c t)")),
                           r=["ones_f", "sqb"], w=[PK(bank)])
                    else:
                        for c4 in range(4):
                            pe(MM(ps[bank][:, c4 * 128:c4 * 128 + n], ones_f[:, :], sqb[:, half * 4 + c4, :n]),
                               r=["ones_f", "sqb"], w=[PK(bank)])
                for half in range(2):
                    bank = 1 + half
                    dve(TS(sqb[:, half * 4:(half + 1) * 4, :n], ps[bank][:].rearrange("p (c t) -> p c t", c=4)[:, :, :n],
                           1.0, EPS, ALU.mult, ALU.add), r=[PK(bank)], w=["sqb"])
                act(ACT(sqb[:, :, :n], sqb[:, :, :n], AF.Ln), r=["sqb"], w=["sqb"])
                act(ACT(sqb[:, :, :n], sqb[:, :, :n], AF.Exp, scale=-0.5), r=["sqb"], w=["sqb"])
                dve(TT(cv[:, 0:8, :n], cv[:, 0:8, :n], sqb[:, :, :n], ALU.mult), r=[KC, "sqb"], w=[KC])
                yield

            def stage_x2r(r0, n, tname, sample, last_prompt, p, q):
                hb = hset[p]
                K = lambda nm: "%s%d" % (nm, p)
                hT = hT2[q]
                KH = "hT%d" % q
                qdT, scT, vr, vk, kr = hb["qdT"], hb["scT"], hb["vr"], hb["vk"], hb["kr"]
                sgr, sgg = hb["sgr"], hb["sgg"]
                M = msk[tname]
                P.dma("sp", DMA(rp[:n, :], cd["rope"][r0:r0 + n, :]), "rp_ld", writes=["rp"])

                def inproj(c0):
                    wtag = "a" if c0 < 2048 else "g"
                    for k in range(8):
                        pe(MM(ps[3][:n, :], hT[:, k, :n], w_in_sb[:, k, c0:c0 + 512], start=(k == 0), stop=(k == 7)),
                           r=[KH, ("w_in_sb", wtag)], w=[PK(3)])

                def rotary(dst, dkey, cq, sq):
                    v4 = ps[3][:n, :].rearrange("p (h t d) -> p h t d", h=H, t=2)
                    x1_, x2_ = v4[:, :, 0, :], v4[:, :, 1, :]
                    ta3 = ta[:n, :].rearrange("p (h d) -> p h d", h=H)
                    tb3 = tb[:n, :].rearrange("p (h d) -> p h d", h=H)
                    dve(TT(ta3, x1_, cq, ALU.mult), r=[PK(3), "rp"], w=["ta"])
                    dve(TT(tb3, x2_, sq, ALU.mult), r=[PK(3), "rp"], w=["tb"])
                    dve(TT(dst[:n, :, 0:64], ta3, tb3, ALU.subtract), r=["ta", "tb"], w=[dkey])
                    dve(TT(ta3, x2_, cq, ALU.mult), r=[PK(3), "rp"], w=["ta"])
                    dve(TT(tb3, x1_, sq, ALU.mult), r=[PK(3), "rp"], w=["tb"])
                    dve(TT(dst[:n, :, 64:128], ta3, tb3, ALU.add), r=["ta", "tb"], w=[dkey])

                inproj(0)
                rotary(qr, "qr", rp[:n, 0:256].rearrange("p (h d) -> p h d", h=H),
                       rp[:n, 256:512].rearrange("p (h d) -> p h d", h=H))
                yield
                inproj(512)
                rotary(kr, K("kr"), bc_mid(rp[:n, 512:576], n, H), bc_mid(rp[:n, 576:640], n, H))
                yield
                inproj(1024)
                act(ACT(vr[:n, :], ps[3][:n, :], AF.Copy), r=[PK(3)], w=[K("vr")])
                dve(TT(vk[:n, :, :], ps[3][:n, :].rearrange("p (h d) -> p h d", h=H), bc_last(rp[:n, 640:644], n, DH), ALU.mult),
                    r=[PK(3), "rp"], w=[K("vk")])
                inproj(1536)
                act(ACT(sgr[:n, :], ps[3][:n, :], AF.Silu), r=[PK(3)], w=[K("sgr")])
                inproj(3584)
                act(ACT(sgg[:n, :], ps[3][:n, :], AF.Silu), r=[PK(3)], w=[K("sgg")])
                yield
                for h in range(H):
                    pe(TR(psb[3][:, h * 128:h * 128 + n], qr[:n, h, :], ident_b[:n, :n]), r=["qr", "ident_b"], w=[PK(3)])
                    pe(TR(psb[3][:, (4 + h) * 128:(4 + h) * 128 + n], kr[:n, h, :], ident_b[:n, :n]), r=[K("kr"), "ident_b"], w=[PK(3)])
                pbv = psb[3][:].rearrange("p (k t) -> p k t", k=8)
                act(ACT(qdT[:, :, :n], pbv[:, 0:4, :n], AF.Copy), r=[PK(3)], w=[K("qdT")])
                act(ACT(kT[:, :, :n], pbv[:, 4:8, :n], AF.Copy), r=[PK(3)], w=["kT"])
                p3v = ps[3][:].rearrange("p (h t) -> p h t", h=H)
                for h in range(H):
                    pe(MM(ps[3][:n, h * 128:h * 128 + n], kT[:, h, :n], qdT[:, h, :n]), r=["kT", K("qdT")], w=[PK(3)])
                dve(TT(scT[:n, :, :n], p3v[:n, :, :n], M["mr"][:n, :, :n], ALU.mult), r=[PK(3), "mr_" + tname], w=[K("scT")])
                yield

            def stage_x2g(r0, n, tname, sample, last_prompt, p, q):
                hb = hset[p]
                K = lambda nm: "%s%d" % (nm, p)
                hT, cv = hT2[q], cv2[q]
                KH, KC = "hT%d" % q, "cv%d" % q
                qkn = cv
                Pm, vb, nwkT, qdTg, qkT, kdec, CDt = hb["Pm"], hb["vb"], hb["nwkT"], hb["qdTg"], hb["qkT"], hb["kdec"], hb["CDt"]
                M = msk[tname]
                levels = {"meta": 3, "prm": 6, "smp": 1}[tname]
                p4v = ps[4][:].rearrange("p (h t) -> p h t", h=H)
                p6v = ps[6][:].rearrange("p (h t) -> p h t", h=H)
                for k in range(8):
                    pe(MM(ps[6][:n, 0:8], hT[:, k, :n], w_in_sb[:, k, 4096:4104], start=(k == 0), stop=(k == 7)),
                       r=[KH, ("w_in_sb", "g")], w=[PK(6)])
                dve(CP(sm[:n, 8:16], ps[6][:n, 0:8]), r=[PK(6)], w=["smg"])
                act(ACT(kTg[:, :, :n], qkn[:, 4:8, :n], AF.Copy), r=[KC], w=["kTg"])
                act(ACT(qTg[:, :, :n], qkn[:, 0:4, :n], AF.Copy, scale=QSCALE), r=[KC], w=["qTg"])
                dve(TT(sm[:n, 16:20], sm[:n, 12:16], dtb[:n, :], ALU.add), r=["smg", "dtb"], w=["smg"])
                act(ACT(sm[:n, 16:20], sm[:n, 16:20], AF.Exp), r=["smg"], w=["smg"])
                act(ACT(sm[:n, 16:20], sm[:n, 16:20], AF.Ln, bias=1.0), r=["smg"], w=["smg"])
                dve(TT(sm[:n, 16:20], sm[:n, 16:20], negA[:n, :], ALU.mult), r=["smg", "negA"], w=["smg"])
                act(ACT(sm[:n, 20:24], sm[:n, 8:12], AF.Exp, scale=-1.0), r=["smg"], w=["smg"])
                dve(TS(sm[:n, 20:24], sm[:n, 20:24], 1.0, 1.0, ALU.mult, ALU.add), r=["smg"], w=["smg"])
                act(ACT(sm[:n, 24:28], sm[:n, 20:24], AF.Ln), r=["smg"], w=["smg"])
                dve(TS(sm[:n, 24:28], sm[:n, 24:28], -1.0, 0.0, ALU.mult, ALU.add), r=["smg"], w=["smg"])
                dve(RCP(sm[:n, 20:24], sm[:n, 20:24]), r=["smg"], w=["smg"])
                pe(MM(ps[6][:n, 8:12], M["tri"][:n, :n], sm[:n, 16:20]), r=["smg", "tri_" + tname], w=[PK(6)])
                pe(MM(ps[6][:n, 12:16], M["blk"][:n, :n], sm[:n, 16:20]), r=["smg", "blk_" + tname], w=[PK(6)])
                dve(CP(sm[:n, 28:36], ps[6][:n, 8:16]), r=[PK(6)], w=["smg"])
                act(ACT(sm[:n, 36:40], sm[:n, 28:32], AF.Exp), r=["smg"], w=["smg"])
                dve(TT(sm[:n, 40:44], sm[:n, 32:36], sm[:n, 28:32], ALU.subtract), r=["smg"], w=["smg"])
                act(ACT(sm[:n, 40:44], sm[:n, 40:44], AF.Exp), r=["smg"], w=["smg"])
                dve(TT(sm[:n, 44:48], sm[:n, 20:24], sm[:n, 36:40], ALU.mult), r=["smg"], w=["smg"])
                if sample:
                    gmf = g2[:n].rearrange("p h t -> p (h t)")[:, 0:NS * H]
                    gm = gmf.rearrange("p (s h) -> p s h", h=H)
                    dve(TT(gm, bc_mid(sm[:n, 16:20], n, NS), bc_last(mtm[:n, :], n, H), ALU.mult), r=["smg", "mtm"], w=["g2"])
                    pe(MM(ps[6][:, 16:16 + NS * H], ones_f[:n, :], gmf), r=["g2", "ones_f"], w=[PK(6)])
                    act(ACT(CDt[:, :], ps[6][:, 16:16 + NS * H], AF.Exp), r=[PK(6)], w=[K("CDt")])
                else:
                    pe(MM(ps[6][:, 16:20], ones_f[:n, :], sm[:n, 16:20]), r=["smg", "ones_f"], w=[PK(6)])
                    act(ACT(CDt[:, 0:4], ps[6][:, 16:20], AF.Exp), r=[PK(6)], w=[K("CDt")])
                yield
                dve(TT(g1[:n, :, :n], bc_mid(ones_f[:n, :n], n, H), bc_last(sm[:n, 28:32], n, n), ALU.mult), r=["ones_f", "smg"], w=["g1"])
                dve(TT(g2[:n, :, :n], bc_mid(ident_f[:n, :n], n, H), bc_last(sm[:n, 28:32], n, n), ALU.mult), r=["ident_f", "smg"], w=["g2"])
                for h in range(H):
                    sl = slice(h * 128, h * 128 + n)
                    pe(MM(ps[4][:n, sl], ident_f[:n, :n], g1[:n, h, :n], start=True, stop=False), r=["ident_f", "g1"], w=[PK(4)])
                    pe(MM(ps[4][:n, sl], negones[:n, :n], g2[:n, h, :n], start=False, stop=True), r=["negones", "g2"], w=[PK(4)])
                    pe(MM(ps[6][:n, sl], ones_f[:n, :n], g2[:n, h, :n], start=True, stop=False), r=["ones_f", "g2"], w=[PK(6)])
                    pe(MM(ps[6][:n, sl], negid[:n, :n], g1[:n, h, :n], start=False, stop=True), r=["negid", "g1"], w=[PK(6)])
                dve(STT(g1[:n, :, :n], p4v[:n, :, :n], 0.0, bc_mid(M["nma"][:n, :n], n, H), ALU.min, ALU.add), r=[PK(4), "nma_" + tname], w=["g1"])
                dve(STT(g2[:n, :, :n], p6v[:n, :, :n], 0.0, bc_mid(M["nmq"][:n, :n], n, H), ALU.min, ALU.add), r=[PK(6), "nmq_" + tname], w=["g2"])
                for h in range(H):
                    act(ACT(g1[:n, h, :n], g1[:n, h, :n], AF.Exp, bias=sm[:n, 24 + h:25 + h]), r=["g1", "smg"], w=["g1"])
                act(ACT(g2[:n, :, :n], g2[:n, :, :n], AF.Exp), r=["g2"], w=["g2"])
                yield
                for h in range(H):
                    sl = slice(h * 128, h * 128 + n)
                    pe(MM(ps[4][:n, sl], kTg[:, h, :n], kTg[:, h, :n]), r=["kTg"], w=[PK(4)])
                    pe(MM(ps[6][:n, sl], kTg[:, h, :n], qTg[:, h, :n]), r=["kTg", "qTg"], w=[PK(6)])
                dve(TT(Am[0][:n, :, :n], p4v[:n, :, :n], g1[:n, :, :n], ALU.mult), r=[PK(4), "g1"], w=["Am0"])
                dve(TT(qkT[:n, :, :n], p6v[:n, :, :n], g2[:n, :, :n], ALU.mult), r=[PK(6), "g2"], w=[K("qkT")])
                for h in range(H):
                    pe(TR(ps[4][:n, h * 128:h * 128 + n], Am[0][:n, h, :n], ident_f[:n, :n]), r=["Am0", "ident_f"], w=[PK(4)])
                act(ACT(Bm[0][:n, :, :n], p4v[:n, :, :n], AF.Copy), r=[PK(4)], w=["Bm0"])
                yield
                dve(TT(g1[:n, :, :n], bc_mid(M["tri"][:n, :n], n, H), bc_last(sm[:n, 16:20], n, n), ALU.mult), r=["tri_" + tname, "smg"], w=["g1"])
                if n == 128:
                    pe(MM(ps[6][:, :], ones_f[:n, :], g1[:n, :, :].rearrange("p h t -> p (h t)")), r=["ones_f", "g1"], w=[PK(6)])
                else:
                    for h in range(H):
                        pe(MM(ps[6][:, h * 128:h * 128 + n], ones_f[:n, :], g1[:n, h, :n]), r=["ones_f", "g1"], w=[PK(6)])
                act(ACT(g2[:, :, :n], p6v[:, :, :n], AF.Exp), r=[PK(6)], w=["g2"])
                dve(STT(qdTg[:, :, :n], qkn[:, 0:4, :n], QSCALE, g2[:, :, :n], ALU.mult, ALU.mult), r=[KC, "g2"], w=[K("qdTg")])
                for h in range(H):
                    pe(TR(ps[4][:n, h * 128:(h + 1) * 128], qkn[:, 4 + h, :n], ident_f[:, :]), r=[KC, "ident_f"], w=[PK(4)])
                    pe(TR(ps[6][:n, h * 128:(h + 1) * 128], cv[:, 8 + h, :n], ident_f[:, :]), r=[KC, "ident_f"], w=[PK(6)])
                p4d = ps[4][:].rearrange("p (h d) -> p h d", h=H)
                p6d = ps[6][:].rearrange("p (h d) -> p h d", h=H)
                dve(TT(vb[:n], p6d[:n], bc_last(sm[:n, 20:24], n, DH), ALU.mult), r=[PK(6), "smg"], w=[K("vb")])
                dve(TT(kb[:n], p4d[:n], bc_last(sm[:n, 44:48], n, DH), ALU.mult), r=[PK(4), "smg"], w=["kb"])
                dve(TT(kdec[:n], p4d[:n], bc_last(sm[:n, 40:44], n, DH), ALU.mult), r=[PK(4), "smg"], w=[K("kdec")])
                yield
                dve(TT(Pm[:n, :, :n], bc_mid(ident_f[:n, :n], n, H), Bm[0][:n, :, :n], ALU.subtract), r=["ident_f", "Bm0"], w=[K("Pm")])
                cur = 0
                for lv in range(1, levels + 1):
                    nxt = 1 - cur
                    for h in range(H):
                        sl = slice(h * 128, h * 128 + n)
                        pe(MM(ps[4][:n, sl], Bm[cur][:n, h, :n], Am[cur][:n, h, :n]), r=["Bm%d" % cur, "Am%d" % cur], w=[PK(4)])
                        if lv < levels:
                            pe(MM(ps[6][:n, sl], Am[cur][:n, h, :n], Bm[cur][:n, h, :n]), r=["Bm%d" % cur, "Am%d" % cur], w=[PK(6)])
                    act(ACT(Am[nxt][:n, :, :n], p4v[:n, :, :n], AF.Copy), r=[PK(4)], w=["Am%d" % nxt])
                    if lv < levels:
                        dve(CP(Bm[nxt][:n, :, :n], p6v[:n, :, :n]), r=[PK(6)], w=["Bm%d" % nxt])
                    for h in range(H):
                        sl = slice(h * 128, h * 128 + n)
                        pe(MM(ps[4][:n, sl], Am[nxt][:n, h, :n], Pm[:n, h, :n]), r=["Am%d" % nxt, K("Pm")], w=[PK(4)])
                    dve(TT(Pm[:n, :, :n], Pm[:n, :, :n], p4v[:n, :, :n], ALU.add), r=[K("Pm"), PK(4)], w=[K("Pm")])
                    cur = nxt
                    yield
                for h in range(H):
                    pe(MM(ps[6][:, h * 128:h * 128 + n], kb[:n, h, :], Pm[:n, h, :n]), r=["kb", K("Pm")], w=[PK(6)])
                act(ACT(nwkT[:, :, :n], p6v[:, :, :n], AF.Copy, scale=-1.0), r=[PK(6)], w=[K("nwkT")])
                yield

            def ret_norm(n, bank, p):
                hb = hset[p]
                K = lambda nm: "%s%d" % (nm, p)
                pd = ps[bank][:].rearrange("p (h d) -> p h d", h=H)
                cen3 = cen[:n, :].rearrange("p (h d) -> p h d", h=H)
                sqh3 = sqh[:n, :].rearrange("p (h d) -> p h d", h=H)
                dve(RED(smy[:n, 0:4], pd[:n]), r=[PK(bank)], w=["smy"])
                dve(TS(smy[:n, 0:4], smy[:n, 0:4], -1.0 / DH, 0.0, ALU.mult, ALU.add), r=["smy"], w=["smy"])
                dve(TT(cen3, pd[:n], bc_last(smy[:n, 0:4], n, DH), ALU.add), r=[PK(bank), "smy"], w=["cen"])
                act(ACT(sqh[:n, :], cen[:n, :], AF.Square), r=["cen"], w=["sqh"])
                dve(RED(smy[:n, 4:8], sqh3), r=["sqh"], w=["smy"])
                rstd_chain(smy[:n, 4:8], smy[:n, 4:8], 1.0 / DH, ["smy"], "smy")
                dve(TT(cen3, cen3, bc_last(smy[:n, 4:8], n, DH), ALU.mult), r=["cen", "smy"], w=["cen"])
                dve(TT(cen[:n, :], cen[:n, :], nrB[:n, :], ALU.mult), r=["cen", "nrB"], w=["cen"])
                dve(TT(mixed[:n, 0:512], cen[:n, :], hb["sgr"][:n, :], ALU.mult), r=["cen", K("sgr")], w=["mixed"])

            def gdn_norm(n, bank, p):
                hb = hset[p]
                K = lambda nm: "%s%d" % (nm, p)
                pd = ps[bank][:].rearrange("p (h d) -> p h d", h=H)
                cen3 = cen[:n, :].rearrange("p (h d) -> p h d", h=H)
                sqh3 = sqh[:n, :].rearrange("p (h d) -> p h d", h=H)
                act(ACT(sqh[:n, :], ps[bank][:n, :], AF.Square), r=[PK(bank)], w=["sqh"])
                dve(RED(smy[:n, 8:12], sqh3), r=["sqh"], w=["smy"])
                rstd_chain(smy[:n, 8:12], smy[:n, 8:12], 1.0 / DH, ["smy"], "smy")
                dve(TT(cen3, pd[:n], bc_last(smy[:n, 8:12], n, DH), ALU.mult), r=[PK(bank), "smy"], w=["cen"])
                dve(TT(cen3, cen3, bc_mid(ngB[:n, :], n, H), ALU.mult), r=["cen", "ngB"], w=["cen"])
                dve(TT(mixed[:n, 512:1024], cen[:n, :], hb["sgg"][:n, :], ALU.mult), r=["cen", K("sgg")], w=["mixed"])

            def out_proj(r0, n, p, btr, b0, b1):
                P.dma("sp", DMA(xres[:n, :], x[r0:r0 + n, :]), "xres_ld", writes=["xres"])
                for k in range(8):
                    pe(TR(psb[btr][:, k * 128:k * 128 + n], mixed[:n, k * 128:(k + 1) * 128], ident_b[:n, :n]), r=["mixed", "ident_b"], w=[PK(btr)])
                act(ACT(mT[:, :, :n], psb[btr][:].rearrange("p (k t) -> p k t", k=8)[:, :, :n], AF.Copy), r=[PK(btr)], w=["mT"])
                for half, bank in enumerate((b0, b1)):
                    for k in range(8):
                        pe(MM(ps[bank][:n, :], mT[:, k, :n], w_out_sb[:, k, half * 512:(half + 1) * 512], start=(k == 0), stop=(k == 7)),
                           r=["mT", "w_out_sb"], w=[PK(bank)])
                    dve(TT(xres[:n, half * 512:(half + 1) * 512], xres[:n, half * 512:(half + 1) * 512], ps[bank][:n, :], ALU.add),
                        r=["xres", PK(bank)], w=["xres"])
                P.dma("sp", DMA(x1s[r0:r0 + n, :], xres[:n, :]), "xres_st", reads=["xres"], writes=["x1s"])

            def stage_y(r0, n, tname, last_prompt, p):
                hb = hset[p]
                K = lambda nm: "%s%d" % (nm, p)
                qdT, scT, vr, vk, kr = hb["qdT"], hb["scT"], hb["vr"], hb["vk"], hb["kr"]
                Pm, vb, nwkT, qdTg, qkT, kdec, CDt = hb["Pm"], hb["vb"], hb["nwkT"], hb["qdTg"], hb["qkT"], hb["kdec"], hb["CDt"]
                for h in range(H):
                    hs = slice(h * 128, (h + 1) * 128)
                    pe(MM(ps[5][:n, hs], scT[:n, h, :n], vr[:n, hs], start=True, stop=False), r=[K("scT"), K("vr")], w=[PK(5)])
                    pe(MM(ps[5][:n, hs], qdT[:, h, :n], sretb[:, h, :], start=False, stop=True), r=[K("qdT"), "sretb"], w=[PK(5)])
                for h in range(H):
                    hs = slice(h * 128, (h + 1) * 128)
                    pe(MM(ps[7][:, hs], kr[:n, h, :], vk[:n, h, :]), r=[K("kr"), K("vk")], w=[PK(7)])
                yield
                for h in range(H):
                    hs = slice(h * 128, (h + 1) * 128)
                    dve(STT(sret[:, h, :], sret[:, h, :], float(GAM[h] ** n), ps[7][:, hs], ALU.mult, ALU.add),
                        r=["sret", PK(7)], w=["sret"])
                act(ACT(sretb[:], sret[:], AF.Copy), r=["sret"], w=["sretb"])
                yield
                for h in range(H):
                    hs = slice(h * 128, (h + 1) * 128)
                    pe(MM(ps[7][:n, hs], Pm[:n, h, :n], vb[:n, h, :], start=True, stop=False), r=[K("Pm"), K("vb")], w=[PK(7)])
                    pe(MM(ps[7][:n, hs], nwkT[:, h, :n], sgdnb[:, h, :], start=False, stop=True), r=[K("nwkT"), "sgdnb"], w=[PK(7)])
                act(ACT(w_sb[:n], ps[7][:n, :].rearrange("p (h d) -> p h d", h=H), AF.Copy), r=[PK(7)], w=["w_sb"])
                yield
                ret_norm(n, 5, p)
                yield
                for h in range(H):
                    hs = slice(h * 128, (h + 1) * 128)
                    pe(MM(ps[5][:n, hs], qdTg[:, h, :n], sgdnb[:, h, :], start=True, stop=False), r=[K("qdTg"), "sgdnb"], w=[PK(5)])
                    pe(MM(ps[5][:n, hs], qkT[:n, h, :n], w_sb[:n, h, :], start=False, stop=True), r=[K("qkT"), "w_sb"], w=[PK(5)])
                for h in range(H):
                    hs = slice(h * 128, (h + 1) * 128)
                    pe(MM(ps[7][:, hs], kdec[:n, h, :], w_sb[:n, h, :]), r=[K("kdec"), "w_sb"], w=[PK(7)])
                yield
                for h in range(H):
                    hs = slice(h * 128, (h + 1) * 128)
                    dve(STT(sgdn[:, h, :], sgdn[:, h, :], CDt[:, h:h + 1], ps[7][:, hs], ALU.mult, ALU.add),
                        r=["sgdn", K("CDt"), PK(7)], w=["sgdn"])
                act(ACT(sgdnb[:], sgdn[:], AF.Copy), r=["sgdn"], w=["sgdnb"])
                if last_prompt:
                    P.dma("sp", DMA(o_sret_p.rearrange("h d e -> d h e"), sret[:]), "sret_st", reads=["sret"])
                    P.dma("sp", DMA(o_sgdn_p.rearrange("h d e -> d h e"), sgdn[:]), "sgdn_st", reads=["sgdn"])
                yield
                gdn_norm(n, 5, p)
                yield
                out_proj(r0, n, p, 7, 5, 7)
                yield

            def run_all(g):
                for _ in g:
                    pass

            def interleave(gens):
                live = [[g, w] for (g, w) in gens if g is not None]
                while live:
                    for item in list(live):
                        for _ in range(item[1]):
                            try:
                                next(item[0])
                            except StopIteration:
                                live.remove(item)
                                break

            ptiles = [(0, NMETA, "meta", False)] + [(NMETA + 128 * i, 128, "prm", i == SEQ // 128 - 1) for i in range(SEQ // 128)]
            NT = len(ptiles)

            xtiles = ptiles + [(LP, NS * DS, "smp", False)]

            def gx1(i):
                if i >= len(xtiles):
                    return None
                r_, n_, t_, l_ = xtiles[i]
                return stage_x1(r_, n_, t_, t_ == "smp", l_, i % 2)

            def gx2(i):
                if i >= len(xtiles):
                    return None
                r_, n_, t_, l_ = xtiles[i]
                return stage_x2r(r_, n_, t_, t_ == "smp", l_, i % 2, i % 2)

            def gx2g(i):
                if i >= len(xtiles):
                    return None
                r_, n_, t_, l_ = xtiles[i]
                return stage_x2g(r_, n_, t_, t_ == "smp", l_, i % 2, i % 2)

            def gy(i):
                r_, n_, t_, l_ = ptiles[i]
                return stage_y(r_, n_, t_, l_, i % 2)

            with contextlib.ExitStack() as st1:
                T1 = lambda name, shape, dt=F32: st1.enter_context(nc.sbuf_tensor("sb_" + name, list(shape), dt))
                hset[0] = alloc_hset(0, T1)
                def rec(g):
                    if g is None:
                        return []
                    P.rec = []
                    run_all(g)
                    r_ = P.rec
                    P.rec = None
                    return r_

                run_all(gx1(0))
                P.merge([rec(gx1(1)), rec(gx2(0)), rec(gx2g(0))])
                for i in range(NT):
                    P.merge([rec(gx1(i + 2)), rec(gx2(i + 1)), rec(gx2g(i + 1)), rec(gy(i))])
                P.barrier()

            with contextlib.ExitStack() as st2:
                T2 = lambda name, shape, dt=F32: st2.enter_context(nc.sbuf_tensor("sb_" + name, list(shape), dt))
                Sb = [T2("Sb0", [128, 2 * H, DH], BF16), T2("Sb1", [128, 2 * H, DH], BF16)]
                Sf = [T2("Sf0", [128, 2 * H, DH]), T2("Sf1", [128, 2 * H, DH])]
                mq = [T2("mq0", [128, H, 64], BF16), T2("mq1", [128, H, 64], BF16)]
                mw = [T2("mw0", [128, H, 64], BF16), T2("mw1", [128, H, 64], BF16)]
                mg = [T2("mg0", [128, H, 64], BF16), T2("mg1", [128, H, 64], BF16)]
                mk = [T2("mk0", [64, H, DH], BF16), T2("mk1", [64, H, DH], BF16)]
                md = [T2("md0", [64, H, DH], BF16), T2("md1", [64, H, DH], BF16)]
                n = NS * DS
                r0 = LP
                assert NT % 2 == 1
                hb = hset[1]
                K = lambda nm: "%s1" % nm
                qdT, scT, vr, vk, kr = hb["qdT"], hb["scT"], hb["vr"], hb["vk"], hb["kr"]
                Pm, vb, nwkT, qdTg, qkT, kdec, CDt = hb["Pm"], hb["vb"], hb["nwkT"], hb["qdTg"], hb["qkT"], hb["kdec"], hb["CDt"]
                for s in range(NS):
                    b = s % 2
                    P.dma("pool", DMA(Sb[b][:, :, :], sall_in[s].rearrange("h d e -> d h e")), "Sb%d" % b, writes=["Sb%d" % b])
                    dve(TT(mq[b][:, :, :n], qdT[:, :, :n], bc_mid(mfm[:, s, :], 128, H), ALU.mult), r=[K("qdT"), "mfm"], w=["mq%d" % b])
                    dve(TT(mw[b][:, :, :n], nwkT[:, :, :n], bc_mid(mfm[:, s, :], 128, H), ALU.mult), r=[K("nwkT"), "mfm"], w=["mw%d" % b])
                    dve(TT(mg[b][:, :, :n], qdTg[:, :, :n], bc_mid(mfm[:, s, :], 128, H), ALU.mult), r=[K("qdTg"), "mfm"], w=["mg%d" % b])
                    for h in range(H):
                        hs = slice(h * 128, (h + 1) * 128)
                        first = (s == 0)
                        if first:
                            pe(MM(ps[5][:n, hs], scT[:n, h, :n], vr[:n, hs], start=(h == 0), stop=False, skip=True), r=[K("scT"), K("vr")], w=[PK(5)])
                            pe(MM(ps[1][:n, hs], Pm[:n, h, :n], vb[:n, h, :], start=(h == 0), stop=False, skip=True), r=[K("Pm"), K("vb")], w=[PK(1)])
                        pe(MM(ps[5][:n, hs], mq[b][:, h, :n], Sb[b][:, h, :], start=False, stop=(s == NS - 1), skip=True),
                           r=["mq%d" % b, "Sb%d" % b], w=[PK(5)])
                        pe(MM(ps[1][:n, hs], mw[b][:, h, :n], Sb[b][:, H + h, :], start=False, stop=(s == NS - 1), skip=True),
                           r=["mw%d" % b, "Sb%d" % b], w=[PK(1)])
                        pe(MM(ps[2][:n, hs], mg[b][:, h, :n], Sb[b][:, H + h, :], start=(first and h == 0), stop=False, skip=True),
                           r=["mg%d" % b, "Sb%d" % b], w=[PK(2)])
                act(ACT(w_sb[:n], ps[1][:n, :].rearrange("p (h d) -> p h d", h=H), AF.Copy), r=[PK(1)], w=["w_sb"])
                for h in range(H):
                    hs = slice(h * 128, (h + 1) * 128)
                    pe(MM(ps[2][:n, hs], qkT[:n, h, :n], w_sb[:n, h, :], start=False, stop=True, skip=True), r=[K("qkT"), "w_sb"], w=[PK(2)])
                def pass2(par):
                    for s in range(par, NS, 2):
                        b = s % 2
                        kr_ = kg_ = "Sf%d" % b
                        P.dma("sp", DMA(Sf[b][:, :, :], sall_in[s].rearrange("h d e -> d h e")), "Sfl%d" % b, writes=[kr_])
                        dve(TS(mk[b][:n], kr[:n], mtm[:n, s:s + 1], 0.0, ALU.mult, ALU.add), r=[K("kr"), "mtm"], w=["mk%d" % b])
                        dve(TS(md[b][:n], kdec[:n], mtm[:n, s:s + 1], 0.0, ALU.mult, ALU.add), r=[K("kdec"), "mtm"], w=["md%d" % b])
                        bank = 3 + (s % 2)
                        bank2 = 6 if (s % 2) else 7
                        for h in range(H):
                            hs = slice(h * 128, (h + 1) * 128)
                            pe(MM(ps[bank][:, hs], mk[b][:n, h, :], vk[:n, h, :]), r=["mk%d" % b, K("vk")], w=[PK(bank)])
                            pe(MM(ps[bank2][:, hs], md[b][:n, h, :], w_sb[:n, h, :]), r=["md%d" % b, "w_sb"], w=[PK(bank2)])
                        for h in range(H):
                            hs = slice(h * 128, (h + 1) * 128)
                            dve(STT(Sf[b][:, h, :], Sf[b][:, h, :], float(GAM[h] ** DS), ps[bank][:, hs], ALU.mult, ALU.add),
                                r=[kr_, PK(bank)], w=[kr_])
                            dve(STT(Sf[b][:, H + h, :], Sf[b][:, H + h, :], CDt[:, s * H + h:s * H + h + 1], ps[bank2][:, hs], ALU.mult, ALU.add),
                                r=[kg_, K("CDt"), PK(bank2)], w=[kg_])
                        P.dma("act", DMA(o_sall_s[s].rearrange("h d e -> d h e"), Sf[b][:, :, :]), "Sfs%d" % b, reads=[kr_])

                def rec_fn(f, *a_):
                    P.rec = []
                    f(*a_)
                    r_ = P.rec
                    P.rec = None
                    return r_

                P.merge([rec_fn(pass2, 0), rec_fn(pass2, 1)])
                ret_norm(n, 5, 1)
                gdn_norm(n, 2, 1)
                out_proj(r0, n, 1, 0, 3, 4)
                P.barrier()

        with contextlib.ExitStack() as st:
            def T(name, shape, dt=F32):
                return st.enter_context(nc.sbuf_tensor("sb_" + name, list(shape), dt))

            w_up_sb = T("w_up_sb", [128, 8, 2 * DFF], BF16)
            w_dn_sb = T("w_dn_sb", [128, 22, D], BF16)
            identB_f = T("identB_f", [128, 128])
            identB_b = T("identB_b", [128, 128], BF16)
            nfc = T("nfc", [128, 8])
            cwf = T("cwf", [128, 44, 3])
            nfinB = T("nfinB", [128, D])
            carryf = T("carryf", [128, 44, 2])
            xf = [T("xf0", [128, D]), T("xf1", [128, D])]
            xr = [T("xr0", [128, D]), T("xr1", [128, D])]
            h2n2 = [T("h2n0", [128, D], BF16), T("h2n1", [128, D], BF16)]
            h2T2 = [T("h2T0", [128, 8, 256], BF16), T("h2T1", [128, 8, 256], BF16)]
            actT2 = [T("actT0", [128, 22, 256], BF16), T("actT1", [128, 22, 256], BF16)]
            ubg2 = [T("ubg0", [128, 258]), T("ubg1", [128, 258])]
            ubv2 = [T("ubv0", [128, 258]), T("ubv1", [128, 258])]
            cg2 = [T("cg0", [128, 256]), T("cg1", [128, 256])]
            cvl2 = [T("cvl0", [128, 256]), T("cvl1", [128, 256])]
            ptmp2 = [[T("pta0", [128, 256]), T("ptb0", [128, 256])], [T("pta1", [128, 256]), T("ptb1", [128, 256])]]
            yt = T("yt", [128, D])
            smb = T("smb", [128, 8])
            sct = T("sct", [32, 256])
            stg = T("stg", [64, 512])

            P.dma("pool", DMA(identB_b[:], cd["ident"]), "idBb", writes=["identB_b"])
            for blk in (0, 5, 6, 1, 7, 2, 8, 3, 9, 4, 10):
                P.dma("pool", DMA(w_up_sb[:, :, blk * 512:(blk + 1) * 512],
                                  w_up[:, blk * 512:(blk + 1) * 512].rearrange("(k p) f -> p k f", p=128)),
                      "w_up%d" % blk, writes=[("w_up_sb", blk)])
            for hf, j0 in enumerate((0, 11)):
                P.dma("pool", DMA(w_dn_sb[:, j0:j0 + 11, :], w_down[j0 * 128:(j0 + 11) * 128, :].rearrange("(j p) m -> p j m", p=128)),
                      "w_dn%d" % hf, writes=[("w_dn_sb", hf)])
            for (o_, i_, key) in [(identB_f[:], cd["ident"], "identB_f"), (nfc[:], nf_col, "nfc"),
                                  (cwf[:].rearrange("p c i -> p (c i)"), cwf_in, "cwf"),
                                  (nfinB[:], nfin_in.partition_broadcast(128), "nfinB")]:
                P.dma("sp", DMA(o_, i_), "setupB", writes=[key])
            P.fix_group("setupB")
            dve(MSET(carryf[:], 0.0), w=["carryf"])

            def tiles_of(rg0, ncol):
                return [(rg0 + o, min(128, ncol - o), o) for o in range(0, ncol, 128)]

            glist = [(256 * gi, 256, False, False) for gi in range(LP // 256)]
            glist.append((256 * (LP // 256), LP - 256 * (LP // 256), False, True))
            glist.append((LP, NS * DS, True, False))

            def front_a(g):
                rg0, ncol, sample, last_prompt = glist[g]
                for ti, (rt0, nt, off) in enumerate(tiles_of(rg0, ncol)):
                    xk = "xf%d" % ti
                    P.dma("sp", DMA(xf[ti][:nt, :], x1s[rt0:rt0 + nt, :]), xk + "_ld", reads=["x1s"], writes=[xk])
                    dve(MSET(smb[:nt, 0:1], 0.0), w=["smbf"])
                    act(ACT(h2n2[ti][:nt, :], xf[ti][:nt, :], AF.Square, accum_out=smb[:nt, 0:1]), r=[xk], w=["h2n%d" % ti, "smbf"])
                    dve(TS(smb[:nt, 0:1], smb[:nt, 0:1], 1.0 / D, EPS, ALU.mult, ALU.add), r=["smbf"], w=["smbf"])
                    act(ACT(smb[:nt, 0:1], smb[:nt, 0:1], AF.Ln), r=["smbf"], w=["smbf"])
                    act(ACT(smb[:nt, 0:1], smb[:nt, 0:1], AF.Exp, scale=-0.5), r=["smbf"], w=["smbf"])
                    act(ACT(h2n2[ti][:nt, :], xf[ti][:nt, :], AF.Copy, scale=smb[:nt, 0:1]), r=[xk, "smbf"], w=["h2n%d" % ti])

            def front_b(g):
                rg0, ncol, sample, last_prompt = glist[g]
                h2T = h2T2[g % 2]
                for ti, (rt0, nt, off) in enumerate(tiles_of(rg0, ncol)):
                    for k in range(8):
                        pe(TR(psb[0][:, k * 128:k * 128 + nt], h2n2[ti][:nt, k * 128:(k + 1) * 128], identB_b[:nt, :nt]),
                           r=["h2n%d" % ti, "identB_b"], w=[PK(0)])
                    dve(TT(h2T[:, :, off:off + nt], psb[0][:].rearrange("p (k t) -> p k t", k=8)[:, :, :nt],
                           nfc[:, :].unsqueeze(2).to_broadcast([128, 8, nt]), ALU.mult), r=[PK(0), "nfc"], w=["h2T%d" % (g % 2)])

            def down_work(g):
                rg0, ncol, sample, last_prompt = glist[g]
                actT = actT2[g % 2]
                ak = "actT%d" % (g % 2)
                work = []
                for ti, (rt0, nt, off) in enumerate(tiles_of(rg0, ncol)):
                    xk = "xr%d" % ti

                    def ld(ti=ti, rt0=rt0, nt=nt, xk=xk):
                        P.dma("sp", DMA(xr[ti][:nt, :], x1s[rt0:rt0 + nt, :]), xk + "_ld", reads=["x1s"], writes=[xk])
                    work.append(ld)
                    for half in range(2):
                        bank = 5 + half
                        for j0 in range(0, 22, 4):
                            def mm(ti=ti, nt=nt, off=off, half=half, bank=bank, j0=j0):
                                for j in range(j0, min(j0 + 4, 22)):
                                    pe(MM(ps[bank][:nt, :], actT[:, j, off:off + nt], w_dn_sb[:, j, half * 512:(half + 1) * 512],
                                          start=(j == 0), stop=(j == 21)), r=[(ak, j), ("w_dn_sb", j // 11)], w=[PK(bank)])
                            work.append(mm)

                        def add(ti=ti, nt=nt, half=half, bank=bank, xk=xk):
                            dve(TT(xr[ti][:nt, half * 512:(half + 1) * 512], xr[ti][:nt, half * 512:(half + 1) * 512], ps[bank][:nt, :], ALU.add),
                                r=[xk, PK(bank)], w=[xk])
                        work.append(add)

                    def fin(ti=ti, rt0=rt0, nt=nt, xk=xk, sample=sample):
                        dve(MSET(smb[:nt, 1:2], 0.0), w=["smbd"])
                        act(ACT(yt[:nt, :], xr[ti][:nt, :], AF.Square, accum_out=smb[:nt, 1:2]), r=[xk], w=["yt", "smbd"])
                        dve(TS(smb[:nt, 1:2], smb[:nt, 1:2], 1.0 / D, EPS, ALU.mult, ALU.add), r=["smbd"], w=["smbd"])
                        act(ACT(smb[:nt, 1:2], smb[:nt, 1:2], AF.Ln), r=["smbd"], w=["smbd"])
                        act(ACT(smb[:nt, 1:2], smb[:nt, 1:2], AF.Exp, scale=-0.5), r=["smbd"], w=["smbd"])
                        act(ACT(yt[:nt, :], xr[ti][:nt, :], AF.Copy, scale=smb[:nt, 1:2]), r=[xk, "smbd"], w=["yt"])
                        dve(TT(yt[:nt, :], yt[:nt, :], nfinB[:nt, :], ALU.mult), r=["yt", "nfinB"], w=["yt"])
                        if sample:
                            P.dma("sp", DMA(y[SEQ:SEQ + nt, :], yt[:nt, :]), "yt_st", reads=["yt"])
                        else:
                            lo = max(rt0, NMETA)
                            hi = rt0 + nt
                            if hi > lo:
                                P.dma("sp", DMA(y[lo - NMETA:hi - NMETA, :], yt[lo - rt0:hi - rt0, :]), "yt_st", reads=["yt"])
                    work.append(fin)
                return work

            def up_group(g, what):
                rg0, ncol, sample, last_prompt = glist[g]
                h2T = h2T2[g % 2]
                h2k = "h2T%d" % (g % 2)
                actT = actT2[g % 2]
                ak = "actT%d" % (g % 2)
                if (sample or last_prompt) and what in ("tm", "all"):
                    for f in range(11):
                        for k in range(8):
                            pe(MM(ps[1][:ncol, :], h2T[:, k, :ncol], w_up_sb[:, k, f * 512:(f + 1) * 512], start=(k == 0), stop=(k == 7)),
                               r=[h2k, ("w_up_sb", f)], w=[PK(1)])
                        act(ACT(stg[:ncol, :], ps[1][:ncol, :], AF.Copy), r=[PK(1)], w=["stg"])
                        if sample:
                            for i in range(2):
                                P.dma("sp", DMA(o_cf_s[:, i, f * 512:(f + 1) * 512], stg[2 + i:64:4, :]), "stg_st", reads=["stg"])
                        else:
                            P.dma("sp", DMA(o_cf_p[:, f * 512:(f + 1) * 512], stg[ncol - 2:ncol, :]), "stg_st", reads=["stg"])

                def vw(ap2):
                    return ap2 if not sample else ap2.rearrange("p (s j) -> p s j", j=4)

                def stage1(j):
                    pj = j % 2
                    for (cidx, ub, bank, cdst, ubk, cdk, isg) in ((j, ubg2[pj], (2, 4)[pj], cg2[pj], "ubg%d" % pj, "cg%d" % pj, True),
                                                                  (22 + j, ubv2[pj], (3, 7)[pj], cvl2[pj], "ubv%d" % pj, "cvl%d" % pj, False)):
                        for k in range(8):
                            pe(MM(ps[bank][:, :ncol], w_up_sb[:, k, cidx * 128:(cidx + 1) * 128], h2T[:, k, :ncol], start=(k == 0), stop=(k == 7)),
                               r=[h2k, ("w_up_sb", cidx // 4)], w=[PK(bank)])
                        if not sample:
                            pool(CP(ub[:, 0:2], carryf[:, cidx, :]), r=[("carryf", cidx)], w=[(ubk, "c")])
                            act(ACT(ub[:, 2:2 + ncol], ps[bank][:, :ncol], AF.Copy), r=[PK(bank)], w=[(ubk, "m")])
                            pool(CP(carryf[:, cidx, :], ub[:, ncol:ncol + 2]), r=[(ubk, "m")], w=[("carryf", cidx)])
                            tp = [ub[:, i:i + ncol] for i in range(3)]
                        else:
                            ubv3 = ub[:, 0:96].rearrange("p (s j) -> p s j", j=6)
                            slot = pj * 2 + (0 if isg else 1)
                            sctv = h2n2[0].bitcast(F32)
                            sl_ = sctv[:32, slot * 128:(slot + 1) * 128]
                            skey = ("h2n0", slot)
                            P.dma("sp", DMA(sl_, scf_in[:, cidx * 128:(cidx + 1) * 128]), "sct_ld%d" % slot, writes=[skey])
                            pe(TR(ps[0][:, slot * 32:(slot + 1) * 32], sl_, identB_f[:32, :32]), r=[skey, "identB_f"], w=[PK(0)])
                            act(ACT(ubv3[:, :, 0:2], ps[0][:, slot * 32:(slot + 1) * 32].rearrange("p (s i) -> p s i", i=2), AF.Copy), r=[PK(0)], w=[ubk])
                            act(ACT(ubv3[:, :, 2:6], ps[bank][:, :ncol].rearrange("p (s j) -> p s j", j=4), AF.Copy), r=[PK(bank)], w=[ubk])
                            tp = [ubv3[:, :, i:i + 4] for i in range(3)]
                        o_ = vw(cdst[:, :ncol])
                        if isg:
                            act(ACT(o_, tp[0], AF.Copy, scale=cwf[:, cidx, 0:1]), r=[ubk, "cwf"], w=[cdk])
                            dve(STT(o_, tp[1], cwf[:, cidx, 1:2], o_, ALU.mult, ALU.add), r=[ubk, "cwf", cdk], w=[cdk])
                            dve(STT(o_, tp[2], cwf[:, cidx, 2:3], o_, ALU.mult, ALU.add), r=[ubk, "cwf", cdk], w=[cdk])
                        else:
                            pa, pb = vw(ptmp2[pj][0][:, :ncol]), vw(ptmp2[pj][1][:, :ncol])
                            act(ACT(pa, tp[0], AF.Copy, scale=cwf[:, cidx, 0:1]), r=[ubk, "cwf"], w=["pta%d" % pj])
                            pool(TS(pb, tp[1], cwf[:, cidx, 1:2], 0.0, ALU.mult, ALU.add), r=[ubk, "cwf"], w=["ptb%d" % pj])
                            dve(STT(o_, tp[2], cwf[:, cidx, 2:3], pa, ALU.mult, ALU.add), r=[ubk, "cwf", "pta%d" % pj], w=[cdk])
                            dve(TT(o_, o_, pb, ALU.add), r=[cdk, "ptb%d" % pj], w=[cdk])

                def stage2(j):
                    pj = j % 2
                    cg, cvl = cg2[pj], cvl2[pj]
                    act(ACT(cg[:, :ncol], cg[:, :ncol], AF.Silu), r=["cg%d" % pj], w=["cg%d" % pj])
                    dve(TT(actT[:, j, :ncol], cg[:, :ncol], cvl[:, :ncol], ALU.mult), r=["cg%d" % pj, "cvl%d" % pj], w=[(ak, j)])

                if what == "all":
                    for j in range(23):
                        if j < 22:
                            stage1(j)
                        if j >= 1:
                            stage2(j - 1)
                elif what in ("even", "odd"):
                    for j in range(0 if what == "even" else 1, 22, 2):
                        stage1(j)
                        stage2(j)

            def rec_call(f):
                P.rec = []
                f()
                r_ = P.rec
                P.rec = None
                return r_

            front_a(0)
            front_b(0)
            G_ = len(glist)
            for g in range(G_):
                lists = []
                if glist[g][2] or glist[g][3]:
                    lists.append(rec_call(lambda: up_group(g, "tm")))
                lists.append(rec_call(lambda: up_group(g, "even")))
                lists.append(rec_call(lambda: up_group(g, "odd")))
                if g > 0:
                    lists.append(rec_call(lambda: [w_() for w_ in down_work(g - 1)]))
                if g + 1 < G_:
                    lists.append(rec_call(lambda: (front_a(g + 1), front_b(g + 1))))
                P.merge(lists)
            for w_ in down_work(G_ - 1):
                w_()
            P.finalize()

        counts = P.emit(nc)
    return nc, consts, counts


_CACHE = {}


def kernel(x_prompt, x_sample, state_ret, state_gdn, state_conv_qkv, state_ffn_conv,
           meta_tokens, norm_mix, w_in, conv_gdn, gdn_a_log, gdn_dt_bias, norm_ret, norm_gdn, w_out,
           norm_ffn, w_up, conv_ffn, w_down, norm_final):
    f = lambda a: np.ascontiguousarray(np.asarray(a, dtype=np.float32))
    x_prompt, x_sample, state_ret, state_gdn = f(x_prompt), f(x_sample), f(state_ret), f(state_gdn)
    state_conv_qkv, state_ffn_conv, meta_tokens = f(state_conv_qkv), f(state_ffn_conv), f(meta_tokens)
    if "nc" not in _CACHE:
        _CACHE["nc"] = build_program()
    nc, consts, counts = _CACHE["nc"]
    shared = {
        "w_in": f(w_in)[0], "w_out": f(w_out)[0], "w_up": f(w_up)[0], "w_down": f(w_down)[0],
        "nm_col": f(f(norm_mix)[0].reshape(8, 128).T), "nf_col": f(f(norm_ffn)[0].reshape(8, 128).T),
        "cwg": f(f(conv_gdn)[0].reshape(4, 12, 128).transpose(2, 1, 0).reshape(128, 48)),
        "cwf": f(f(conv_ffn)[0].reshape(3, 44, 128).transpose(2, 1, 0).reshape(128, 132)),
        "alog": f(gdn_a_log)[0], "dtb": f(gdn_dt_bias)[0], "nret": f(norm_ret)[0], "ngdn": f(norm_gdn)[0],
        "nfin": f(norm_final),
    }
    for k, v in consts.items():
        shared["c_" + k] = v
    in_maps = []
    for c in range(8):
        m = dict(shared)
        m["x"] = np.concatenate([meta_tokens, x_prompt[c], x_sample[c * NS:(c + 1) * NS].reshape(NS * DS, D)], axis=0)
        m["sall"] = np.concatenate([state_ret[0, c * NS:(c + 1) * NS], state_gdn[0, c * NS:(c + 1) * NS]], axis=1)
        m["scq"] = state_conv_qkv[0, c * NS:(c + 1) * NS].reshape(NS * 3, 1536)
        m["scf"] = state_ffn_conv[0, c * NS:(c + 1) * NS].reshape(NS * 2, 2 * DFF)
        in_maps.append(m)
    res = run_bass_kernel_spmd(nc, in_maps, core_ids=list(range(8)))
    R = res.results
    y_prompt = np.stack([R[c]["y"][:SEQ] for c in range(8)])
    y_sample = np.concatenate([R[c]["y"][SEQ:].reshape(NS, DS, D) for c in range(8)])
    p_ret = np.stack([R[c]["o_sret_p"] for c in range(8)])[None]
    p_gdn = np.stack([R[c]["o_sgdn_p"] for c in range(8)])[None]
    p_cq = np.stack([R[c]["o_cq_p"] for c in range(8)])[None]
    p_cf = np.stack([R[c]["o_cf_p"] for c in range(8)])[None]
    s_ret = np.concatenate([R[c]["o_sall_s"][:, :H] for c in range(8)])[None]
    s_gdn = np.concatenate([R[c]["o_sall_s"][:, H:] for c in range(8)])[None]
    s_cq = np.concatenate([R[c]["o_cq_s"] for c in range(8)])[None]
    s_cf = np.concatenate([R[c]["o_cf_s"] for c in range(8)])[None]
    outs = (y_prompt, y_sample, p_ret, p_gdn, p_cq, p_cf, s_ret, s_gdn, s_cq, s_cf)
    return tuple(np.ascontiguousarray(o, dtype=np.float32) for o in outs)
```

```python
import contextlib
import numpy as np
import concourse.bass as bass
import concourse.mybir as mybir
from concourse.bass_utils import run_bass_kernel_spmd

F32 = mybir.dt.float32
BF16 = mybir.dt.bfloat16
AF = mybir.ActivationFunctionType
ALU = mybir.AluOpType
AX = mybir.AxisListType

D = 1024
NMETA = 16
SEQ = 2048
LP = NMETA + SEQ
NS = 16
DS = 4
NROW = LP + NS * DS
H = 4
DH = 128
INW = 4104
DFF = 2816
EPS = 1e-6
PAST = 16384
GAM = [1.0 - 2.0 ** (-5 - h) for h in range(H)]
QSCALE = DH ** -0.5
NEG = -1.0e5


class _Op:
    __slots__ = ("fn", "waits", "signal", "dma", "idx")

    def __init__(self, fn, waits, dma):
        self.fn = fn
        self.waits = waits
        self.signal = False
        self.dma = dma
        self.idx = None


class Plan:
    ENGS = ("pe", "act", "dve", "pool", "sp")

    def __init__(self, same_engine_sync=True):
        self.streams = {e: [] for e in self.ENGS}
        self.last_write = {}
        self.readers = {}
        self.by_root = {}
        self.dma_count = {}
        self.waited = {e: {} for e in self.ENGS}
        self.same_engine_sync = same_engine_sync
        self.barrier_toks = []
        self.rec = None
        self.t_eng = {}
        self.t_tok = {}

    @staticmethod
    def _k(r):
        return r if isinstance(r, tuple) else (r,)

    def _conf(self, key):
        for k in self.by_root.get(key[0], ()):
            n = min(len(k), len(key))
            if k[:n] == key[:n]:
                yield k

    def _deps(self, reads, writes, eng=None):
        toks = list(self.barrier_toks)
        for r in reads:
            key = self._k(r)
            psum = isinstance(key[0], str) and key[0].startswith("ps") and key[0][2:].isdigit()
            for k in self._conf(key):
                t = self.last_write.get(k)
                if t is not None:
                    toks.append(t)
                if psum:
                    toks.extend(t2 for t2 in self.readers.get(k, ()) if not (t2[0] == "eng" and t2[1] == eng))
        for w in writes:
            key = self._k(w)
            for k in self._conf(key):
                t = self.last_write.get(k)
                if t is not None:
                    toks.append(t)
                toks.extend(self.readers.get(k, ()))
        return toks

    def _commit(self, tok, reads, writes):
        for w in writes:
            key = self._k(w)
            for k in list(self._conf(key)):
                if len(k) >= len(key):
                    self.readers.pop(k, None)
                    if k != key:
                        self.last_write.pop(k, None)
            self.by_root.setdefault(key[0], set()).add(key)
            self.last_write[key] = tok
        for r in reads:
            key = self._k(r)
            self.by_root.setdefault(key[0], set()).add(key)
            self.readers.setdefault(key, []).append(tok)

    def _filter(self, eng, toks):
        wd = self.waited[eng]
        best = {}
        for t in toks:
            kind, who, val = t
            if kind == "eng" and who == eng:
                if eng in ("pe", "sp") or not self.same_engine_sync:
                    continue
            key = (kind, who)
            if wd.get(key, -1) >= val:
                continue
            if key not in best or best[key][2] < val:
                best[key] = t
        for key, t in best.items():
            wd[key] = t[2]
        return list(best.values())

    def op(self, eng, fn, reads=(), writes=()):
        if self.rec is not None:
            self.rec.append(("op", eng, fn, None, tuple(reads), tuple(writes)))
            return None
        toks = self._filter(eng, self._deps(reads, writes, eng))
        o = _Op(fn, toks, None)
        o.idx = len(self.streams[eng])
        self.streams[eng].append(o)
        tok = ("eng", eng, o.idx)
        self._commit(tok, reads, writes)
        return tok

    def dma(self, eng, fn, semkey, reads=(), writes=()):
        if self.rec is not None:
            self.rec.append(("dma", eng, fn, semkey, tuple(reads), tuple(writes)))
            return None
        toks = self._filter(eng, self._deps(reads, writes))
        cnt = self.dma_count.get(semkey, 0) + 16
        self.dma_count[semkey] = cnt
        o = _Op(fn, toks, (semkey, cnt))
        o.idx = len(self.streams[eng])
        self.streams[eng].append(o)
        tok = ("dma", semkey, cnt)
        self._commit(tok, reads, writes)
        return tok

    def merge(self, lists):
        lists = [l for l in lists if l]
        heads = [0] * len(lists)
        rem = []
        for l in lists:
            acc, suf = 0.0, [0.0] * (len(l) + 1)
            for i_ in range(len(l) - 1, -1, -1):
                acc += getattr(l[i_][2], "cost", 0.3) + 0.1
                suf[i_] = acc
            rem.append(suf)
        while True:
            best = None
            for li, l in enumerate(lists):
                if heads[li] >= len(l):
                    continue
                kind, eng, fn, semkey, reads, writes = l[heads[li]]
                toks = self._deps(reads, writes, eng)
                ready = 0.0
                for t in toks:
                    tt_ = self.t_tok.get(t)
                    if tt_ is not None and tt_ > ready:
                        ready = tt_
                start = max(self.t_eng.get(eng, 0.0), ready)
                cand = (start, -rem[li][heads[li]], li)
                if best is None or cand < best:
                    best = cand
            if best is None:
                break
            start, _, li = best
            kind, eng, fn, semkey, reads, writes = lists[li][heads[li]]
            heads[li] += 1
            if kind == "op":
                tok = self.op(eng, fn, reads, writes)
                cost = getattr(fn, "cost", 0.3)
                self.t_eng[eng] = start + cost
                self.t_tok[tok] = start + cost + 0.06
            else:
                tok = self.dma(eng, fn, semkey, reads, writes)
                self.t_eng[eng] = start + 0.1
                self.t_tok[tok] = start + 2.5

    def fix_group(self, semkey):
        fin = self.dma_count[semkey]
        for k, t in list(self.last_write.items()):
            if t[0] == "dma" and t[1] == semkey:
                self.last_write[k] = ("dma", semkey, fin)
        for k, lst in self.readers.items():
            self.readers[k] = [("dma", semkey, fin) if (t[0] == "dma" and t[1] == semkey) else t for t in lst]

    def barrier(self):
        toks = []
        for e in ("pe", "act", "dve", "pool"):
            for o in reversed(self.streams[e]):
                if o.dma is None and o.fn is not None:
                    toks.append(("eng", e, o.idx))
                    break
        toks += [("dma", k, c) for k, c in self.dma_count.items()]
        self.barrier_toks = toks

    def finalize(self):
        toks = [("dma", k, c) for k, c in self.dma_count.items()]
        toks = self._filter("sp", toks)
        o = _Op(None, toks, None)
        o.idx = len(self.streams["sp"])
        self.streams["sp"].append(o)

    def emit(self, nc):
        for e in self.ENGS:
            for o in self.streams[e]:
                for (kind, who, val) in o.waits:
                    if kind == "eng":
                        self.streams[who][val].signal = True
        semval = {}
        for e in self.ENGS:
            c = 0
            for o in self.streams[e]:
                if o.signal:
                    c += 1
                    semval[(e, o.idx)] = c
        with contextlib.ExitStack() as st:
            esem = {e: st.enter_context(nc.semaphore("s_" + e)) for e in self.ENGS}
            dsem = {k: st.enter_context(nc.semaphore("d_%d" % i)) for i, k in enumerate(self.dma_count)}
            block = st.enter_context(nc.Block())

            def run(engname):
                def body(e):
                    for o in self.streams[engname]:
                        for (kind, who, val) in o.waits:
                            if kind == "eng":
                                e.wait_ge(esem[who], semval[(who, val)])
                            else:
                                e.wait_ge(dsem[who], val)
                        if o.fn is None:
                            continue
                        ins = o.fn(e)
                        if o.dma is not None:
                            ins.then_inc(dsem[o.dma[0]], 16)
                        elif o.signal:
                            ins.then_inc(esem[engname], 1)
                return body

            block.tensor(run("pe"))
            block.scalar(run("act"))
            block.vector(run("dve"))
            block.gpsimd(run("pool"))
            block.sync(run("sp"))
        return {e: len(self.streams[e]) for e in self.ENGS}


class Ins:
    def __init__(self, fn, cost):
        self.fn = fn
        self.cost = cost

    def __call__(self, e):
        return self.fn(e)


def _fsz(ap):
    n = 1
    for d in tuple(ap.shape)[1:]:
        n *= int(d)
    return n


def MM(out, lhsT, rhs, start=True, stop=True, skip=False):
    cost = max(_fsz(out) * (2 if lhsT.dtype == F32 else 1) / 2400.0, 0.107) + 0.01
    if skip:
        return Ins(lambda e: e.matmul(out, lhsT=lhsT, rhs=rhs, start=start, stop=stop, skip_group_check=True), cost)
    return Ins(lambda e: e.matmul(out, lhsT=lhsT, rhs=rhs, start=start, stop=stop), cost)


def TR(out, in_, idt):
    return Ins(lambda e: e.transpose(out, in_, idt), max(_fsz(out) * (2 if in_.dtype == F32 else 1) / 2400.0, 0.107) + 0.01)


def ACT(out, in_, func, **kw):
    return Ins(lambda e: e.activation(out, in_, func, **kw), _fsz(out) / 1000.0 + 0.12)


def TT(out, in0, in1, op):
    return Ins(lambda e: e.tensor_tensor(out=out, in0=in0, in1=in1, op=op), _fsz(out) / 1000.0 + 0.07)


def TS(out, in0, s1, s2, op0, op1):
    return Ins(lambda e: e.tensor_scalar(out, in0, s1, s2, op0=op0, op1=op1), _fsz(out) / 1000.0 + 0.07)


def STT(out, in0, scalar, in1, op0, op1):
    return Ins(lambda e: e.scalar_tensor_tensor(out=out, in0=in0, scalar=scalar, in1=in1, op0=op0, op1=op1), _fsz(out) / 1000.0 + 0.07)


def RED(out, in_):
    return Ins(lambda e: e.tensor_reduce(out, in_, axis=AX.X, op=ALU.add), _fsz(in_) / 1000.0 + 0.07)


def CP(out, in_):
    return Ins(lambda e: e.tensor_copy(out, in_), _fsz(out) / 1000.0 + 0.07)


def RCP(out, in_):
    return Ins(lambda e: e.reciprocal(out, in_), _fsz(out) / 1000.0 + 0.07)


def MSET(ap, v):
    return Ins(lambda e: e.memset(ap, v), _fsz(ap) / 1000.0 + 0.07)


def DMA(out, in_):
    return Ins(lambda e: e.dma_start(out=out, in_=in_), 2.0)


def _tile_types():
    return {"meta": (16, 16), "prm": (128, 128), "smp": (64, 4)}


def host_constants():
    c = {}
    c["ident"] = np.eye(128, dtype=np.float32)
    pos = np.concatenate([np.arange(LP), np.tile(PAST + np.arange(DS), NS)]).astype(np.float32)
    cl = np.concatenate([np.arange(NMETA), np.tile(np.arange(128), SEQ // 128), np.tile(np.arange(DS), NS)])
    cn = np.concatenate([np.full(NMETA, NMETA), np.full(SEQ, 128), np.full(NS * DS, DS)])
    half = DH // 2
    inv_freq = np.power(np.float32(10000.0), -np.arange(half, dtype=np.float32) / np.float32(half)).astype(np.float32)
    ang = (pos[:, None] * inv_freq[None, :]).astype(np.float32)
    cos = np.cos(ang).astype(np.float32).astype(np.float64)
    sin = np.sin(ang).astype(np.float32).astype(np.float64)
    g = np.array(GAM, dtype=np.float64)
    qdec = g[None, :] ** (cl[:, None] + 1.0)
    kdec = g[None, :] ** (cn[:, None] - 1.0 - cl[:, None])
    rope = np.zeros((NROW, 644), np.float64)
    rope[:, 0:256] = (cos[:, None, :] * qdec[:, :, None]).reshape(NROW, 256)
    rope[:, 256:512] = (sin[:, None, :] * qdec[:, :, None]).reshape(NROW, 256)
    rope[:, 512:576] = cos * QSCALE
    rope[:, 576:640] = sin * QSCALE
    rope[:, 640:644] = kdec
    c["rope"] = rope.astype(np.float32)
    for name, (n, sl) in _tile_types().items():
        idx = np.arange(n)
        seq = idx // sl
        loc = idx % sl
        same = seq[:, None] == seq[None, :]
        ok = same & (idx[None, :] >= idx[:, None])
        mr = np.zeros((n, H, n), np.float64)
        for h in range(H):
            mr[:, h, :] = np.where(ok, (g[h] ** (-(loc[:, None] + 1.0))), 0.0)
        c["mr_" + name] = mr.astype(np.float32)
        c["nma_" + name] = np.where(same & (idx[:, None] > idx[None, :]), 0.0, NEG).astype(np.float32)
        c["nmq_" + name] = np.where(same & (idx[None, :] >= idx[:, None]), 0.0, NEG).astype(np.float32)
        c["tri_" + name] = (same & (idx[:, None] <= idx[None, :])).astype(np.float32)
        c["blk_" + name] = same.astype(np.float32)
    n = NS * DS
    seq = np.arange(n) // DS
    c["mfm"] = np.broadcast_to((np.arange(NS)[:, None] == seq[None, :]).astype(np.float32)[None], (128, NS, n)).copy()
    c["mtm"] = (seq[:, None] == np.arange(NS)[None, :]).astype(np.float32)
    return c


CONST_SHAPES = None


def build_program():
    nc = bass.Bass("TRN2", target_bir_lowering=False)
    consts = host_constants()

    def din(name, shape):
        return nc.dram_tensor(name, list(shape), F32, kind="ExternalInput").ap()

    def dout(name, shape):
        return nc.dram_tensor(name, list(shape), F32, kind="ExternalOutput").ap()

    x = din("x", [NROW, D])
    sall_in = din("sall", [NS, 2 * H, DH, DH])
    scq_in = din("scq", [NS * 3, 1536])
    scf_in = din("scf", [NS * 2, 2 * DFF])
    w_in = din("w_in", [D, INW])
    w_out = din("w_out", [D, D])
    w_up = din("w_up", [D, 2 * DFF])
    w_down = din("w_down", [DFF, D])
    nm_col = din("nm_col", [128, 8])
    nf_col = din("nf_col", [128, 8])
    cwg_in = din("cwg", [128, 48])
    cwf_in = din("cwf", [128, 132])
    alog_in = din("alog", [4])
    dtb_in = din("dtb", [4])
    nret_in = din("nret", [512])
    ngdn_in = din("ngdn", [128])
    nfin_in = din("nfin", [D])
    cd = {k: din("c_" + k, v.shape) for k, v in consts.items()}

    y = dout("y", [SEQ + NS * DS, D])
    o_sret_p = dout("o_sret_p", [H, DH, DH])
    o_sgdn_p = dout("o_sgdn_p", [H, DH, DH])
    o_cq_p = dout("o_cq_p", [3, 1536])
    o_cf_p = dout("o_cf_p", [2, 2 * DFF])
    o_sall_s = dout("o_sall_s", [NS, 2 * H, DH, DH])
    o_cq_s = dout("o_cq_s", [NS, 3, 1536])
    o_cf_s = dout("o_cf_s", [NS, 2, 2 * DFF])
    x1s = nc.dram_tensor("x1s", [NROW, D], F32, kind="Internal").ap()

    P = Plan()
    pe = lambda fn, r=(), w=(): P.op("pe", fn, r, w)
    act = lambda fn, r=(), w=(): P.op("act", fn, r, w)
    dve = lambda fn, r=(), w=(): P.op("dve", fn, r, w)
    pool = lambda fn, r=(), w=(): P.op("pool", fn, r, w)

    with contextlib.ExitStack() as top:
        ps = [top.enter_context(nc.psum_tensor("ps%d" % i, [128, 512], F32)) for i in range(8)]
        psb = [p.bitcast(BF16) for p in ps]
        PK = lambda i: "ps%d" % i

        with contextlib.ExitStack() as st:
            def T(name, shape, dt=F32):
                return st.enter_context(nc.sbuf_tensor("sb_" + name, list(shape), dt))

            w_in_sb = T("w_in_sb", [128, 8, INW], BF16)
            w_out_sb = T("w_out_sb", [128, 8, D], BF16)
            ident_f = T("ident_f", [128, 128])
            ident_b = T("ident_b", [128, 128], BF16)
            negid = T("negid", [128, 128])
            ones_f = T("ones_f", [128, 128])
            negones = T("negones", [128, 128])
            msk = {}
            for name, (n, sl) in _tile_types().items():
                msk[name] = dict(
                    mr=T("mr_" + name, [n, H, n]), nma=T("nma_" + name, [n, n]), nmq=T("nmq_" + name, [n, n]),
                    tri=T("tri_" + name, [n, n]), blk=T("blk_" + name, [n, n]))
            mfm = T("mfm", [128, NS, NS * DS], BF16)
            mtm = T("mtm", [NS * DS, NS])
            nrB = T("nrB", [128, 512])
            ngB = T("ngB", [128, 128])
            nmc = T("nmc", [128, 8])
            cwg = T("cwg", [128, 12, 4])
            dtb = T("dtb", [128, 4])
            negA = T("negA", [128, 4])
            xn = T("xn", [128, D], BF16)
            hT2 = [T("hT0", [128, 8, 128], BF16), T("hT1", [128, 8, 128], BF16)]
            xt = T("xt", [128, D])
            xres = T("xres", [128, D])
            rp = T("rp", [128, 644])
            ta = T("ta", [128, 256])
            tb = T("tb", [128, 256])
            qr = T("qr", [128, H, DH], BF16)
            kT = T("kT", [128, H, 128], BF16)
            pc = T("pc", [128, 12, 131])
            carry = T("carry", [128, 12, 3])
            cv2 = [T("cv0", [128, 12, 128]), T("cv1", [128, 12, 128])]
            sqb = T("sqb", [128, 8, 128])
            kTg = T("kTg", [128, H, 128], BF16)
            qTg = T("qTg", [128, H, 128], BF16)
            sm = T("sm", [128, 64])
            g1 = T("g1", [128, H, 128])
            g2 = T("g2", [128, H, 128])
            Am = [T("Am0", [128, H, 128]), T("Am1", [128, H, 128])]
            Bm = [T("Bm0", [128, H, 128]), T("Bm1", [128, H, 128])]
            kb = T("kb", [128, H, DH])
            cst = T("cst", [128, 512])
            cen = T("cen", [128, 512])
            sqh = T("sqh", [128, 512])
            mixed = T("mixed", [128, D], BF16)
            mT = T("mT", [128, 8, 128], BF16)
            w_sb = T("w_sb", [128, H, DH], BF16)
            sret = T("sret", [128, H, DH])
            sgdn = T("sgdn", [128, H, DH])
            sretb = T("sretb", [128, H, DH], BF16)
            sgdnb = T("sgdnb", [128, H, DH], BF16)
            smy = T("smy", [128, 16])

            HSPEC = [("qdT", [128, H, 128], BF16), ("scT", [128, H, 128], BF16),
                     ("vr", [128, 512], BF16), ("vk", [128, H, DH], BF16), ("kr", [128, H, DH], BF16),
                     ("Pm", [128, H, 128], F32), ("vb", [128, H, DH], F32), ("nwkT", [128, H, 128], BF16),
                     ("qdTg", [128, H, 128], BF16), ("qkT", [128, H, 128], BF16), ("kdec", [128, H, DH], BF16),
                     ("sgr", [128, 512], F32), ("sgg", [128, 512], F32), ("CDt", [128, NS * H], F32)]

            def alloc_hset(p, TT_):
                return {nm: TT_("%s%d" % (nm, p), shp, dt) for (nm, shp, dt) in HSPEC}

            hset = {1: alloc_hset(1, T)}

            SU = "setup"
            P.dma("pool", DMA(ident_b[:], cd["ident"]), "identb", writes=["ident_b"])
            w_in_v = w_in.rearrange("(k p) f -> p k f", p=128)
            for (tag, c0, c1) in (("q", 2048, 3584), ("a", 0, 2048), ("g", 3584, INW)):
                P.dma("pool", DMA(w_in_sb[:, :, c0:c1], w_in_v[:, :, c0:c1]), "w_in_" + tag, writes=[("w_in_sb", tag)])
            P.dma("pool", DMA(w_out_sb[:], w_out.rearrange("(k p) m -> p k m", p=128)), "w_out", writes=["w_out_sb"])
            P.dma("pool", DMA(mfm[:], cd["mfm"]), "mfm", writes=["mfm"])
            sp_setup = [(ident_f[:], cd["ident"], "ident_f"), (mtm[:], cd["mtm"], "mtm"),
                        (nrB[:], nret_in.partition_broadcast(128), "nrB"), (ngB[:], ngdn_in.partition_broadcast(128), "ngB"),
                        (nmc[:], nm_col, "nmc"), (cwg[:].rearrange("p c i -> p (c i)"), cwg_in, "cwg"),
                        (dtb[:], dtb_in.partition_broadcast(128), "dtb"), (negA[:], alog_in.partition_broadcast(128), "negA")]
            for name in msk:
                for kk_ in ("mr", "nma", "nmq", "tri", "blk"):
                    sp_setup.append((msk[name][kk_][:], cd[kk_ + "_" + name], kk_ + "_" + name))
            for (o_, i_, key) in sp_setup:
                P.dma("sp", DMA(o_, i_), SU, writes=[key])
            P.fix_group(SU)
            dve(MSET(ones_f[:], 1.0), w=["ones_f"])
            dve(MSET(negones[:], -1.0), w=["negones"])
            dve(TS(negid[:], ident_f[:], -1.0, 0.0, ALU.mult, ALU.add), r=["ident_f"], w=["negid"])
            act(ACT(negA[:], negA[:], AF.Exp), r=["negA"], w=["negA"])
            dve(TS(negA[:], negA[:], -1.0, 0.0, ALU.mult, ALU.add), r=["negA"], w=["negA"])
            dve(MSET(sret[:], 0.0), w=["sret"])
            dve(MSET(sgdn[:], 0.0), w=["sgdn"])
            dve(MSET(sretb[:], 0.0), w=["sretb"])
            dve(MSET(sgdnb[:], 0.0), w=["sgdnb"])
            dve(MSET(carry[:], 0.0), w=["carry"])

            def bc_mid(ap2, n, reps):
                return ap2.unsqueeze(1).to_broadcast([n, reps, ap2.shape[1]])

            def bc_last(ap2, n, m):
                return ap2.unsqueeze(2).to_broadcast([n, ap2.shape[1], m])

            def rstd_chain(dst, src, scale, r_keys, key):
                dve(TS(dst, src, scale, EPS, ALU.mult, ALU.add), r=r_keys, w=[key])
                act(ACT(dst, dst, AF.Ln), r=[key], w=[key])
                act(ACT(dst, dst, AF.Exp, scale=-0.5), r=[key], w=[key])

            def stage_x1(r0, n, tname, sample, last_prompt, q):
                hT, cv = hT2[q], cv2[q]
                KH, KC = "hT%d" % q, "cv%d" % q
                P.dma("sp", DMA(xt[:n, :], x[r0:r0 + n, :]), "xt_ld", writes=["xt"])
                junk = pc[:].rearrange("p c n -> p (c n)")[:n, 0:D]
                dve(MSET(sm[:n, 0:1], 0.0), w=["sm0"])
                act(ACT(junk, xt[:n, :], AF.Square, accum_out=sm[:n, 0:1]), r=["xt"], w=["pc", "sm0"])
                rstd_chain(sm[:n, 0:1], sm[:n, 0:1], 1.0 / D, ["sm0"], "sm0")
                act(ACT(xn[:n, :], xt[:n, :], AF.Copy, scale=sm[:n, 0:1]), r=["xt", "sm0"], w=["xn"])
                for k in range(8):
                    pe(TR(psb[0][:, k * 128:k * 128 + n], xn[:n, k * 128:(k + 1) * 128], ident_b[:n, :n]),
                       r=["xn", "ident_b"], w=[PK(0)])
                dve(TT(hT[:, :, :n], psb[0][:].rearrange("p (k t) -> p k t", k=8)[:, :, :n],
                       bc_last(nmc[:, :], 128, n), ALU.mult), r=[PK(0), "nmc"], w=[KH])
                yield
                if not sample:
                    dve(CP(pc[:, :, 0:3], carry[:, :, :]), r=["carry"], w=["pc"])
                else:
                    scq_tm = cv[:].rearrange("p c n -> p (c n)")[:48, :]
                    P.dma("sp", DMA(scq_tm, scq_in), "scq_ld", writes=[KC])
                    pcv = pc[:, :, 0:112].rearrange("p c (s j) -> p c s j", j=7)
                    for c in range(12):
                        bank = 2 if c < 6 else 0
                        off = (c % 6) * 48
                        pe(TR(ps[bank][:, off:off + 48], scq_tm[:, c * 128:(c + 1) * 128], ident_f[:48, :48]),
                           r=[KC, "ident_f"], w=[PK(bank)])
                    for c in range(12):
                        bank = 2 if c < 6 else 0
                        off = (c % 6) * 48
                        act(ACT(pcv[:, c, :, 0:3], ps[bank][:, off:off + 48].rearrange("p (s i) -> p s i", i=3), AF.Copy),
                            r=[PK(bank)], w=["pc"])
                fm_banks = [1, 2, 1]
                for gi in range(3):
                    bank = fm_banks[gi]
                    for c in range(gi * 4, gi * 4 + 4):
                        for k in range(8):
                            pe(MM(ps[bank][:, (c % 4) * 128:(c % 4) * 128 + n], w_in_sb[:, k, 2048 + c * 128:2048 + (c + 1) * 128],
                                  hT[:, k, :n], start=(k == 0), stop=(k == 7)), r=[KH, ("w_in_sb", "q")], w=[PK(bank)])
                    if not sample:
                        act(ACT(pc[:, gi * 4:(gi + 1) * 4, 3:3 + n], ps[bank][:].rearrange("p (c t) -> p c t", c=4)[:, :, :n], AF.Copy),
                            r=[PK(bank)], w=["pc"])
                    else:
                        for c in range(gi * 4, gi * 4 + 4):
                            act(ACT(pcv[:, c, :, 3:7], ps[bank][:, (c % 4) * 128:(c % 4) * 128 + n].rearrange("p (s j) -> p s j", j=4), AF.Copy),
                                r=[PK(bank)], w=["pc"])
                    yield
                if not sample:
                    dve(CP(carry[:, :, :], pc[:, :, n:n + 3]), r=["pc"], w=["carry"])
                for c in range(12):
                    if not sample:
                        o_ = cv[:, c, :n]
                        tp = [pc[:, c, i:i + n] for i in range(4)]
                    else:
                        o_ = cv[:, c, :n].rearrange("p (s j) -> p s j", j=4)
                        tp = [pcv[:, c, :, i:i + 4] for i in range(4)]
                    act(ACT(o_, tp[0], AF.Copy, scale=cwg[:, c, 0:1]), r=["pc", "cwg"], w=[(KC, c)])
                    for i in range(1, 4):
                        dve(STT(o_, tp[i], cwg[:, c, i:i + 1], o_, ALU.mult, ALU.add), r=["pc", "cwg", (KC, c)], w=[(KC, c)])
                    if c % 3 == 2:
                        yield
                act(ACT(cv[:, :, :n], cv[:, :, :n], AF.Silu), r=[KC], w=[KC])
                if sample or last_prompt:
                    cols = slice(0, n) if sample else slice(n - 3, n)
                    m_ = n if sample else 3
                    for j in range(3):
                        for k in range(8):
                            pe(MM(ps[2][:m_, :512], hT[:, k, cols], w_in_sb[:, k, 2048 + j * 512:2048 + (j + 1) * 512],
                                  start=(k == 0), stop=(k == 7)), r=[KH, ("w_in_sb", "q")], w=[PK(2)])
                        act(ACT(cst[:m_, :], ps[2][:m_, :512], AF.Copy), r=[PK(2)], w=["cst"])
                        if sample:
                            for i in range(3):
                                P.dma("sp", DMA(o_cq_s[:, i, j * 512:(j + 1) * 512], cst[1 + i:64:4, :]), "cst_st", reads=["cst"])
                        else:
                            P.dma("sp", DMA(o_cq_p[:, j * 512:(j + 1) * 512], cst[:3, :]), "cst_st", reads=["cst"])
                yield
                act(ACT(sqb[:, :, :n], cv[:, 0:8, :n], AF.Square), r=[KC], w=["sqb"])
                for half in range(2):
                    bank = 1 + half
                    if n == 128:
                        pe(MM(ps[bank][:, :], ones_f[:, :], sqb[:, half * 4:(half + 1) * 4, :].rearrange("p c t -> p (c t)")),
                           r=["ones_f", "sqb"], w=[PK(bank)])
                    else:
                        for c4 in range(4):
                            pe(MM(ps[bank][:, c4 * 128:c4 * 128 + n], ones_f[:, :], sqb[:, half * 4 + c4, :n]),
                               r=["ones_f", "sqb"], w=[PK(bank)])
                for half in range(2):
                    bank = 1 + half
                    dve(TS(sqb[:, half * 4:(half + 1) * 4, :n], ps[bank][:].rearrange("p (c t) -> p c t", c=4)[:, :, :n],
                           1.0, EPS, ALU.mult, ALU.add), r=[PK(bank)], w=["sqb"])
                act(ACT(sqb[:, :, :n], sqb[:, :, :n], AF.Ln), r=["sqb"], w=["sqb"])
                act(ACT(sqb[:, :, :n], sqb[:, :, :n], AF.Exp, scale=-0.5), r=["sqb"], w=["sqb"])
                dve(TT(cv[:, 0:8, :n], cv[:, 0:8, :n], sqb[:, :, :n], ALU.mult), r=[KC, "sqb"], w=[KC])
                yield

            def stage_x2r(r0, n, tname, sample, last_prompt, p, q):
                hb = hset[p]
                K = lambda nm: "%s%d" % (nm, p)
                hT = hT2[q]
                KH = "hT%d" % q
                qdT, scT, vr, vk, kr = hb["qdT"], hb["scT"], hb["vr"], hb["vk"], hb["kr"]
                sgr, sgg = hb["sgr"], hb["sgg"]
                M = msk[tname]
                P.dma("sp", DMA(rp[:n, :], cd["rope"][r0:r0 + n, :]), "rp_ld", writes=["rp"])

                def inproj(c0):
                    wtag = "a" if c0 < 2048 else "g"
                    for k in range(8):
                        pe(MM(ps[3][:n, :], hT[:, k, :n], w_in_sb[:, k, c0:c0 + 512], start=(k == 0), stop=(k == 7)),
                           r=[KH, ("w_in_sb", wtag)], w=[PK(3)])

                def rotary(dst, dkey, cq, sq):
                    v4 = ps[3][:n, :].rearrange("p (h t d) -> p h t d", h=H, t=2)
                    x1_, x2_ = v4[:, :, 0, :], v4[:, :, 1, :]
                    ta3 = ta[:n, :].rearrange("p (h d) -> p h d", h=H)
                    tb3 = tb[:n, :].rearrange("p (h d) -> p h d", h=H)
                    dve(TT(ta3, x1_, cq, ALU.mult), r=[PK(3), "rp"], w=["ta"])
                    dve(TT(tb3, x2_, sq, ALU.mult), r=[PK(3), "rp"], w=["tb"])
                    dve(TT(dst[:n, :, 0:64], ta3, tb3, ALU.subtract), r=["ta", "tb"], w=[dkey])
                    dve(TT(ta3, x2_, cq, ALU.mult), r=[PK(3), "rp"], w=["ta"])
                    dve(TT(tb3, x1_, sq, ALU.mult), r=[PK(3), "rp"], w=["tb"])
                    dve(TT(dst[:n, :, 64:128], ta3, tb3, ALU.add), r=["ta", "tb"], w=[dkey])

                inproj(0)
                rotary(qr, "qr", rp[:n, 0:256].rearrange("p (h d) -> p h d", h=H),
                       rp[:n, 256:512].rearrange("p (h d) -> p h d", h=H))
                yield
                inproj(512)
                rotary(kr, K("kr"), bc_mid(rp[:n, 512:576], n, H), bc_mid(rp[:n, 576:640], n, H))
                yield
                inproj(1024)
                act(ACT(vr[:n, :], ps[3][:n, :], AF.Copy), r=[PK(3)], w=[K("vr")])
                dve(TT(vk[:n, :, :], ps[3][:n, :].rearrange("p (h d) -> p h d", h=H), bc_last(rp[:n, 640:644], n, DH), ALU.mult),
                    r=[PK(3), "rp"], w=[K("vk")])
                inproj(1536)
                act(ACT(sgr[:n, :], ps[3][:n, :], AF.Silu), r=[PK(3)], w=[K("sgr")])
                inproj(3584)
                act(ACT(sgg[:n, :], ps[3][:n, :], AF.Silu), r=[PK(3)], w=[K("sgg")])
                yield
                for h in range(H):
                    pe(TR(psb[3][:, h * 128:h * 128 + n], qr[:n, h, :], ident_b[:n, :n]), r=["qr", "ident_b"], w=[PK(3)])
                    pe(TR(psb[3][:, (4 + h) * 128:(4 + h) * 128 + n], kr[:n, h, :], ident_b[:n, :n]), r=[K("kr"), "ident_b"], w=[PK(3)])
                pbv = psb[3][:].rearrange("p (k t) -> p k t", k=8)
                act(ACT(qdT[:, :, :n], pbv[:, 0:4, :n], AF.Copy), r=[PK(3)], w=[K("qdT")])
                act(ACT(kT[:, :, :n], pbv[:, 4:8, :n], AF.Copy), r=[PK(3)], w=["kT"])
                p3v = ps[3][:].rearrange("p (h t) -> p h t", h=H)
                for h in range(H):
                    pe(MM(ps[3][:n, h * 128:h * 128 + n], kT[:, h, :n], qdT[:, h, :n]), r=["kT", K("qdT")], w=[PK(3)])
                dve(TT(scT[:n, :, :n], p3v[:n, :, :n], M["mr"][:n, :, :n], ALU.mult), r=[PK(3), "mr_" + tname], w=[K("scT")])
                yield

            def stage_x2g(r0, n, tname, sample, last_prompt, p, q):
                hb = hset[p]
                K = lambda nm: "%s%d" % (nm, p)
                hT, cv = hT2[q], cv2[q]
                KH, KC = "hT%d" % q, "cv%d" % q
                qkn = cv
                Pm, vb, nwkT, qdTg, qkT, kdec, CDt = hb["Pm"], hb["vb"], hb["nwkT"], hb["qdTg"], hb["qkT"], hb["kdec"], hb["CDt"]
                M = msk[tname]
                levels = {"meta": 3, "prm": 6, "smp": 1}[tname]
                p4v = ps[4][:].rearrange("p (h t) -> p h t", h=H)
                p6v = ps[6][:].rearrange("p (h t) -> p h t", h=H)
                for k in range(8):
                    pe(MM(ps[6][:n, 0:8], hT[:, k, :n], w_in_sb[:, k, 4096:4104], start=(k == 0), stop=(k == 7)),
                       r=[KH, ("w_in_sb", "g")], w=[PK(6)])
                dve(CP(sm[:n, 8:16], ps[6][:n, 0:8]), r=[PK(6)], w=["smg"])
                act(ACT(kTg[:, :, :n], qkn[:, 4:8, :n], AF.Copy), r=[KC], w=["kTg"])
                act(ACT(qTg[:, :, :n], qkn[:, 0:4, :n], AF.Copy, scale=QSCALE), r=[KC], w=["qTg"])
                dve(TT(sm[:n, 16:20], sm[:n, 12:16], dtb[:n, :], ALU.add), r=["smg", "dtb"], w=["smg"])
                act(ACT(sm[:n, 16:20], sm[:n, 16:20], AF.Exp), r=["smg"], w=["smg"])
                act(ACT(sm[:n, 16:20], sm[:n, 16:20], AF.Ln, bias=1.0), r=["smg"], w=["smg"])
                dve(TT(sm[:n, 16:20], sm[:n, 16:20], negA[:n, :], ALU.mult), r=["smg", "negA"], w=["smg"])
                act(ACT(sm[:n, 20:24], sm[:n, 8:12], AF.Exp, scale=-1.0), r=["smg"], w=["smg"])
                dve(TS(sm[:n, 20:24], sm[:n, 20:24], 1.0, 1.0, ALU.mult, ALU.add), r=["smg"], w=["smg"])
                act(ACT(sm[:n, 24:28], sm[:n, 20:24], AF.Ln), r=["smg"], w=["smg"])
                dve(TS(sm[:n, 24:28], sm[:n, 24:28], -1.0, 0.0, ALU.mult, ALU.add), r=["smg"], w=["smg"])
                dve(RCP(sm[:n, 20:24], sm[:n, 20:24]), r=["smg"], w=["smg"])
                pe(MM(ps[6][:n, 8:12], M["tri"][:n, :n], sm[:n, 16:20]), r=["smg", "tri_" + tname], w=[PK(6)])
                pe(MM(ps[6][:n, 12:16], M["blk"][:n, :n], sm[:n, 16:20]), r=["smg", "blk_" + tname], w=[PK(6)])
                dve(CP(sm[:n, 28:36], ps[6][:n, 8:16]), r=[PK(6)], w=["smg"])
                act(ACT(sm[:n, 36:40], sm[:n, 28:32], AF.Exp), r=["smg"], w=["smg"])
                dve(TT(sm[:n, 40:44], sm[:n, 32:36], sm[:n, 28:32], ALU.subtract), r=["smg"], w=["smg"])
                act(ACT(sm[:n, 40:44], sm[:n, 40:44], AF.Exp), r=["smg"], w=["smg"])
                dve(TT(sm[:n, 44:48], sm[:n, 20:24], sm[:n, 36:40], ALU.mult), r=["smg"], w=["smg"])
                if sample:
                    gmf = g2[:n].rearrange("p h t -> p (h t)")[:, 0:NS * H]
                    gm = gmf.rearrange("p (s h) -> p s h", h=H)
                    dve(TT(gm, bc_mid(sm[:n, 16:20], n, NS), bc_last(mtm[:n, :], n, H), ALU.mult), r=["smg", "mtm"], w=["g2"])
                    pe(MM(ps[6][:, 16:16 + NS * H], ones_f[:n, :], gmf), r=["g2", "ones_f"], w=[PK(6)])
                    act(ACT(CDt[:, :], ps[6][:, 16:16 + NS * H], AF.Exp), r=[PK(6)], w=[K("CDt")])
                else:
                    pe(MM(ps[6][:, 16:20], ones_f[:n, :], sm[:n, 16:20]), r=["smg", "ones_f"], w=[PK(6)])
                    act(ACT(CDt[:, 0:4], ps[6][:, 16:20], AF.Exp), r=[PK(6)], w=[K("CDt")])
                yield
                dve(TT(g1[:n, :, :n], bc_mid(ones_f[:n, :n], n, H), bc_last(sm[:n, 28:32], n, n), ALU.mult), r=["ones_f", "smg"], w=["g1"])
                dve(TT(g2[:n, :, :n], bc_mid(ident_f[:n, :n], n, H), bc_last(sm[:n, 28:32], n, n), ALU.mult), r=["ident_f", "smg"], w=["g2"])
                for h in range(H):
                    sl = slice(h * 128, h * 128 + n)
                    pe(MM(ps[4][:n, sl], ident_f[:n, :n], g1[:n, h, :n], start=True, stop=False), r=["ident_f", "g1"], w=[PK(4)])
                    pe(MM(ps[4][:n, sl], negones[:n, :n], g2[:n, h, :n], start=False, stop=True), r=["negones", "g2"], w=[PK(4)])
                    pe(MM(ps[6][:n, sl], ones_f[:n, :n], g2[:n, h, :n], start=True, stop=False), r=["ones_f", "g2"], w=[PK(6)])
                    pe(MM(ps[6][:n, sl], negid[:n, :n], g1[:n, h, :n], start=False, stop=True), r=["negid", "g1"], w=[PK(6)])
                dve(STT(g1[:n, :, :n], p4v[:n, :, :n], 0.0, bc_mid(M["nma"][:n, :n], n, H), ALU.min, ALU.add), r=[PK(4), "nma_" + tname], w=["g1"])
                dve(STT(g2[:n, :, :n], p6v[:n, :, :n], 0.0, bc_mid(M["nmq"][:n, :n], n, H), ALU.min, ALU.add), r=[PK(6), "nmq_" + tname], w=["g2"])
                for h in range(H):
                    act(ACT(g1[:n, h, :n], g1[:n, h, :n], AF.Exp, bias=sm[:n, 24 + h:25 + h]), r=["g1", "smg"], w=["g1"])
                act(ACT(g2[:n, :, :n], g2[:n, :, :n], AF.Exp), r=["g2"], w=["g2"])
                yield
                for h in range(H):
                    sl = slice(h * 128, h * 128 + n)
                    pe(MM(ps[4][:n, sl], kTg[:, h, :n], kTg[:, h, :n]), r=["kTg"], w=[PK(4)])
                    pe(MM(ps[6][:n, sl], kTg[:, h, :n], qTg[:, h, :n]), r=["kTg", "qTg"], w=[PK(6)])
                dve(TT(Am[0][:n, :, :n], p4v[:n, :, :n], g1[:n, :, :n], ALU.mult), r=[PK(4), "g1"], w=["Am0"])
                dve(TT(qkT[:n, :, :n], p6v[:n, :, :n], g2[:n, :, :n], ALU.mult), r=[PK(6), "g2"], w=[K("qkT")])
                for h in range(H):
                    pe(TR(ps[4][:n, h * 128:h * 128 + n], Am[0][:n, h, :n], ident_f[:n, :n]), r=["Am0", "ident_f"], w=[PK(4)])
                act(ACT(Bm[0][:n, :, :n], p4v[:n, :, :n], AF.Copy), r=[PK(4)], w=["Bm0"])
                yield
                dve(TT(g1[:n, :, :n], bc_mid(M["tri"][:n, :n], n, H), bc_last(sm[:n, 16:20], n, n), ALU.mult), r=["tri_" + tname, "smg"], w=["g1"])
                if n == 128:
                    pe(MM(ps[6][:, :], ones_f[:n, :], g1[:n, :, :].rearrange("p h t -> p (h t)")), r=["ones_f", "g1"], w=[PK(6)])
                else:
                    for h in range(H):
                        pe(MM(ps[6][:, h * 128:h * 128 + n], ones_f[:n, :], g1[:n, h, :n]), r=["ones_f", "g1"], w=[PK(6)])
                act(ACT(g2[:, :, :n], p6v[:, :, :n], AF.Exp), r=[PK(6)], w=["g2"])
                dve(STT(qdTg[:, :, :n], qkn[:, 0:4, :n], QSCALE, g2[:, :, :n], ALU.mult, ALU.mult), r=[KC, "g2"], w=[K("qdTg")])
                for h in range(H):
                    pe(TR(ps[4][:n, h * 128:(h + 1) * 128], qkn[:, 4 + h, :n], ident_f[:, :]), r=[KC, "ident_f"], w=[PK(4)])
                    pe(TR(ps[6][:n, h * 128:(h + 1) * 128], cv[:, 8 + h, :n], ident_f[:, :]), r=[KC, "ident_f"], w=[PK(6)])
                p4d = ps[4][:].rearrange("p (h d) -> p h d", h=H)
                p6d = ps[6][:].rearrange("p (h d) -> p h d", h=H)
                dve(TT(vb[:n], p6d[:n], bc_last(sm[:n, 20:24], n, DH), ALU.mult), r=[PK(6), "smg"], w=[K("vb")])
                dve(TT(kb[:n], p4d[:n], bc_last(sm[:n, 44:48], n, DH), ALU.mult), r=[PK(4), "smg"], w=["kb"])
                dve(TT(kdec[:n], p4d[:n], bc_last(sm[:n, 40:44], n, DH), ALU.mult), r=[PK(4), "smg"], w=[K("kdec")])
                yield
                dve(TT(Pm[:n, :, :n], bc_mid(ident_f[:n, :n], n, H), Bm[0][:n, :, :n], ALU.subtract), r=["ident_f", "Bm0"], w=[K("Pm")])
                cur = 0
                for lv in range(1, levels + 1):
                    nxt = 1 - cur
                    for h in range(H):
                        sl = slice(h * 128, h * 128 + n)
                        pe(MM(ps[4][:n, sl], Bm[cur][:n, h, :n], Am[cur][:n, h, :n]), r=["Bm%d" % cur, "Am%d" % cur], w=[PK(4)])
                        if lv < levels:
                            pe(MM(ps[6][:n, sl], Am[cur][:n, h, :n], Bm[cur][:n, h, :n]), r=["Bm%d" % cur, "Am%d" % cur], w=[PK(6)])
                    act(ACT(Am[nxt][:n, :, :n], p4v[:n, :, :n], AF.Copy), r=[PK(4)], w=["Am%d" % nxt])
                    if lv < levels:
                        dve(CP(Bm[nxt][:n, :, :n], p6v[:n, :, :n]), r=[PK(6)], w=["Bm%d" % nxt])
                    for h in range(H):
                        sl = slice(h * 128, h * 128 + n)
                        pe(MM(ps[4][:n, sl], Am[nxt][:n, h, :n], Pm[:n, h, :n]), r=["Am%d" % nxt, K("Pm")], w=[PK(4)])
                    dve(TT(Pm[:n, :, :n], Pm[:n, :, :n], p4v[:n, :, :n], ALU.add), r=[K("Pm"), PK(4)], w=[K("Pm")])
                    cur = nxt
                    yield
                for h in range(H):
                    pe(MM(ps[6][:, h * 128:h * 128 + n], kb[:n, h, :], Pm[:n, h, :n]), r=["kb", K("Pm")], w=[PK(6)])
                act(ACT(nwkT[:, :, :n], p6v[:, :, :n], AF.Copy, scale=-1.0), r=[PK(6)], w=[K("nwkT")])
                yield

            def ret_norm(n, bank, p):
                hb = hset[p]
                K = lambda nm: "%s%d" % (nm, p)
                pd = ps[bank][:].rearrange("p (h d) -> p h d", h=H)
                cen3 = cen[:n, :].rearrange("p (h d) -> p h d", h=H)
                sqh3 = sqh[:n, :].rearrange("p (h d) -> p h d", h=H)
                dve(RED(smy[:n, 0:4], pd[:n]), r=[PK(bank)], w=["smy"])
                dve(TS(smy[:n, 0:4], smy[:n, 0:4], -1.0 / DH, 0.0, ALU.mult, ALU.add), r=["smy"], w=["smy"])
                dve(TT(cen3, pd[:n], bc_last(smy[:n, 0:4], n, DH), ALU.add), r=[PK(bank), "smy"], w=["cen"])
                act(ACT(sqh[:n, :], cen[:n, :], AF.Square), r=["cen"], w=["sqh"])
                dve(RED(smy[:n, 4:8], sqh3), r=["sqh"], w=["smy"])
                rstd_chain(smy[:n, 4:8], smy[:n, 4:8], 1.0 / DH, ["smy"], "smy")
                dve(TT(cen3, cen3, bc_last(smy[:n, 4:8], n, DH), ALU.mult), r=["cen", "smy"], w=["cen"])
                dve(TT(cen[:n, :], cen[:n, :], nrB[:n, :], ALU.mult), r=["cen", "nrB"], w=["cen"])
                dve(TT(mixed[:n, 0:512], cen[:n, :], hb["sgr"][:n, :], ALU.mult), r=["cen", K("sgr")], w=["mixed"])

            def gdn_norm(n, bank, p):
                hb = hset[p]
                K = lambda nm: "%s%d" % (nm, p)
                pd = ps[bank][:].rearrange("p (h d) -> p h d", h=H)
                cen3 = cen[:n, :].rearrange("p (h d) -> p h d", h=H)
                sqh3 = sqh[:n, :].rearrange("p (h d) -> p h d", h=H)
                act(ACT(sqh[:n, :], ps[bank][:n, :], AF.Square), r=[PK(bank)], w=["sqh"])
                dve(RED(smy[:n, 8:12], sqh3), r=["sqh"], w=["smy"])
                rstd_chain(smy[:n, 8:12], smy[:n, 8:12], 1.0 / DH, ["smy"], "smy")
                dve(TT(cen3, pd[:n], bc_last(smy[:n, 8:12], n, DH), ALU.mult), r=[PK(bank), "smy"], w=["cen"])
                dve(TT(cen3, cen3, bc_mid(ngB[:n, :], n, H), ALU.mult), r=["cen", "ngB"], w=["cen"])
                dve(TT(mixed[:n, 512:1024], cen[:n, :], hb["sgg"][:n, :], ALU.mult), r=["cen", K("sgg")], w=["mixed"])

            def out_proj(r0, n, p, btr, b0, b1):
                P.dma("sp", DMA(xres[:n, :], x[r0:r0 + n, :]), "xres_ld", writes=["xres"])
                for k in range(8):
                    pe(TR(psb[btr][:, k * 128:k * 128 + n], mixed[:n, k * 128:(k + 1) * 128], ident_b[:n, :n]), r=["mixed", "ident_b"], w=[PK(btr)])
                act(ACT(mT[:, :, :n], psb[btr][:].rearrange("p (k t) -> p k t", k=8)[:, :, :n], AF.Copy), r=[PK(btr)], w=["mT"])
                for half, bank in enumerate((b0, b1)):
                    for k in range(8):
                        pe(MM(ps[bank][:n, :], mT[:, k, :n], w_out_sb[:, k, half * 512:(half + 1) * 512], start=(k == 0), stop=(k == 7)),
                           r=["mT", "w_out_sb"], w=[PK(bank)])
                    dve(TT(xres[:n, half * 512:(half + 1) * 512], xres[:n, half * 512:(half + 1) * 512], ps[bank][:n, :], ALU.add),
                        r=["xres", PK(bank)], w=["xres"])
                P.dma("sp", DMA(x1s[r0:r0 + n, :], xres[:n, :]), "xres_st", reads=["xres"], writes=["x1s"])

            def stage_y(r0, n, tname, last_prompt, p):
                hb = hset[p]
                K = lambda nm: "%s%d" % (nm, p)
                qdT, scT, vr, vk, kr = hb["qdT"], hb["scT"], hb["vr"], hb["vk"], hb["kr"]
                Pm, vb, nwkT, qdTg, qkT, kdec, CDt = hb["Pm"], hb["vb"], hb["nwkT"], hb["qdTg"], hb["qkT"], hb["kdec"], hb["CDt"]
                for h in range(H):
                    hs = slice(h * 128, (h + 1) * 128)
                    pe(MM(ps[5][:n, hs], scT[:n, h, :n], vr[:n, hs], start=True, stop=False), r=[K("scT"), K("vr")], w=[PK(5)])
                    pe(MM(ps[5][:n, hs], qdT[:, h, :n], sretb[:, h, :], start=False, stop=True), r=[K("qdT"), "sretb"], w=[PK(5)])
                for h in range(H):
                    hs = slice(h * 128, (h + 1) * 128)
                    pe(MM(ps[7][:, hs], kr[:n, h, :], vk[:n, h, :]), r=[K("kr"), K("vk")], w=[PK(7)])
                yield
                for h in range(H):
                    hs = slice(h * 128, (h + 1) * 128)
                    dve(STT(sret[:, h, :], sret[:, h, :], float(GAM[h] ** n), ps[7][:, hs], ALU.mult, ALU.add),
                        r=["sret", PK(7)], w=["sret"])
                act(ACT(sretb[:], sret[:], AF.Copy), r=["sret"], w=["sretb"])
                yield
                for h in range(H):
                    hs = slice(h * 128, (h + 1) * 128)
                    pe(MM(ps[7][:n, hs], Pm[:n, h, :n], vb[:n, h, :], start=True, stop=False), r=[K("Pm"), K("vb")], w=[PK(7)])
                    pe(MM(ps[7][:n, hs], nwkT[:, h, :n], sgdnb[:, h, :], start=False, stop=True), r=[K("nwkT"), "sgdnb"], w=[PK(7)])
                act(ACT(w_sb[:n], ps[7][:n, :].rearrange("p (h d) -> p h d", h=H), AF.Copy), r=[PK(7)], w=["w_sb"])
                yield
                ret_norm(n, 5, p)
                yield
                for h in range(H):
                    hs = slice(h * 128, (h + 1) * 128)
                    pe(MM(ps[5][:n, hs], qdTg[:, h, :n], sgdnb[:, h, :], start=True, stop=False), r=[K("qdTg"), "sgdnb"], w=[PK(5)])
                    pe(MM(ps[5][:n, hs], qkT[:n, h, :n], w_sb[:n, h, :], start=False, stop=True), r=[K("qkT"), "w_sb"], w=[PK(5)])
                for h in range(H):
                    hs = slice(h * 128, (h + 1) * 128)
                    pe(MM(ps[7][:, hs], kdec[:n, h, :], w_sb[:n, h, :]), r=[K("kdec"), "w_sb"], w=[PK(7)])
                yield
                for h in range(H):
                    hs = slice(h * 128, (h + 1) * 128)
                    dve(STT(sgdn[:, h, :], sgdn[:, h, :], CDt[:, h:h + 1], ps[7][:, hs], ALU.mult, ALU.add),
                        r=["sgdn", K("CDt"), PK(7)], w=["sgdn"])
                act(ACT(sgdnb[:], sgdn[:], AF.Copy), r=["sgdn"], w=["sgdnb"])
                if last_prompt:
                    P.dma("sp", DMA(o_sret_p.rearrange("h d e -> d h e"), sret[:]), "sret_st", reads=["sret"])
                    P.dma("sp", DMA(o_sgdn_p.rearrange("h d e -> d h e"), sgdn[:]), "sgdn_st", reads=["sgdn"])
                yield
                gdn_norm(n, 5, p)
                yield
                out_proj(r0, n, p, 7, 5, 7)
                yield

            def run_all(g):
                for _ in g:
                    pass

            def interleave(gens):
                live = [[g, w] for (g, w) in gens if g is not None]
                while live:
                    for item in list(live):
                        for _ in range(item[1]):
                            try:
                                next(item[0])
                            except StopIteration:
                                live.remove(item)
                                break

            ptiles = [(0, NMETA, "meta", False)] + [(NMETA + 128 * i, 128, "prm", i == SEQ // 128 - 1) for i in range(SEQ // 128)]
            NT = len(ptiles)

            xtiles = ptiles + [(LP, NS * DS, "smp", False)]

            def gx1(i):
                if i >= len(xtiles):
                    return None
                r_, n_, t_, l_ = xtiles[i]
                return stage_x1(r_, n_, t_, t_ == "smp", l_, i % 2)

            def gx2(i):
                if i >= len(xtiles):
                    return None
                r_, n_, t_, l_ = xtiles[i]
                return stage_x2r(r_, n_, t_, t_ == "smp", l_, i % 2, i % 2)

            def gx2g(i):
                if i >= len(xtiles):
                    return None
                r_, n_, t_, l_ = xtiles[i]
                return stage_x2g(r_, n_, t_, t_ == "smp", l_, i % 2, i % 2)

            def gy(i):
                r_, n_, t_, l_ = ptiles[i]
                return stage_y(r_, n_, t_, l_, i % 2)

            with contextlib.ExitStack() as st1:
                T1 = lambda name, shape, dt=F32: st1.enter_context(nc.sbuf_tensor("sb_" + name, list(shape), dt))
                hset[0] = alloc_hset(0, T1)
                def rec(g):
                    if g is None:
                        return []
                    P.rec = []
                    run_all(g)
                    r_ = P.rec
                    P.rec = None
                    return r_

                run_all(gx1(0))
                P.merge([rec(gx1(1)), rec(gx2(0)), rec(gx2g(0))])
                for i in range(NT):
                    P.merge([rec(gx1(i + 2)), rec(gx2(i + 1)), rec(gx2g(i + 1)), rec(gy(i))])
                P.barrier()

            with contextlib.ExitStack() as st2:
                T2 = lambda name, shape, dt=F32: st2.enter_context(nc.sbuf_tensor("sb_" + name, list(shape), dt))
                Sb = [T2("Sb0", [128, 2 * H, DH], BF16), T2("Sb1", [128, 2 * H, DH], BF16)]
                Sf = [T2("Sf0", [128, 2 * H, DH]), T2("Sf1", [128, 2 * H, DH])]
                mq = [T2("mq0", [128, H, 64], BF16), T2("mq1", [128, H, 64], BF16)]
                mw = [T2("mw0", [128, H, 64], BF16), T2("mw1", [128, H, 64], BF16)]
                mg = [T2("mg0", [128, H, 64], BF16), T2("mg1", [128, H, 64], BF16)]
                mk = [T2("mk0", [64, H, DH], BF16), T2("mk1", [64, H, DH], BF16)]
                md = [T2("md0", [64, H, DH], BF16), T2("md1", [64, H, DH], BF16)]
                n = NS * DS
                r0 = LP
                assert NT % 2 == 1
                hb = hset[1]
                K = lambda nm: "%s1" % nm
                qdT, scT, vr, vk, kr = hb["qdT"], hb["scT"], hb["vr"], hb["vk"], hb["kr"]
                Pm, vb, nwkT, qdTg, qkT, kdec, CDt = hb["Pm"], hb["vb"], hb["nwkT"], hb["qdTg"], hb["qkT"], hb["kdec"], hb["CDt"]
                for bk in (5, 1, 2):
                    dve(MSET(ps[bk][:n, :], 0.0), w=[PK(bk)])

                def pass1(par):
                    for s in range(par, NS, 2):
                        b = s % 2
                        P.dma("pool", DMA(Sb[b][:, :, :], sall_in[s].rearrange("h d e -> d h e")), "Sb%d" % b, writes=["Sb%d" % b])
                        dve(TT(mq[b][:, :, :n], qdT[:, :, :n], bc_mid(mfm[:, s, :], 128, H), ALU.mult), r=[K("qdT"), "mfm"], w=["mq%d" % b])
                        dve(TT(mw[b][:, :, :n], nwkT[:, :, :n], bc_mid(mfm[:, s, :], 128, H), ALU.mult), r=[K("nwkT"), "mfm"], w=["mw%d" % b])
                        dve(TT(mg[b][:, :, :n], qdTg[:, :, :n], bc_mid(mfm[:, s, :], 128, H), ALU.mult), r=[K("qdTg"), "mfm"], w=["mg%d" % b])
                        for h in range(H):
                            hs = slice(h * 128, (h + 1) * 128)
                            if s == 0:
                                pe(MM(ps[5][:n, hs], scT[:n, h, :n], vr[:n, hs], start=False, stop=False, skip=True), r=[K("scT"), K("vr")], w=[PK(5)])
                                pe(MM(ps[1][:n, hs], Pm[:n, h, :n], vb[:n, h, :], start=False, stop=False, skip=True), r=[K("Pm"), K("vb")], w=[PK(1)])
                            pe(MM(ps[5][:n, hs], mq[b][:, h, :n], Sb[b][:, h, :], start=False, stop=False, skip=True),
                               r=["mq%d" % b, "Sb%d" % b], w=[PK(5)])
                            pe(MM(ps[1][:n, hs], mw[b][:, h, :n], Sb[b][:, H + h, :], start=False, stop=False, skip=True),
                               r=["mw%d" % b, "Sb%d" % b], w=[PK(1)])
                            pe(MM(ps[2][:n, hs], mg[b][:, h, :n], Sb[b][:, H + h, :], start=False, stop=False, skip=True),
                               r=["mg%d" % b, "Sb%d" % b], w=[PK(2)])

                def rec_fn(f, *a_):
                    P.rec = []
                    f(*a_)
                    r_ = P.rec
                    P.rec = None
                    return r_

                P.merge([rec_fn(pass1, 0), rec_fn(pass1, 1)])
                act(ACT(w_sb[:n], ps[1][:n, :].rearrange("p (h d) -> p h d", h=H), AF.Copy), r=[PK(1)], w=["w_sb"])
                for h in range(H):
                    hs = slice(h * 128, (h + 1) * 128)
                    pe(MM(ps[2][:n, hs], qkT[:n, h, :n], w_sb[:n, h, :], start=False, stop=True, skip=True), r=[K("qkT"), "w_sb"], w=[PK(2)])
                def pass2(par):
                    for s in range(par, NS, 2):
                        b = s % 2
                        kr_ = kg_ = "Sf%d" % b
                        P.dma("sp", DMA(Sf[b][:, :, :], sall_in[s].rearrange("h d e -> d h e")), "Sfl%d" % b, writes=[kr_])
                        dve(TS(mk[b][:n], kr[:n], mtm[:n, s:s + 1], 0.0, ALU.mult, ALU.add), r=[K("kr"), "mtm"], w=["mk%d" % b])
                        dve(TS(md[b][:n], kdec[:n], mtm[:n, s:s + 1], 0.0, ALU.mult, ALU.add), r=[K("kdec"), "mtm"], w=["md%d" % b])
                        bank = 3 + (s % 2)
                        bank2 = 6 if (s % 2) else 7
                        for h in range(H):
                            hs = slice(h * 128, (h + 1) * 128)
                            pe(MM(ps[bank][:, hs], mk[b][:n, h, :], vk[:n, h, :]), r=["mk%d" % b, K("vk")], w=[PK(bank)])
                            pe(MM(ps[bank2][:, hs], md[b][:n, h, :], w_sb[:n, h, :]), r=["md%d" % b, "w_sb"], w=[PK(bank2)])
                        for h in range(H):
                            hs = slice(h * 128, (h + 1) * 128)
                            dve(STT(Sf[b][:, h, :], Sf[b][:, h, :], float(GAM[h] ** DS), ps[bank][:, hs], ALU.mult, ALU.add),
                                r=[kr_, PK(bank)], w=[kr_])
                            dve(STT(Sf[b][:, H + h, :], Sf[b][:, H + h, :], CDt[:, s * H + h:s * H + h + 1], ps[bank2][:, hs], ALU.mult, ALU.add),
                                r=[kg_, K("CDt"), PK(bank2)], w=[kg_])
                        P.dma("act", DMA(o_sall_s[s].rearrange("h d e -> d h e"), Sf[b][:, :, :]), "Sfs%d" % b, reads=[kr_])

                def rec_fn(f, *a_):
                    P.rec = []
                    f(*a_)
                    r_ = P.rec
                    P.rec = None
                    return r_

                P.merge([rec_fn(pass2, 0), rec_fn(pass2, 1)])
                ret_norm(n, 5, 1)
                gdn_norm(n, 2, 1)
                out_proj(r0, n, 1, 0, 3, 4)
                P.barrier()

        with contextlib.ExitStack() as st:
            def T(name, shape, dt=F32):
                return st.enter_context(nc.sbuf_tensor("sb_" + name, list(shape), dt))

            w_up_sb = T("w_up_sb", [128, 8, 2 * DFF], BF16)
            w_dn_sb = T("w_dn_sb", [128, 22, D], BF16)
            identB_f = T("identB_f", [128, 128])
            identB_b = T("identB_b", [128, 128], BF16)
            nfc = T("nfc", [128, 8])
            cwf = T("cwf", [128, 44, 3])
            nfinB = T("nfinB", [128, D])
            carryf = T("carryf", [128, 44, 2])
            xf = [T("xf0", [128, D]), T("xf1", [128, D])]
            xr = [T("xr0", [128, D]), T("xr1", [128, D])]
            h2n2 = [T("h2n0", [128, D], BF16), T("h2n1", [128, D], BF16)]
            h2T2 = [T("h2T0", [128, 8, 256], BF16), T("h2T1", [128, 8, 256], BF16)]
            actT2 = [T("actT0", [128, 22, 256], BF16), T("actT1", [128, 22, 256], BF16)]
            ubg2 = [T("ubg0", [128, 258]), T("ubg1", [128, 258])]
            ubv2 = [T("ubv0", [128, 258]), T("ubv1", [128, 258])]
            cg2 = [T("cg0", [128, 256]), T("cg1", [128, 256])]
            cvl2 = [T("cvl0", [128, 256]), T("cvl1", [128, 256])]
            ptmp2 = [[T("pta0", [128, 256]), T("ptb0", [128, 256])], [T("pta1", [128, 256]), T("ptb1", [128, 256])]]
            yt = T("yt", [128, D])
            smb = T("smb", [128, 8])
            sct = T("sct", [32, 256])
            stg = T("stg", [64, 512])

            P.dma("pool", DMA(identB_b[:], cd["ident"]), "idBb", writes=["identB_b"])
            for blk in (0, 5, 6, 1, 7, 2, 8, 3, 9, 4, 10):
                P.dma("pool", DMA(w_up_sb[:, :, blk * 512:(blk + 1) * 512],
                                  w_up[:, blk * 512:(blk + 1) * 512].rearrange("(k p) f -> p k f", p=128)),
                      "w_up%d" % blk, writes=[("w_up_sb", blk)])
            for hf, j0 in enumerate((0, 11)):
                P.dma("pool", DMA(w_dn_sb[:, j0:j0 + 11, :], w_down[j0 * 128:(j0 + 11) * 128, :].rearrange("(j p) m -> p j m", p=128)),
                      "w_dn%d" % hf, writes=[("w_dn_sb", hf)])
            for (o_, i_, key) in [(identB_f[:], cd["ident"], "identB_f"), (nfc[:], nf_col, "nfc"),
                                  (cwf[:].rearrange("p c i -> p (c i)"), cwf_in, "cwf"),
                                  (nfinB[:], nfin_in.partition_broadcast(128), "nfinB")]:
                P.dma("sp", DMA(o_, i_), "setupB", writes=[key])
            P.fix_group("setupB")
            dve(MSET(carryf[:], 0.0), w=["carryf"])

            def tiles_of(rg0, ncol):
                return [(rg0 + o, min(128, ncol - o), o) for o in range(0, ncol, 128)]

            glist = [(256 * gi, 256, False, False) for gi in range(LP // 256)]
            glist.append((256 * (LP // 256), LP - 256 * (LP // 256), False, True))
            glist.append((LP, NS * DS, True, False))

            def front_a(g):
                rg0, ncol, sample, last_prompt = glist[g]
                for ti, (rt0, nt, off) in enumerate(tiles_of(rg0, ncol)):
                    xk = "xf%d" % ti
                    P.dma("sp", DMA(xf[ti][:nt, :], x1s[rt0:rt0 + nt, :]), xk + "_ld", reads=["x1s"], writes=[xk])
                    dve(MSET(smb[:nt, 0:1], 0.0), w=["smbf"])
                    act(ACT(h2n2[ti][:nt, :], xf[ti][:nt, :], AF.Square, accum_out=smb[:nt, 0:1]), r=[xk], w=["h2n%d" % ti, "smbf"])
                    dve(TS(smb[:nt, 0:1], smb[:nt, 0:1], 1.0 / D, EPS, ALU.mult, ALU.add), r=["smbf"], w=["smbf"])
                    act(ACT(smb[:nt, 0:1], smb[:nt, 0:1], AF.Ln), r=["smbf"], w=["smbf"])
                    act(ACT(smb[:nt, 0:1], smb[:nt, 0:1], AF.Exp, scale=-0.5), r=["smbf"], w=["smbf"])
                    act(ACT(h2n2[ti][:nt, :], xf[ti][:nt, :], AF.Copy, scale=smb[:nt, 0:1]), r=[xk, "smbf"], w=["h2n%d" % ti])

            def front_b(g):
                rg0, ncol, sample, last_prompt = glist[g]
                h2T = h2T2[g % 2]
                for ti, (rt0, nt, off) in enumerate(tiles_of(rg0, ncol)):
                    for k in range(8):
                        pe(TR(psb[0][:, k * 128:k * 128 + nt], h2n2[ti][:nt, k * 128:(k + 1) * 128], identB_b[:nt, :nt]),
                           r=["h2n%d" % ti, "identB_b"], w=[PK(0)])
                    dve(TT(h2T[:, :, off:off + nt], psb[0][:].rearrange("p (k t) -> p k t", k=8)[:, :, :nt],
                           nfc[:, :].unsqueeze(2).to_broadcast([128, 8, nt]), ALU.mult), r=[PK(0), "nfc"], w=["h2T%d" % (g % 2)])

            def down_work(g):
                rg0, ncol, sample, last_prompt = glist[g]
                actT = actT2[g % 2]
                ak = "actT%d" % (g % 2)
                work = []
                for ti, (rt0, nt, off) in enumerate(tiles_of(rg0, ncol)):
                    xk = "xr%d" % ti

                    def ld(ti=ti, rt0=rt0, nt=nt, xk=xk):
                        P.dma("sp", DMA(xr[ti][:nt, :], x1s[rt0:rt0 + nt, :]), xk + "_ld", reads=["x1s"], writes=[xk])
                    work.append(ld)
                    for half in range(2):
                        bank = 5 + half
                        for j0 in range(0, 22, 4):
                            def mm(ti=ti, nt=nt, off=off, half=half, bank=bank, j0=j0):
                                for j in range(j0, min(j0 + 4, 22)):
                                    pe(MM(ps[bank][:nt, :], actT[:, j, off:off + nt], w_dn_sb[:, j, half * 512:(half + 1) * 512],
                                          start=(j == 0), stop=(j == 21)), r=[(ak, j), ("w_dn_sb", j // 11)], w=[PK(bank)])
                            work.append(mm)

                        def add(ti=ti, nt=nt, half=half, bank=bank, xk=xk):
                            dve(TT(xr[ti][:nt, half * 512:(half + 1) * 512], xr[ti][:nt, half * 512:(half + 1) * 512], ps[bank][:nt, :], ALU.add),
                                r=[xk, PK(bank)], w=[xk])
                        work.append(add)

                    def fin(ti=ti, rt0=rt0, nt=nt, xk=xk, sample=sample):
                        dve(MSET(smb[:nt, 1:2], 0.0), w=["smbd"])
                        act(ACT(yt[:nt, :], xr[ti][:nt, :], AF.Square, accum_out=smb[:nt, 1:2]), r=[xk], w=["yt", "smbd"])
                        dve(TS(smb[:nt, 1:2], smb[:nt, 1:2], 1.0 / D, EPS, ALU.mult, ALU.add), r=["smbd"], w=["smbd"])
                        act(ACT(smb[:nt, 1:2], smb[:nt, 1:2], AF.Ln), r=["smbd"], w=["smbd"])
                        act(ACT(smb[:nt, 1:2], smb[:nt, 1:2], AF.Exp, scale=-0.5), r=["smbd"], w=["smbd"])
                        act(ACT(yt[:nt, :], xr[ti][:nt, :], AF.Copy, scale=smb[:nt, 1:2]), r=[xk, "smbd"], w=["yt"])
                        dve(TT(yt[:nt, :], yt[:nt, :], nfinB[:nt, :], ALU.mult), r=["yt", "nfinB"], w=["yt"])
                        if sample:
                            P.dma("sp", DMA(y[SEQ:SEQ + nt, :], yt[:nt, :]), "yt_st", reads=["yt"])
                        else:
                            lo = max(rt0, NMETA)
                            hi = rt0 + nt
                            if hi > lo:
                                P.dma("sp", DMA(y[lo - NMETA:hi - NMETA, :], yt[lo - rt0:hi - rt0, :]), "yt_st", reads=["yt"])
                    work.append(fin)
                return work

            def up_group(g, what):
                rg0, ncol, sample, last_prompt = glist[g]
                h2T = h2T2[g % 2]
                h2k = "h2T%d" % (g % 2)
                actT = actT2[g % 2]
                ak = "actT%d" % (g % 2)
                if (sample or last_prompt) and what in ("tm", "all"):
                    for f in range(11):
                        for k in range(8):
                            pe(MM(ps[1][:ncol, :], h2T[:, k, :ncol], w_up_sb[:, k, f * 512:(f + 1) * 512], start=(k == 0), stop=(k == 7)),
                               r=[h2k, ("w_up_sb", f)], w=[PK(1)])
                        act(ACT(stg[:ncol, :], ps[1][:ncol, :], AF.Copy), r=[PK(1)], w=["stg"])
                        if sample:
                            for i in range(2):
                                P.dma("sp", DMA(o_cf_s[:, i, f * 512:(f + 1) * 512], stg[2 + i:64:4, :]), "stg_st", reads=["stg"])
                        else:
                            P.dma("sp", DMA(o_cf_p[:, f * 512:(f + 1) * 512], stg[ncol - 2:ncol, :]), "stg_st", reads=["stg"])

                def vw(ap2):
                    return ap2 if not sample else ap2.rearrange("p (s j) -> p s j", j=4)

                def stage1(j):
                    pj = j % 2
                    for (cidx, ub, bank, cdst, ubk, cdk, isg) in ((j, ubg2[pj], (2, 4)[pj], cg2[pj], "ubg%d" % pj, "cg%d" % pj, True),
                                                                  (22 + j, ubv2[pj], (3, 7)[pj], cvl2[pj], "ubv%d" % pj, "cvl%d" % pj, False)):
                        for k in range(8):
                            pe(MM(ps[bank][:, :ncol], w_up_sb[:, k, cidx * 128:(cidx + 1) * 128], h2T[:, k, :ncol], start=(k == 0), stop=(k == 7)),
                               r=[h2k, ("w_up_sb", cidx // 4)], w=[PK(bank)])
                        if not sample:
                            pool(CP(ub[:, 0:2], carryf[:, cidx, :]), r=[("carryf", cidx)], w=[(ubk, "c")])
                            act(ACT(ub[:, 2:2 + ncol], ps[bank][:, :ncol], AF.Copy), r=[PK(bank)], w=[(ubk, "m")])
                            pool(CP(carryf[:, cidx, :], ub[:, ncol:ncol + 2]), r=[(ubk, "m")], w=[("carryf", cidx)])
                            tp = [ub[:, i:i + ncol] for i in range(3)]
                        else:
                            ubv3 = ub[:, 0:96].rearrange("p (s j) -> p s j", j=6)
                            slot = pj * 2 + (0 if isg else 1)
                            sctv = h2n2[0].bitcast(F32)
                            sl_ = sctv[:32, slot * 128:(slot + 1) * 128]
                            skey = ("h2n0", slot)
                            P.dma("sp", DMA(sl_, scf_in[:, cidx * 128:(cidx + 1) * 128]), "sct_ld%d" % slot, writes=[skey])
                            pe(TR(ps[0][:, slot * 32:(slot + 1) * 32], sl_, identB_f[:32, :32]), r=[skey, "identB_f"], w=[PK(0)])
                            act(ACT(ubv3[:, :, 0:2], ps[0][:, slot * 32:(slot + 1) * 32].rearrange("p (s i) -> p s i", i=2), AF.Copy), r=[PK(0)], w=[ubk])
                            act(ACT(ubv3[:, :, 2:6], ps[bank][:, :ncol].rearrange("p (s j) -> p s j", j=4), AF.Copy), r=[PK(bank)], w=[ubk])
                            tp = [ubv3[:, :, i:i + 4] for i in range(3)]
                        o_ = vw(cdst[:, :ncol])
                        if isg:
                            act(ACT(o_, tp[0], AF.Copy, scale=cwf[:, cidx, 0:1]), r=[ubk, "cwf"], w=[cdk])
                            dve(STT(o_, tp[1], cwf[:, cidx, 1:2], o_, ALU.mult, ALU.add), r=[ubk, "cwf", cdk], w=[cdk])
                            dve(STT(o_, tp[2], cwf[:, cidx, 2:3], o_, ALU.mult, ALU.add), r=[ubk, "cwf", cdk], w=[cdk])
                        else:
                            pa, pb = vw(ptmp2[pj][0][:, :ncol]), vw(ptmp2[pj][1][:, :ncol])
                            act(ACT(pa, tp[0], AF.Copy, scale=cwf[:, cidx, 0:1]), r=[ubk, "cwf"], w=["pta%d" % pj])
                            pool(TS(pb, tp[1], cwf[:, cidx, 1:2], 0.0, ALU.mult, ALU.add), r=[ubk, "cwf"], w=["ptb%d" % pj])
                            dve(STT(o_, tp[2], cwf[:, cidx, 2:3], pa, ALU.mult, ALU.add), r=[ubk, "cwf", "pta%d" % pj], w=[cdk])
                            dve(TT(o_, o_, pb, ALU.add), r=[cdk, "ptb%d" % pj], w=[cdk])

                def stage2(j):
                    pj = j % 2
                    cg, cvl = cg2[pj], cvl2[pj]
                    act(ACT(cg[:, :ncol], cg[:, :ncol], AF.Silu), r=["cg%d" % pj], w=["cg%d" % pj])
                    dve(TT(actT[:, j, :ncol], cg[:, :ncol], cvl[:, :ncol], ALU.mult), r=["cg%d" % pj, "cvl%d" % pj], w=[(ak, j)])

                if what == "all":
                    for j in range(23):
                        if j < 22:
                            stage1(j)
                        if j >= 1:
                            stage2(j - 1)
                elif what in ("even", "odd"):
                    for j in range(0 if what == "even" else 1, 22, 2):
                        stage1(j)
                        stage2(j)

            def rec_call(f):
                P.rec = []
                f()
                r_ = P.rec
                P.rec = None
                return r_

            front_a(0)
            front_b(0)
            G_ = len(glist)
            for g in range(G_):
                lists = []
                if glist[g][2] or glist[g][3]:
                    lists.append(rec_call(lambda: up_group(g, "tm")))
                lists.append(rec_call(lambda: up_group(g, "even")))
                lists.append(rec_call(lambda: up_group(g, "odd")))
                if g > 0:
                    lists.append(rec_call(lambda: [w_() for w_ in down_work(g - 1)]))
                if g + 1 < G_:
                    lists.append(rec_call(lambda: (front_a(g + 1), front_b(g + 1))))
                P.merge(lists)
            for w_ in down_work(G_ - 1):
                w_()
            P.finalize()

        counts = P.emit(nc)
    return nc, consts, counts


_CACHE = {}


def kernel(x_prompt, x_sample, state_ret, state_gdn, state_conv_qkv, state_ffn_conv,
           meta_tokens, norm_mix, w_in, conv_gdn, gdn_a_log, gdn_dt_bias, norm_ret, norm_gdn, w_out,
           norm_ffn, w_up, conv_ffn, w_down, norm_final):
    f = lambda a: np.ascontiguousarray(np.asarray(a, dtype=np.float32))
    x_prompt, x_sample, state_ret, state_gdn = f(x_prompt), f(x_sample), f(state_ret), f(state_gdn)
    state_conv_qkv, state_ffn_conv, meta_tokens = f(state_conv_qkv), f(state_ffn_conv), f(meta_tokens)
    if "nc" not in _CACHE:
        _CACHE["nc"] = build_program()
    nc, consts, counts = _CACHE["nc"]
    shared = {
        "w_in": f(w_in)[0], "w_out": f(w_out)[0], "w_up": f(w_up)[0], "w_down": f(w_down)[0],
        "nm_col": f(f(norm_mix)[0].reshape(8, 128).T), "nf_col": f(f(norm_ffn)[0].reshape(8, 128).T),
        "cwg": f(f(conv_gdn)[0].reshape(4, 12, 128).transpose(2, 1, 0).reshape(128, 48)),
        "cwf": f(f(conv_ffn)[0].reshape(3, 44, 128).transpose(2, 1, 0).reshape(128, 132)),
        "alog": f(gdn_a_log)[0], "dtb": f(gdn_dt_bias)[0], "nret": f(norm_ret)[0], "ngdn": f(norm_gdn)[0],
        "nfin": f(norm_final),
    }
    for k, v in consts.items():
        shared["c_" + k] = v
    in_maps = []
    for c in range(8):
        m = dict(shared)
        m["x"] = np.concatenate([meta_tokens, x_prompt[c], x_sample[c * NS:(c + 1) * NS].reshape(NS * DS, D)], axis=0)
        m["sall"] = np.concatenate([state_ret[0, c * NS:(c + 1) * NS], state_gdn[0, c * NS:(c + 1) * NS]], axis=1)
        m["scq"] = state_conv_qkv[0, c * NS:(c + 1) * NS].reshape(NS * 3, 1536)
        m["scf"] = state_ffn_conv[0, c * NS:(c + 1) * NS].reshape(NS * 2, 2 * DFF)
        in_maps.append(m)
    res = run_bass_kernel_spmd(nc, in_maps, core_ids=list(range(8)))
    R = res.results
    y_prompt = np.stack([R[c]["y"][:SEQ] for c in range(8)])
    y_sample = np.concatenate([R[c]["y"][SEQ:].reshape(NS, DS, D) for c in range(8)])
    p_ret = np.stack([R[c]["o_sret_p"] for c in range(8)])[None]
    p_gdn = np.stack([R[c]["o_sgdn_p"] for c in range(8)])[None]
    p_cq = np.stack([R[c]["o_cq_p"] for c in range(8)])[None]
    p_cf = np.stack([R[c]["o_cf_p"] for c in range(8)])[None]
    s_ret = np.concatenate([R[c]["o_sall_s"][:, :H] for c in range(8)])[None]
    s_gdn = np.concatenate([R[c]["o_sall_s"][:, H:] for c in range(8)])[None]
    s_cq = np.concatenate([R[c]["o_cq_s"] for c in range(8)])[None]
    s_cf = np.concatenate([R[c]["o_cf_s"] for c in range(8)])[None]
    outs = (y_prompt, y_sample, p_ret, p_gdn, p_cq, p_cf, s_ret, s_gdn, s_cq, s_cf)
    return tuple(np.ascontiguousarray(o, dtype=np.float32) for o in outs)
```

```python
import contextlib
import numpy as np
import concourse.bass as bass
import concourse.mybir as mybir
from concourse.bass_utils import run_bass_kernel_spmd

F32 = mybir.dt.float32
BF16 = mybir.dt.bfloat16
AF = mybir.ActivationFunctionType
ALU = mybir.AluOpType
AX = mybir.AxisListType

D = 1024
NMETA = 16
SEQ = 2048
LP = NMETA + SEQ
NS = 16
DS = 4
NROW = LP + NS * DS
H = 4
DH = 128
INW = 4104
DFF = 2816
EPS = 1e-6
PAST = 16384
GAM = [1.0 - 2.0 ** (-5 - h) for h in range(H)]
QSCALE = DH ** -0.5
NEG = -1.0e5


class _Op:
    __slots__ = ("fn", "waits", "signal", "dma", "idx")

    def __init__(self, fn, waits, dma):
        self.fn = fn
        self.waits = waits
        self.signal = False
        self.dma = dma
        self.idx = None


class Plan:
    ENGS = ("pe", "act", "dve", "pool", "sp")

    def __init__(self, same_engine_sync=True):
        self.streams = {e: [] for e in self.ENGS}
        self.last_write = {}
        self.readers = {}
        self.by_root = {}
        self.dma_count = {}
        self.waited = {e: {} for e in self.ENGS}
        self.same_engine_sync = same_engine_sync
        self.barrier_toks = []
        self.rec = None
        self.t_eng = {}
        self.t_tok = {}

    @staticmethod
    def _k(r):
        return r if isinstance(r, tuple) else (r,)

    def _conf(self, key):
        for k in self.by_root.get(key[0], ()):
            n = min(len(k), len(key))
            if k[:n] == key[:n]:
                yield k

    def _deps(self, reads, writes, eng=None):
        toks = list(self.barrier_toks)
        for r in reads:
            key = self._k(r)
            psum = isinstance(key[0], str) and key[0].startswith("ps") and key[0][2:].isdigit()
            for k in self._conf(key):
                t = self.last_write.get(k)
                if t is not None:
                    toks.append(t)
                if psum:
                    toks.extend(t2 for t2 in self.readers.get(k, ()) if not (t2[0] == "eng" and t2[1] == eng))
        for w in writes:
            key = self._k(w)
            for k in self._conf(key):
                t = self.last_write.get(k)
                if t is not None:
                    toks.append(t)
                toks.extend(self.readers.get(k, ()))
        return toks

    def _commit(self, tok, reads, writes):
        for w in writes:
            key = self._k(w)
            for k in list(self._conf(key)):
                if len(k) >= len(key):
                    self.readers.pop(k, None)
                    if k != key:
                        self.last_write.pop(k, None)
            self.by_root.setdefault(key[0], set()).add(key)
            self.last_write[key] = tok
        for r in reads:
            key = self._k(r)
            self.by_root.setdefault(key[0], set()).add(key)
            self.readers.setdefault(key, []).append(tok)

    def _filter(self, eng, toks):
        wd = self.waited[eng]
        best = {}
        for t in toks:
            kind, who, val = t
            if kind == "eng" and who == eng:
                if eng in ("pe", "sp") or not self.same_engine_sync:
                    continue
            key = (kind, who)
            if wd.get(key, -1) >= val:
                continue
            if key not in best or best[key][2] < val:
                best[key] = t
        for key, t in best.items():
            wd[key] = t[2]
        return list(best.values())

    def op(self, eng, fn, reads=(), writes=()):
        if self.rec is not None:
            self.rec.append(("op", eng, fn, None, tuple(reads), tuple(writes)))
            return None
        toks = self._filter(eng, self._deps(reads, writes, eng))
        o = _Op(fn, toks, None)
        o.idx = len(self.streams[eng])
        self.streams[eng].append(o)
        tok = ("eng", eng, o.idx)
        self._commit(tok, reads, writes)
        return tok

    def dma(self, eng, fn, semkey, reads=(), writes=()):
        if self.rec is not None:
            self.rec.append(("dma", eng, fn, semkey, tuple(reads), tuple(writes)))
            return None
        toks = self._filter(eng, self._deps(reads, writes))
        cnt = self.dma_count.get(semkey, 0) + 16
        self.dma_count[semkey] = cnt
        o = _Op(fn, toks, (semkey, cnt))
        o.idx = len(self.streams[eng])
        self.streams[eng].append(o)
        tok = ("dma", semkey, cnt)
        self._commit(tok, reads, writes)
        return tok

    def merge(self, lists):
        lists = [l for l in lists if l]
        heads = [0] * len(lists)
        rem = []
        for l in lists:
            acc, suf = 0.0, [0.0] * (len(l) + 1)
            for i_ in range(len(l) - 1, -1, -1):
                acc += getattr(l[i_][2], "cost", 0.3) + 0.1
                suf[i_] = acc
            rem.append(suf)
        while True:
            best = None
            for li, l in enumerate(lists):
                if heads[li] >= len(l):
                    continue
                kind, eng, fn, semkey, reads, writes = l[heads[li]]
                toks = self._deps(reads, writes, eng)
                ready = 0.0
                for t in toks:
                    tt_ = self.t_tok.get(t)
                    if tt_ is not None and tt_ > ready:
                        ready = tt_
                start = max(self.t_eng.get(eng, 0.0), ready)
                cand = (start, -rem[li][heads[li]], li)
                if best is None or cand < best:
                    best = cand
            if best is None:
                break
            start, _, li = best
            kind, eng, fn, semkey, reads, writes = lists[li][heads[li]]
            heads[li] += 1
            if kind == "op":
                tok = self.op(eng, fn, reads, writes)
                cost = getattr(fn, "cost", 0.3)
                self.t_eng[eng] = start + cost
                self.t_tok[tok] = start + cost + 0.06
            else:
                tok = self.dma(eng, fn, semkey, reads, writes)
                self.t_eng[eng] = start + 0.1
                self.t_tok[tok] = start + 2.5

    def fix_group(self, semkey):
        fin = self.dma_count[semkey]
        for k, t in list(self.last_write.items()):
            if t[0] == "dma" and t[1] == semkey:
                self.last_write[k] = ("dma", semkey, fin)
        for k, lst in self.readers.items():
            self.readers[k] = [("dma", semkey, fin) if (t[0] == "dma" and t[1] == semkey) else t for t in lst]

    def barrier(self):
        toks = []
        for e in ("pe", "act", "dve", "pool"):
            for o in reversed(self.streams[e]):
                if o.dma is None and o.fn is not None:
                    toks.append(("eng", e, o.idx))
                    break
        toks += [("dma", k, c) for k, c in self.dma_count.items()]
        self.barrier_toks = toks

    def finalize(self):
        toks = [("dma", k, c) for k, c in self.dma_count.items()]
        toks = self._filter("sp", toks)
        o = _Op(None, toks, None)
        o.idx = len(self.streams["sp"])
        self.streams["sp"].append(o)

    def emit(self, nc):
        for e in self.ENGS:
            for o in self.streams[e]:
                for (kind, who, val) in o.waits:
                    if kind == "eng":
                        self.streams[who][val].signal = True
        semval = {}
        for e in self.ENGS:
            c = 0
            for o in self.streams[e]:
                if o.signal:
                    c += 1
                    semval[(e, o.idx)] = c
        with contextlib.ExitStack() as st:
            esem = {e: st.enter_context(nc.semaphore("s_" + e)) for e in self.ENGS}
            dsem = {k: st.enter_context(nc.semaphore("d_%d" % i)) for i, k in enumerate(self.dma_count)}
            block = st.enter_context(nc.Block())

            def run(engname):
                def body(e):
                    for o in self.streams[engname]:
                        for (kind, who, val) in o.waits:
                            if kind == "eng":
                                e.wait_ge(esem[who], semval[(who, val)])
                            else:
                                e.wait_ge(dsem[who], val)
                        if o.fn is None:
                            continue
                        ins = o.fn(e)
                        if o.dma is not None:
                            ins.then_inc(dsem[o.dma[0]], 16)
                        elif o.signal:
                            ins.then_inc(esem[engname], 1)
                return body

            block.tensor(run("pe"))
            block.scalar(run("act"))
            block.vector(run("dve"))
            block.gpsimd(run("pool"))
            block.sync(run("sp"))
        return {e: len(self.streams[e]) for e in self.ENGS}


class Ins:
    def __init__(self, fn, cost):
        self.fn = fn
        self.cost = cost

    def __call__(self, e):
        return self.fn(e)


def _fsz(ap):
    n = 1
    for d in tuple(ap.shape)[1:]:
        n *= int(d)
    return n


def MM(out, lhsT, rhs, start=True, stop=True, skip=False):
    cost = max(_fsz(out) * (2 if lhsT.dtype == F32 else 1) / 2400.0, 0.107) + 0.01
    if skip:
        return Ins(lambda e: e.matmul(out, lhsT=lhsT, rhs=rhs, start=start, stop=stop, skip_group_check=True), cost)
    return Ins(lambda e: e.matmul(out, lhsT=lhsT, rhs=rhs, start=start, stop=stop), cost)


def TR(out, in_, idt):
    return Ins(lambda e: e.transpose(out, in_, idt), max(_fsz(out) * (2 if in_.dtype == F32 else 1) / 2400.0, 0.107) + 0.01)


def ACT(out, in_, func, **kw):
    return Ins(lambda e: e.activation(out, in_, func, **kw), _fsz(out) / 1000.0 + 0.12)


def TT(out, in0, in1, op):
    return Ins(lambda e: e.tensor_tensor(out=out, in0=in0, in1=in1, op=op), _fsz(out) / 1000.0 + 0.07)


def TS(out, in0, s1, s2, op0, op1):
    return Ins(lambda e: e.tensor_scalar(out, in0, s1, s2, op0=op0, op1=op1), _fsz(out) / 1000.0 + 0.07)


def STT(out, in0, scalar, in1, op0, op1):
    return Ins(lambda e: e.scalar_tensor_tensor(out=out, in0=in0, scalar=scalar, in1=in1, op0=op0, op1=op1), _fsz(out) / 1000.0 + 0.07)


def RED(out, in_):
    return Ins(lambda e: e.tensor_reduce(out, in_, axis=AX.X, op=ALU.add), _fsz(in_) / 1000.0 + 0.07)


def CP(out, in_):
    return Ins(lambda e: e.tensor_copy(out, in_), _fsz(out) / 1000.0 + 0.07)


def RCP(out, in_):
    return Ins(lambda e: e.reciprocal(out, in_), _fsz(out) / 1000.0 + 0.07)


def MSET(ap, v):
    return Ins(lambda e: e.memset(ap, v), _fsz(ap) / 1000.0 + 0.07)


def DMA(out, in_):
    return Ins(lambda e: e.dma_start(out=out, in_=in_), 2.0)


def _tile_types():
    return {"meta": (16, 16), "prm": (128, 128), "smp": (64, 4)}


def host_constants():
    c = {}
    c["ident"] = np.eye(128, dtype=np.float32)
    pos = np.concatenate([np.arange(LP), np.tile(PAST + np.arange(DS), NS)]).astype(np.float32)
    cl = np.concatenate([np.arange(NMETA), np.tile(np.arange(128), SEQ // 128), np.tile(np.arange(DS), NS)])
    cn = np.concatenate([np.full(NMETA, NMETA), np.full(SEQ, 128), np.full(NS * DS, DS)])
    half = DH // 2
    inv_freq = np.power(np.float32(10000.0), -np.arange(half, dtype=np.float32) / np.float32(half)).astype(np.float32)
    ang = (pos[:, None] * inv_freq[None, :]).astype(np.float32)
    cos = np.cos(ang).astype(np.float32).astype(np.float64)
    sin = np.sin(ang).astype(np.float32).astype(np.float64)
    g = np.array(GAM, dtype=np.float64)
    qdec = g[None, :] ** (cl[:, None] + 1.0)
    kdec = g[None, :] ** (cn[:, None] - 1.0 - cl[:, None])
    rope = np.zeros((NROW, 644), np.float64)
    rope[:, 0:256] = (cos[:, None, :] * qdec[:, :, None]).reshape(NROW, 256)
    rope[:, 256:512] = (sin[:, None, :] * qdec[:, :, None]).reshape(NROW, 256)
    rope[:, 512:576] = cos * QSCALE
    rope[:, 576:640] = sin * QSCALE
    rope[:, 640:644] = kdec
    c["rope"] = rope.astype(np.float32)
    for name, (n, sl) in _tile_types().items():
        idx = np.arange(n)
        seq = idx // sl
        loc = idx % sl
        same = seq[:, None] == seq[None, :]
        ok = same & (idx[None, :] >= idx[:, None])
        mr = np.zeros((n, H, n), np.float64)
        for h in range(H):
            mr[:, h, :] = np.where(ok, (g[h] ** (-(loc[:, None] + 1.0))), 0.0)
        c["mr_" + name] = mr.astype(np.float32)
        c["nma_" + name] = np.where(same & (idx[:, None] > idx[None, :]), 0.0, NEG).astype(np.float32)
        c["nmq_" + name] = np.where(same & (idx[None, :] >= idx[:, None]), 0.0, NEG).astype(np.float32)
        c["tri_" + name] = (same & (idx[:, None] <= idx[None, :])).astype(np.float32)
        c["blk_" + name] = same.astype(np.float32)
    n = NS * DS
    seq = np.arange(n) // DS
    c["mfm"] = np.broadcast_to((np.arange(NS)[:, None] == seq[None, :]).astype(np.float32)[None], (128, NS, n)).copy()
    c["mtm"] = (seq[:, None] == np.arange(NS)[None, :]).astype(np.float32)
    return c


CONST_SHAPES = None


def build_program():
    nc = bass.Bass("TRN2", target_bir_lowering=False)
    consts = host_constants()

    def din(name, shape):
        return nc.dram_tensor(name, list(shape), F32, kind="ExternalInput").ap()

    def dout(name, shape):
        return nc.dram_tensor(name, list(shape), F32, kind="ExternalOutput").ap()

    x = din("x", [NROW, D])
    sall_in = din("sall", [NS, 2 * H, DH, DH])
    scq_in = din("scq", [NS * 3, 1536])
    scf_in = din("scf", [NS * 2, 2 * DFF])
    w_in = din("w_in", [D, INW])
    w_out = din("w_out", [D, D])
    w_up = din("w_up", [D, 2 * DFF])
    w_down = din("w_down", [DFF, D])
    nm_col = din("nm_col", [128, 8])
    nf_col = din("nf_col", [128, 8])
    cwg_in = din("cwg", [128, 48])
    cwf_in = din("cwf", [128, 132])
    alog_in = din("alog", [4])
    dtb_in = din("dtb", [4])
    nret_in = din("nret", [512])
    ngdn_in = din("ngdn", [128])
    nfin_in = din("nfin", [D])
    cd = {k: din("c_" + k, v.shape) for k, v in consts.items()}

    y = dout("y", [SEQ + NS * DS, D])
    o_sret_p = dout("o_sret_p", [H, DH, DH])
    o_sgdn_p = dout("o_sgdn_p", [H, DH, DH])
    o_cq_p = dout("o_cq_p", [3, 1536])
    o_cf_p = dout("o_cf_p", [2, 2 * DFF])
    o_sall_s = dout("o_sall_s", [NS, 2 * H, DH, DH])
    o_cq_s = dout("o_cq_s", [NS, 3, 1536])
    o_cf_s = dout("o_cf_s", [NS, 2, 2 * DFF])
    x1s = nc.dram_tensor("x1s", [NROW, D], F32, kind="Internal").ap()

    P = Plan()
    pe = lambda fn, r=(), w=(): P.op("pe", fn, r, w)
    act = lambda fn, r=(), w=(): P.op("act", fn, r, w)
    dve = lambda fn, r=(), w=(): P.op("dve", fn, r, w)
    pool = lambda fn, r=(), w=(): P.op("pool", fn, r, w)

    with contextlib.ExitStack() as top:
        ps = [top.enter_context(nc.psum_tensor("ps%d" % i, [128, 512], F32)) for i in range(8)]
        psb = [p.bitcast(BF16) for p in ps]
        PK = lambda i: "ps%d" % i

        with contextlib.ExitStack() as st:
            def T(name, shape, dt=F32):
                return st.enter_context(nc.sbuf_tensor("sb_" + name, list(shape), dt))

            w_in_sb = T("w_in_sb", [128, 8, INW], BF16)
            w_out_sb = T("w_out_sb", [128, 8, D], BF16)
            ident_f = T("ident_f", [128, 128])
            ident_b = T("ident_b", [128, 128], BF16)
            negid = T("negid", [128, 128])
            ones_f = T("ones_f", [128, 128])
            negones = T("negones", [128, 128])
            msk = {}
            for name, (n, sl) in _tile_types().items():
                msk[name] = dict(
                    mr=T("mr_" + name, [n, H, n]), nma=T("nma_" + name, [n, n]), nmq=T("nmq_" + name, [n, n]),
                    tri=T("tri_" + name, [n, n]), blk=T("blk_" + name, [n, n]))
            mfm = T("mfm", [128, NS, NS * DS], BF16)
            mtm = T("mtm", [NS * DS, NS])
            nrB = T("nrB", [128, 512])
            ngB = T("ngB", [128, 128])
            nmc = T("nmc", [128, 8])
            cwg = T("cwg", [128, 12, 4])
            dtb = T("dtb", [128, 4])
            negA = T("negA", [128, 4])
            xn = T("xn", [128, D], BF16)
            hT2 = [T("hT0", [128, 8, 128], BF16), T("hT1", [128, 8, 128], BF16)]
            xt = T("xt", [128, D])
            xres = T("xres", [128, D])
            rp = T("rp", [128, 644])
            ta = T("ta", [128, 256])
            tb = T("tb", [128, 256])
            qr = T("qr", [128, H, DH], BF16)
            kT = T("kT", [128, H, 128], BF16)
            pc = T("pc", [128, 12, 131])
            carry = T("carry", [128, 12, 3])
            cv2 = [T("cv0", [128, 12, 128]), T("cv1", [128, 12, 128])]
            sqb = T("sqb", [128, 8, 128])
            kTg = T("kTg", [128, H, 128], BF16)
            qTg = T("qTg", [128, H, 128], BF16)
            sm = T("sm", [128, 64])
            g1 = T("g1", [128, H, 128])
            g2 = T("g2", [128, H, 128])
            Am = [T("Am0", [128, H, 128]), T("Am1", [128, H, 128])]
            Bm = [T("Bm0", [128, H, 128]), T("Bm1", [128, H, 128])]
            kb = T("kb", [128, H, DH])
            cst = T("cst", [128, 512])
            cen = T("cen", [128, 512])
            sqh = T("sqh", [128, 512])
            mixed = T("mixed", [128, D], BF16)
            mT = T("mT", [128, 8, 128], BF16)
            w_sb = T("w_sb", [128, H, DH], BF16)
            sret = T("sret", [128, H, DH])
            sgdn = T("sgdn", [128, H, DH])
            sretb = T("sretb", [128, H, DH], BF16)
            sgdnb = T("sgdnb", [128, H, DH], BF16)
            smy = T("smy", [128, 16])

            HSPEC = [("qdT", [128, H, 128], BF16), ("scT", [128, H, 128], BF16),
                     ("vr", [128, 512], BF16), ("vk", [128, H, DH], BF16), ("kr", [128, H, DH], BF16),
                     ("Pm", [128, H, 128], F32), ("vb", [128, H, DH], F32), ("nwkT", [128, H, 128], BF16),
                     ("qdTg", [128, H, 128], BF16), ("qkT", [128, H, 128], BF16), ("kdec", [128, H, DH], BF16),
                     ("sgr", [128, 512], F32), ("sgg", [128, 512], F32), ("CDt", [128, NS * H], F32)]

            def alloc_hset(p, TT_):
                return {nm: TT_("%s%d" % (nm, p), shp, dt) for (nm, shp, dt) in HSPEC}

            hset = {1: alloc_hset(1, T)}

            SU = "setup"
            P.dma("pool", DMA(ident_b[:], cd["ident"]), "identb", writes=["ident_b"])
            w_in_v = w_in.rearrange("(k p) f -> p k f", p=128)
            for (tag, c0, c1) in (("q", 2048, 3584), ("a", 0, 2048), ("g", 3584, INW)):
                P.dma("pool", DMA(w_in_sb[:, :, c0:c1], w_in_v[:, :, c0:c1]), "w_in_" + tag, writes=[("w_in_sb", tag)])
            P.dma("pool", DMA(w_out_sb[:], w_out.rearrange("(k p) m -> p k m", p=128)), "w_out", writes=["w_out_sb"])
            P.dma("pool", DMA(mfm[:], cd["mfm"]), "mfm", writes=["mfm"])
            sp_setup = [(ident_f[:], cd["ident"], "ident_f"), (mtm[:], cd["mtm"], "mtm"),
                        (nrB[:], nret_in.partition_broadcast(128), "nrB"), (ngB[:], ngdn_in.partition_broadcast(128), "ngB"),
                        (nmc[:], nm_col, "nmc"), (cwg[:].rearrange("p c i -> p (c i)"), cwg_in, "cwg"),
                        (dtb[:], dtb_in.partition_broadcast(128), "dtb"), (negA[:], alog_in.partition_broadcast(128), "negA")]
            for name in msk:
                for kk_ in ("mr", "nma", "nmq", "tri", "blk"):
                    sp_setup.append((msk[name][kk_][:], cd[kk_ + "_" + name], kk_ + "_" + name))
            for (o_, i_, key) in sp_setup:
                P.dma("sp", DMA(o_, i_), SU, writes=[key])
            P.fix_group(SU)
            dve(MSET(ones_f[:], 1.0), w=["ones_f"])
            dve(MSET(negones[:], -1.0), w=["negones"])
            dve(TS(negid[:], ident_f[:], -1.0, 0.0, ALU.mult, ALU.add), r=["ident_f"], w=["negid"])
            act(ACT(negA[:], negA[:], AF.Exp), r=["negA"], w=["negA"])
            dve(TS(negA[:], negA[:], -1.0, 0.0, ALU.mult, ALU.add), r=["negA"], w=["negA"])
            dve(MSET(sret[:], 0.0), w=["sret"])
            dve(MSET(sgdn[:], 0.0), w=["sgdn"])
            dve(MSET(sretb[:], 0.0), w=["sretb"])
            dve(MSET(sgdnb[:], 0.0), w=["sgdnb"])
            dve(MSET(carry[:], 0.0), w=["carry"])

            def bc_mid(ap2, n, reps):
                return ap2.unsqueeze(1).to_broadcast([n, reps, ap2.shape[1]])

            def bc_last(ap2, n, m):
                return ap2.unsqueeze(2).to_broadcast([n, ap2.shape[1], m])

            def rstd_chain(dst, src, scale, r_keys, key):
                dve(TS(dst, src, scale, EPS, ALU.mult, ALU.add), r=r_keys, w=[key])
                act(ACT(dst, dst, AF.Ln), r=[key], w=[key])
                act(ACT(dst, dst, AF.Exp, scale=-0.5), r=[key], w=[key])

            def stage_x1(r0, n, tname, sample, last_prompt, q):
                hT, cv = hT2[q], cv2[q]
                KH, KC = "hT%d" % q, "cv%d" % q
                P.dma("sp", DMA(xt[:n, :], x[r0:r0 + n, :]), "xt_ld", writes=["xt"])
                junk = pc[:].rearrange("p c n -> p (c n)")[:n, 0:D]
                dve(MSET(sm[:n, 0:1], 0.0), w=["sm0"])
                act(ACT(junk, xt[:n, :], AF.Square, accum_out=sm[:n, 0:1]), r=["xt"], w=["pc", "sm0"])
                rstd_chain(sm[:n, 0:1], sm[:n, 0:1], 1.0 / D, ["sm0"], "sm0")
                act(ACT(xn[:n, :], xt[:n, :], AF.Copy, scale=sm[:n, 0:1]), r=["xt", "sm0"], w=["xn"])
                for k in range(8):
                    pe(TR(psb[1][:, k * 128:k * 128 + n], xn[:n, k * 128:(k + 1) * 128], ident_b[:n, :n]),
                       r=["xn", "ident_b"], w=[PK(1)])
                dve(TT(hT[:, :, :n], psb[1][:].rearrange("p (k t) -> p k t", k=8)[:, :, :n],
                       bc_last(nmc[:, :], 128, n), ALU.mult), r=[PK(1), "nmc"], w=[KH])
                yield
                if not sample:
                    dve(CP(pc[:, :, 0:3], carry[:, :, :]), r=["carry"], w=["pc"])
                else:
                    scq_tm = cv[:].rearrange("p c n -> p (c n)")[:48, :]
                    P.dma("sp", DMA(scq_tm, scq_in), "scq_ld", writes=[KC])
                    pcv = pc[:, :, 0:112].rearrange("p c (s j) -> p c s j", j=7)
                    for c in range(12):
                        bank = 2 if c < 6 else 1
                        off = (c % 6) * 48
                        pe(TR(ps[bank][:, off:off + 48], scq_tm[:, c * 128:(c + 1) * 128], ident_f[:48, :48]),
                           r=[KC, "ident_f"], w=[PK(bank)])
                    for c in range(12):
                        bank = 2 if c < 6 else 1
                        off = (c % 6) * 48
                        act(ACT(pcv[:, c, :, 0:3], ps[bank][:, off:off + 48].rearrange("p (s i) -> p s i", i=3), AF.Copy),
                            r=[PK(bank)], w=["pc"])
                fm_banks = [1, 2, 1]
                for gi in range(3):
                    bank = fm_banks[gi]
                    for c in range(gi * 4, gi * 4 + 4):
                        for k in range(8):
                            pe(MM(ps[bank][:, (c % 4) * 128:(c % 4) * 128 + n], w_in_sb[:, k, 2048 + c * 128:2048 + (c + 1) * 128],
                                  hT[:, k, :n], start=(k == 0), stop=(k == 7)), r=[KH, ("w_in_sb", "q")], w=[PK(bank)])
                    if not sample:
                        act(ACT(pc[:, gi * 4:(gi + 1) * 4, 3:3 + n], ps[bank][:].rearrange("p (c t) -> p c t", c=4)[:, :, :n], AF.Copy),
                            r=[PK(bank)], w=["pc"])
                    else:
                        for c in range(gi * 4, gi * 4 + 4):
                            act(ACT(pcv[:, c, :, 3:7], ps[bank][:, (c % 4) * 128:(c % 4) * 128 + n].rearrange("p (s j) -> p s j", j=4), AF.Copy),
                                r=[PK(bank)], w=["pc"])
                    yield
                if not sample:
                    dve(CP(carry[:, :, :], pc[:, :, n:n + 3]), r=["pc"], w=["carry"])
                for c in range(12):
                    if not sample:
                        o_ = cv[:, c, :n]
                        tp = [pc[:, c, i:i + n] for i in range(4)]
                    else:
                        o_ = cv[:, c, :n].rearrange("p (s j) -> p s j", j=4)
                        tp = [pcv[:, c, :, i:i + 4] for i in range(4)]
                    act(ACT(o_, tp[0], AF.Copy, scale=cwg[:, c, 0:1]), r=["pc", "cwg"], w=[(KC, c)])
                    for i in range(1, 4):
                        dve(STT(o_, tp[i], cwg[:, c, i:i + 1], o_, ALU.mult, ALU.add), r=["pc", "cwg", (KC, c)], w=[(KC, c)])
                    if c % 3 == 2:
                        yield
                act(ACT(cv[:, :, :n], cv[:, :, :n], AF.Silu), r=[KC], w=[KC])
                if sample or last_prompt:
                    cols = slice(0, n) if sample else slice(n - 3, n)
                    m_ = n if sample else 3
                    for j in range(3):
                        for k in range(8):
                            pe(MM(ps[2][:m_, :512], hT[:, k, cols], w_in_sb[:, k, 2048 + j * 512:2048 + (j + 1) * 512],
                                  start=(k == 0), stop=(k == 7)), r=[KH, ("w_in_sb", "q")], w=[PK(2)])
                        act(ACT(cst[:m_, :], ps[2][:m_, :512], AF.Copy), r=[PK(2)], w=["cst"])
                        if sample:
                            for i in range(3):
                                P.dma("sp", DMA(o_cq_s[:, i, j * 512:(j + 1) * 512], cst[1 + i:64:4, :]), "cst_st", reads=["cst"])
                        else:
                            P.dma("sp", DMA(o_cq_p[:, j * 512:(j + 1) * 512], cst[:3, :]), "cst_st", reads=["cst"])
                yield
                act(ACT(sqb[:, :, :n], cv[:, 0:8, :n], AF.Square), r=[KC], w=["sqb"])
                for half in range(2):
                    bank = 1 + half
                    if n == 128:
                        pe(MM(ps[bank][:, :], ones_f[:, :], sqb[:, half * 4:(half + 1) * 4, :].rearrange("p c t -> p (c t)")),
                           r=["ones_f", "sqb"], w=[PK(bank)])
                    else:
                        for c4 in range(4):
                            pe(MM(ps[bank][:, c4 * 128:c4 * 128 + n], ones_f[:, :], sqb[:, half * 4 + c4, :n]),
                               r=["ones_f", "sqb"], w=[PK(bank)])
                for half in range(2):
                    bank = 1 + half
                    dve(TS(sqb[:, half * 4:(half + 1) * 4, :n], ps[bank][:].rearrange("p (c t) -> p c t", c=4)[:, :, :n],
                           1.0, EPS, ALU.mult, ALU.add), r=[PK(bank)], w=["sqb"])
                act(ACT(sqb[:, :, :n], sqb[:, :, :n], AF.Ln), r=["sqb"], w=["sqb"])
                act(ACT(sqb[:, :, :n], sqb[:, :, :n], AF.Exp, scale=-0.5), r=["sqb"], w=["sqb"])
                dve(TT(cv[:, 0:8, :n], cv[:, 0:8, :n], sqb[:, :, :n], ALU.mult), r=[KC, "sqb"], w=[KC])
                yield

            def stage_x2r(r0, n, tname, sample, last_prompt, p, q):
                hb = hset[p]
                K = lambda nm: "%s%d" % (nm, p)
                hT = hT2[q]
                KH = "hT%d" % q
                qdT, scT, vr, vk, kr = hb["qdT"], hb["scT"], hb["vr"], hb["vk"], hb["kr"]
                sgr, sgg = hb["sgr"], hb["sgg"]
                M = msk[tname]
                P.dma("sp", DMA(rp[:n, :], cd["rope"][r0:r0 + n, :]), "rp_ld", writes=["rp"])

                def inproj(c0):
                    wtag = "a" if c0 < 2048 else "g"
                    for k in range(8):
                        pe(MM(ps[3][:n, :], hT[:, k, :n], w_in_sb[:, k, c0:c0 + 512], start=(k == 0), stop=(k == 7)),
                           r=[KH, ("w_in_sb", wtag)], w=[PK(3)])

                def rotary(dst, dkey, cq, sq):
                    v4 = ps[3][:n, :].rearrange("p (h t d) -> p h t d", h=H, t=2)
                    x1_, x2_ = v4[:, :, 0, :], v4[:, :, 1, :]
                    ta3 = ta[:n, :].rearrange("p (h d) -> p h d", h=H)
                    tb3 = tb[:n, :].rearrange("p (h d) -> p h d", h=H)
                    dve(TT(ta3, x1_, cq, ALU.mult), r=[PK(3), "rp"], w=["ta"])
                    dve(TT(tb3, x2_, sq, ALU.mult), r=[PK(3), "rp"], w=["tb"])
                    dve(TT(dst[:n, :, 0:64], ta3, tb3, ALU.subtract), r=["ta", "tb"], w=[dkey])
                    dve(TT(ta3, x2_, cq, ALU.mult), r=[PK(3), "rp"], w=["ta"])
                    dve(TT(tb3, x1_, sq, ALU.mult), r=[PK(3), "rp"], w=["tb"])
                    dve(TT(dst[:n, :, 64:128], ta3, tb3, ALU.add), r=["ta", "tb"], w=[dkey])

                inproj(0)
                rotary(qr, "qr", rp[:n, 0:256].rearrange("p (h d) -> p h d", h=H),
                       rp[:n, 256:512].rearrange("p (h d) -> p h d", h=H))
                yield
                inproj(512)
                rotary(kr, K("kr"), bc_mid(rp[:n, 512:576], n, H), bc_mid(rp[:n, 576:640], n, H))
                yield
                inproj(1024)
                act(ACT(vr[:n, :], ps[3][:n, :], AF.Copy), r=[PK(3)], w=[K("vr")])
                dve(TT(vk[:n, :, :], ps[3][:n, :].rearrange("p (h d) -> p h d", h=H), bc_last(rp[:n, 640:644], n, DH), ALU.mult),
                    r=[PK(3), "rp"], w=[K("vk")])
                inproj(1536)
                act(ACT(sgr[:n, :], ps[3][:n, :], AF.Silu), r=[PK(3)], w=[K("sgr")])
                inproj(3584)
                act(ACT(sgg[:n, :], ps[3][:n, :], AF.Silu), r=[PK(3)], w=[K("sgg")])
                yield
                for h in range(H):
                    pe(TR(psb[3][:, h * 128:h * 128 + n], qr[:n, h, :], ident_b[:n, :n]), r=["qr", "ident_b"], w=[PK(3)])
                    pe(TR(psb[3][:, (4 + h) * 128:(4 + h) * 128 + n], kr[:n, h, :], ident_b[:n, :n]), r=[K("kr"), "ident_b"], w=[PK(3)])
                pbv = psb[3][:].rearrange("p (k t) -> p k t", k=8)
                act(ACT(qdT[:, :, :n], pbv[:, 0:4, :n], AF.Copy), r=[PK(3)], w=[K("qdT")])
                act(ACT(kT[:, :, :n], pbv[:, 4:8, :n], AF.Copy), r=[PK(3)], w=["kT"])
                p3v = ps[3][:].rearrange("p (h t) -> p h t", h=H)
                for h in range(H):
                    pe(MM(ps[3][:n, h * 128:h * 128 + n], kT[:, h, :n], qdT[:, h, :n]), r=["kT", K("qdT")], w=[PK(3)])
                dve(TT(scT[:n, :, :n], p3v[:n, :, :n], M["mr"][:n, :, :n], ALU.mult), r=[PK(3), "mr_" + tname], w=[K("scT")])
                yield

            def stage_x2g(r0, n, tname, sample, last_prompt, p, q):
                hb = hset[p]
                K = lambda nm: "%s%d" % (nm, p)
                hT, cv = hT2[q], cv2[q]
                KH, KC = "hT%d" % q, "cv%d" % q
                qkn = cv
                Pm, vb, nwkT, qdTg, qkT, kdec, CDt = hb["Pm"], hb["vb"], hb["nwkT"], hb["qdTg"], hb["qkT"], hb["kdec"], hb["CDt"]
                M = msk[tname]
                levels = {"meta": 3, "prm": 6, "smp": 1}[tname]
                p4v = ps[4][:].rearrange("p (h t) -> p h t", h=H)
                p6v = ps[6][:].rearrange("p (h t) -> p h t", h=H)
                for k in range(8):
                    pe(MM(ps[6][:n, 0:8], hT[:, k, :n], w_in_sb[:, k, 4096:4104], start=(k == 0), stop=(k == 7)),
                       r=[KH, ("w_in_sb", "g")], w=[PK(6)])
                dve(CP(sm[:n, 8:16], ps[6][:n, 0:8]), r=[PK(6)], w=["smg"])
                act(ACT(kTg[:, :, :n], qkn[:, 4:8, :n], AF.Copy), r=[KC], w=["kTg"])
                act(ACT(qTg[:, :, :n], qkn[:, 0:4, :n], AF.Copy, scale=QSCALE), r=[KC], w=["qTg"])
                dve(TT(sm[:n, 16:20], sm[:n, 12:16], dtb[:n, :], ALU.add), r=["smg", "dtb"], w=["smg"])
                act(ACT(sm[:n, 16:20], sm[:n, 16:20], AF.Exp), r=["smg"], w=["smg"])
                act(ACT(sm[:n, 16:20], sm[:n, 16:20], AF.Ln, bias=1.0), r=["smg"], w=["smg"])
                dve(TT(sm[:n, 16:20], sm[:n, 16:20], negA[:n, :], ALU.mult), r=["smg", "negA"], w=["smg"])
                act(ACT(sm[:n, 20:24], sm[:n, 8:12], AF.Exp, scale=-1.0), r=["smg"], w=["smg"])
                dve(TS(sm[:n, 20:24], sm[:n, 20:24], 1.0, 1.0, ALU.mult, ALU.add), r=["smg"], w=["smg"])
                act(ACT(sm[:n, 24:28], sm[:n, 20:24], AF.Ln), r=["smg"], w=["smg"])
                dve(TS(sm[:n, 24:28], sm[:n, 24:28], -1.0, 0.0, ALU.mult, ALU.add), r=["smg"], w=["smg"])
                dve(RCP(sm[:n, 20:24], sm[:n, 20:24]), r=["smg"], w=["smg"])
                pe(MM(ps[6][:n, 8:12], M["tri"][:n, :n], sm[:n, 16:20]), r=["smg", "tri_" + tname], w=[PK(6)])
                pe(MM(ps[6][:n, 12:16], M["blk"][:n, :n], sm[:n, 16:20]), r=["smg", "blk_" + tname], w=[PK(6)])
                dve(CP(sm[:n, 28:36], ps[6][:n, 8:16]), r=[PK(6)], w=["smg"])
                act(ACT(sm[:n, 36:40], sm[:n, 28:32], AF.Exp), r=["smg"], w=["smg"])
                dve(TT(sm[:n, 40:44], sm[:n, 32:36], sm[:n, 28:32], ALU.subtract), r=["smg"], w=["smg"])
                act(ACT(sm[:n, 40:44], sm[:n, 40:44], AF.Exp), r=["smg"], w=["smg"])
                dve(TT(sm[:n, 44:48], sm[:n, 20:24], sm[:n, 36:40], ALU.mult), r=["smg"], w=["smg"])
                if sample:
                    gmf = g2[:n].rearrange("p h t -> p (h t)")[:, 0:NS * H]
                    gm = gmf.rearrange("p (s h) -> p s h", h=H)
                    dve(TT(gm, bc_mid(sm[:n, 16:20], n, NS), bc_last(mtm[:n, :], n, H), ALU.mult), r=["smg", "mtm"], w=["g2"])
                    pe(MM(ps[6][:, 16:16 + NS * H], ones_f[:n, :], gmf), r=["g2", "ones_f"], w=[PK(6)])
                    act(ACT(CDt[:, :], ps[6][:, 16:16 + NS * H], AF.Exp), r=[PK(6)], w=[K("CDt")])
                else:
                    pe(MM(ps[6][:, 16:20], ones_f[:n, :], sm[:n, 16:20]), r=["smg", "ones_f"], w=[PK(6)])
                    act(ACT(CDt[:, 0:4], ps[6][:, 16:20], AF.Exp), r=[PK(6)], w=[K("CDt")])
                yield
                dve(TT(g1[:n, :, :n], bc_mid(ones_f[:n, :n], n, H), bc_last(sm[:n, 28:32], n, n), ALU.mult), r=["ones_f", "smg"], w=["g1"])
                dve(TT(g2[:n, :, :n], bc_mid(ident_f[:n, :n], n, H), bc_last(sm[:n, 28:32], n, n), ALU.mult), r=["ident_f", "smg"], w=["g2"])
                for h in range(H):
                    sl = slice(h * 128, h * 128 + n)
                    pe(MM(ps[4][:n, sl], ident_f[:n, :n], g1[:n, h, :n], start=True, stop=False), r=["ident_f", "g1"], w=[PK(4)])
                    pe(MM(ps[4][:n, sl], negones[:n, :n], g2[:n, h, :n], start=False, stop=True), r=["negones", "g2"], w=[PK(4)])
                    pe(MM(ps[6][:n, sl], ones_f[:n, :n], g2[:n, h, :n], start=True, stop=False), r=["ones_f", "g2"], w=[PK(6)])
                    pe(MM(ps[6][:n, sl], negid[:n, :n], g1[:n, h, :n], start=False, stop=True), r=["negid", "g1"], w=[PK(6)])
                dve(STT(g1[:n, :, :n], p4v[:n, :, :n], 0.0, bc_mid(M["nma"][:n, :n], n, H), ALU.min, ALU.add), r=[PK(4), "nma_" + tname], w=["g1"])
                dve(STT(g2[:n, :, :n], p6v[:n, :, :n], 0.0, bc_mid(M["nmq"][:n, :n], n, H), ALU.min, ALU.add), r=[PK(6), "nmq_" + tname], w=["g2"])
                for h in range(H):
                    act(ACT(g1[:n, h, :n], g1[:n, h, :n], AF.Exp, bias=sm[:n, 24 + h:25 + h]), r=["g1", "smg"], w=["g1"])
                act(ACT(g2[:n, :, :n], g2[:n, :, :n], AF.Exp), r=["g2"], w=["g2"])
                yield
                for h in range(H):
                    sl = slice(h * 128, h * 128 + n)
                    pe(MM(ps[4][:n, sl], kTg[:, h, :n], kTg[:, h, :n]), r=["kTg"], w=[PK(4)])
                    pe(MM(ps[6][:n, sl], kTg[:, h, :n], qTg[:, h, :n]), r=["kTg", "qTg"], w=[PK(6)])
                dve(TT(Am[0][:n, :, :n], p4v[:n, :, :n], g1[:n, :, :n], ALU.mult), r=[PK(4), "g1"], w=["Am0"])
                dve(TT(qkT[:n, :, :n], p6v[:n, :, :n], g2[:n, :, :n], ALU.mult), r=[PK(6), "g2"], w=[K("qkT")])
                for h in range(H):
                    pe(TR(ps[4][:n, h * 128:h * 128 + n], Am[0][:n, h, :n], ident_f[:n, :n]), r=["Am0", "ident_f"], w=[PK(4)])
                act(ACT(Bm[0][:n, :, :n], p4v[:n, :, :n], AF.Copy), r=[PK(4)], w=["Bm0"])
                yield
                dve(TT(g1[:n, :, :n], bc_mid(M["tri"][:n, :n], n, H), bc_last(sm[:n, 16:20], n, n), ALU.mult), r=["tri_" + tname, "smg"], w=["g1"])
                if n == 128:
                    pe(MM(ps[6][:, :], ones_f[:n, :], g1[:n, :, :].rearrange("p h t -> p (h t)")), r=["ones_f", "g1"], w=[PK(6)])
                else:
                    for h in range(H):
                        pe(MM(ps[6][:, h * 128:h * 128 + n], ones_f[:n, :], g1[:n, h, :n]), r=["ones_f", "g1"], w=[PK(6)])
                act(ACT(g2[:, :, :n], p6v[:, :, :n], AF.Exp), r=[PK(6)], w=["g2"])
                dve(STT(qdTg[:, :, :n], qkn[:, 0:4, :n], QSCALE, g2[:, :, :n], ALU.mult, ALU.mult), r=[KC, "g2"], w=[K("qdTg")])
                for h in range(H):
                    pe(TR(ps[4][:n, h * 128:(h + 1) * 128], qkn[:, 4 + h, :n], ident_f[:, :]), r=[KC, "ident_f"], w=[PK(4)])
                    pe(TR(ps[6][:n, h * 128:(h + 1) * 128], cv[:, 8 + h, :n], ident_f[:, :]), r=[KC, "ident_f"], w=[PK(6)])
                p4d = ps[4][:].rearrange("p (h d) -> p h d", h=H)
                p6d = ps[6][:].rearrange("p (h d) -> p h d", h=H)
                dve(TT(vb[:n], p6d[:n], bc_last(sm[:n, 20:24], n, DH), ALU.mult), r=[PK(6), "smg"], w=[K("vb")])
                dve(TT(kb[:n], p4d[:n], bc_last(sm[:n, 44:48], n, DH), ALU.mult), r=[PK(4), "smg"], w=["kb"])
                dve(TT(kdec[:n], p4d[:n], bc_last(sm[:n, 40:44], n, DH), ALU.mult), r=[PK(4), "smg"], w=[K("kdec")])
                yield
                dve(TT(Pm[:n, :, :n], bc_mid(ident_f[:n, :n], n, H), Bm[0][:n, :, :n], ALU.subtract), r=["ident_f", "Bm0"], w=[K("Pm")])
                cur = 0
                for lv in range(1, levels + 1):
                    nxt = 1 - cur
                    for h in range(H):
                        sl = slice(h * 128, h * 128 + n)
                        pe(MM(ps[4][:n, sl], Bm[cur][:n, h, :n], Am[cur][:n, h, :n]), r=["Bm%d" % cur, "Am%d" % cur], w=[PK(4)])
                        if lv < levels:
                            pe(MM(ps[6][:n, sl], Am[cur][:n, h, :n], Bm[cur][:n, h, :n]), r=["Bm%d" % cur, "Am%d" % cur], w=[PK(6)])
                    act(ACT(Am[nxt][:n, :, :n], p4v[:n, :, :n], AF.Copy), r=[PK(4)], w=["Am%d" % nxt])
                    if lv < levels:
                        dve(CP(Bm[nxt][:n, :, :n], p6v[:n, :, :n]), r=[PK(6)], w=["Bm%d" % nxt])
                    for h in range(H):
                        sl = slice(h * 128, h * 128 + n)
                        pe(MM(ps[0][:n, sl], Am[nxt][:n, h, :n], Pm[:n, h, :n]), r=["Am%d" % nxt, K("Pm")], w=[PK(0)])
                    dve(TT(Pm[:n, :, :n], Pm[:n, :, :n], ps[0][:].rearrange("p (h t) -> p h t", h=H)[:n, :, :n], ALU.add),
                        r=[K("Pm"), PK(0)], w=[K("Pm")])
                    cur = nxt
                    yield
                for h in range(H):
                    pe(MM(ps[6][:, h * 128:h * 128 + n], kb[:n, h, :], Pm[:n, h, :n]), r=["kb", K("Pm")], w=[PK(6)])
                act(ACT(nwkT[:, :, :n], p6v[:, :, :n], AF.Copy, scale=-1.0), r=[PK(6)], w=[K("nwkT")])
                yield

            def ret_norm(n, bank, p):
                hb = hset[p]
                K = lambda nm: "%s%d" % (nm, p)
                pd = ps[bank][:].rearrange("p (h d) -> p h d", h=H)
                cen3 = cen[:n, :].rearrange("p (h d) -> p h d", h=H)
                sqh3 = sqh[:n, :].rearrange("p (h d) -> p h d", h=H)
                dve(RED(smy[:n, 0:4], pd[:n]), r=[PK(bank)], w=["smy"])
                dve(TS(smy[:n, 0:4], smy[:n, 0:4], -1.0 / DH, 0.0, ALU.mult, ALU.add), r=["smy"], w=["smy"])
                dve(TT(cen3, pd[:n], bc_last(smy[:n, 0:4], n, DH), ALU.add), r=[PK(bank), "smy"], w=["cen"])
                act(ACT(sqh[:n, :], cen[:n, :], AF.Square), r=["cen"], w=["sqh"])
                dve(RED(smy[:n, 4:8], sqh3), r=["sqh"], w=["smy"])
                rstd_chain(smy[:n, 4:8], smy[:n, 4:8], 1.0 / DH, ["smy"], "smy")
                dve(TT(cen3, cen3, bc_last(smy[:n, 4:8], n, DH), ALU.mult), r=["cen", "smy"], w=["cen"])
                dve(TT(cen[:n, :], cen[:n, :], nrB[:n, :], ALU.mult), r=["cen", "nrB"], w=["cen"])
                dve(TT(mixed[:n, 0:512], cen[:n, :], hb["sgr"][:n, :], ALU.mult), r=["cen", K("sgr")], w=["mixed"])

            def gdn_norm(n, bank, p):
                hb = hset[p]
                K = lambda nm: "%s%d" % (nm, p)
                pd = ps[bank][:].rearrange("p (h d) -> p h d", h=H)
                cen3 = cen[:n, :].rearrange("p (h d) -> p h d", h=H)
                sqh3 = sqh[:n, :].rearrange("p (h d) -> p h d", h=H)
                act(ACT(sqh[:n, :], ps[bank][:n, :], AF.Square), r=[PK(bank)], w=["sqh"])
                dve(RED(smy[:n, 8:12], sqh3), r=["sqh"], w=["smy"])
                rstd_chain(smy[:n, 8:12], smy[:n, 8:12], 1.0 / DH, ["smy"], "smy")
                dve(TT(cen3, pd[:n], bc_last(smy[:n, 8:12], n, DH), ALU.mult), r=[PK(bank), "smy"], w=["cen"])
                dve(TT(cen3, cen3, bc_mid(ngB[:n, :], n, H), ALU.mult), r=["cen", "ngB"], w=["cen"])
                dve(TT(mixed[:n, 512:1024], cen[:n, :], hb["sgg"][:n, :], ALU.mult), r=["cen", K("sgg")], w=["mixed"])

            def out_proj(r0, n, p, btr, b0, b1):
                P.dma("sp", DMA(xres[:n, :], x[r0:r0 + n, :]), "xres_ld", writes=["xres"])
                for k in range(8):
                    pe(TR(psb[btr][:, k * 128:k * 128 + n], mixed[:n, k * 128:(k + 1) * 128], ident_b[:n, :n]), r=["mixed", "ident_b"], w=[PK(btr)])
                act(ACT(mT[:, :, :n], psb[btr][:].rearrange("p (k t) -> p k t", k=8)[:, :, :n], AF.Copy), r=[PK(btr)], w=["mT"])
                for half, bank in enumerate((b0, b1)):
                    for k in range(8):
                        pe(MM(ps[bank][:n, :], mT[:, k, :n], w_out_sb[:, k, half * 512:(half + 1) * 512], start=(k == 0), stop=(k == 7)),
                           r=["mT", "w_out_sb"], w=[PK(bank)])
                    dve(TT(xres[:n, half * 512:(half + 1) * 512], xres[:n, half * 512:(half + 1) * 512], ps[bank][:n, :], ALU.add),
                        r=["xres", PK(bank)], w=["xres"])
                P.dma("sp", DMA(x1s[r0:r0 + n, :], xres[:n, :]), "xres_st", reads=["xres"], writes=["x1s"])

            def stage_y(r0, n, tname, last_prompt, p):
                hb = hset[p]
                K = lambda nm: "%s%d" % (nm, p)
                qdT, scT, vr, vk, kr = hb["qdT"], hb["scT"], hb["vr"], hb["vk"], hb["kr"]
                Pm, vb, nwkT, qdTg, qkT, kdec, CDt = hb["Pm"], hb["vb"], hb["nwkT"], hb["qdTg"], hb["qkT"], hb["kdec"], hb["CDt"]
                for h in range(H):
                    hs = slice(h * 128, (h + 1) * 128)
                    pe(MM(ps[5][:n, hs], scT[:n, h, :n], vr[:n, hs], start=True, stop=False), r=[K("scT"), K("vr")], w=[PK(5)])
                    pe(MM(ps[5][:n, hs], qdT[:, h, :n], sretb[:, h, :], start=False, stop=True), r=[K("qdT"), "sretb"], w=[PK(5)])
                for h in range(H):
                    hs = slice(h * 128, (h + 1) * 128)
                    pe(MM(ps[7][:, hs], kr[:n, h, :], vk[:n, h, :]), r=[K("kr"), K("vk")], w=[PK(7)])
                yield
                for h in range(H):
                    hs = slice(h * 128, (h + 1) * 128)
                    dve(STT(sret[:, h, :], sret[:, h, :], float(GAM[h] ** n), ps[7][:, hs], ALU.mult, ALU.add),
                        r=["sret", PK(7)], w=["sret"])
                act(ACT(sretb[:], sret[:], AF.Copy), r=["sret"], w=["sretb"])
                yield
                for h in range(H):
                    hs = slice(h * 128, (h + 1) * 128)
                    pe(MM(ps[7][:n, hs], Pm[:n, h, :n], vb[:n, h, :], start=True, stop=False), r=[K("Pm"), K("vb")], w=[PK(7)])
                    pe(MM(ps[7][:n, hs], nwkT[:, h, :n], sgdnb[:, h, :], start=False, stop=True), r=[K("nwkT"), "sgdnb"], w=[PK(7)])
                act(ACT(w_sb[:n], ps[7][:n, :].rearrange("p (h d) -> p h d", h=H), AF.Copy), r=[PK(7)], w=["w_sb"])
                yield
                ret_norm(n, 5, p)
                yield
                for h in range(H):
                    hs = slice(h * 128, (h + 1) * 128)
                    pe(MM(ps[5][:n, hs], qdTg[:, h, :n], sgdnb[:, h, :], start=True, stop=False), r=[K("qdTg"), "sgdnb"], w=[PK(5)])
                    pe(MM(ps[5][:n, hs], qkT[:n, h, :n], w_sb[:n, h, :], start=False, stop=True), r=[K("qkT"), "w_sb"], w=[PK(5)])
                for h in range(H):
                    hs = slice(h * 128, (h + 1) * 128)
                    pe(MM(ps[7][:, hs], kdec[:n, h, :], w_sb[:n, h, :]), r=[K("kdec"), "w_sb"], w=[PK(7)])
                yield
                for h in range(H):
                    hs = slice(h * 128, (h + 1) * 128)
                    dve(STT(sgdn[:, h, :], sgdn[:, h, :], CDt[:, h:h + 1], ps[7][:, hs], ALU.mult, ALU.add),
                        r=["sgdn", K("CDt"), PK(7)], w=["sgdn"])
                act(ACT(sgdnb[:], sgdn[:], AF.Copy), r=["sgdn"], w=["sgdnb"])
                if last_prompt:
                    P.dma("sp", DMA(o_sret_p.rearrange("h d e -> d h e"), sret[:]), "sret_st", reads=["sret"])
                    P.dma("sp", DMA(o_sgdn_p.rearrange("h d e -> d h e"), sgdn[:]), "sgdn_st", reads=["sgdn"])
                yield
                gdn_norm(n, 5, p)
                yield
                out_proj(r0, n, p, 7, 5, 7)
                yield

            def run_all(g):
                for _ in g:
                    pass

            def interleave(gens):
                live = [[g, w] for (g, w) in gens if g is not None]
                while live:
                    for item in list(live):
                        for _ in range(item[1]):
                            try:
                                next(item[0])
                            except StopIteration:
                                live.remove(item)
                                break

            ptiles = [(0, NMETA, "meta", False)] + [(NMETA + 128 * i, 128, "prm", i == SEQ // 128 - 1) for i in range(SEQ // 128)]
            NT = len(ptiles)

            xtiles = ptiles + [(LP, NS * DS, "smp", False)]

            def gx1(i):
                if i >= len(xtiles):
                    return None
                r_, n_, t_, l_ = xtiles[i]
                return stage_x1(r_, n_, t_, t_ == "smp", l_, i % 2)

            def gx2(i):
                if i >= len(xtiles):
                    return None
                r_, n_, t_, l_ = xtiles[i]
                return stage_x2r(r_, n_, t_, t_ == "smp", l_, i % 2, i % 2)

            def gx2g(i):
                if i >= len(xtiles):
                    return None
                r_, n_, t_, l_ = xtiles[i]
                return stage_x2g(r_, n_, t_, t_ == "smp", l_, i % 2, i % 2)

            def gy(i):
                r_, n_, t_, l_ = ptiles[i]
                return stage_y(r_, n_, t_, l_, i % 2)

            with contextlib.ExitStack() as st1:
                T1 = lambda name, shape, dt=F32: st1.enter_context(nc.sbuf_tensor("sb_" + name, list(shape), dt))
                hset[0] = alloc_hset(0, T1)
                def rec(g):
                    if g is None:
                        return []
                    P.rec = []
                    run_all(g)
                    r_ = P.rec
                    P.rec = None
                    return r_

                run_all(gx1(0))
                P.merge([rec(gx1(1)), rec(gx2(0)), rec(gx2g(0))])
                for i in range(NT):
                    P.merge([rec(gx1(i + 2)), rec(gx2(i + 1)), rec(gx2g(i + 1)), rec(gy(i))])
                P.barrier()

            with contextlib.ExitStack() as st2:
                T2 = lambda name, shape, dt=F32: st2.enter_context(nc.sbuf_tensor("sb_" + name, list(shape), dt))
                Sb = [T2("Sb0", [128, 2 * H, DH], BF16), T2("Sb1", [128, 2 * H, DH], BF16)]
                Sf = [T2("Sf0", [128, 2 * H, DH]), T2("Sf1", [128, 2 * H, DH])]
                mq = [T2("mq0", [128, H, 64], BF16), T2("mq1", [128, H, 64], BF16)]
                mw = [T2("mw0", [128, H, 64], BF16), T2("mw1", [128, H, 64], BF16)]
                mg = [T2("mg0", [128, H, 64], BF16), T2("mg1", [128, H, 64], BF16)]
                mk = [T2("mk0", [64, H, DH], BF16), T2("mk1", [64, H, DH], BF16)]
                md = [T2("md0", [64, H, DH], BF16), T2("md1", [64, H, DH], BF16)]
                n = NS * DS
                r0 = LP
                assert NT % 2 == 1
                hb = hset[1]
                K = lambda nm: "%s1" % nm
                qdT, scT, vr, vk, kr = hb["qdT"], hb["scT"], hb["vr"], hb["vk"], hb["kr"]
                Pm, vb, nwkT, qdTg, qkT, kdec, CDt = hb["Pm"], hb["vb"], hb["nwkT"], hb["qdTg"], hb["qkT"], hb["kdec"], hb["CDt"]
                for s in range(NS):
                    b = s % 2
                    P.dma("pool", DMA(Sb[b][:, :, :], sall_in[s].rearrange("h d e -> d h e")), "Sb%d" % b, writes=["Sb%d" % b])
                    dve(TT(mq[b][:, :, :n], qdT[:, :, :n], bc_mid(mfm[:, s, :], 128, H), ALU.mult), r=[K("qdT"), "mfm"], w=["mq%d" % b])
                    dve(TT(mw[b][:, :, :n], nwkT[:, :, :n], bc_mid(mfm[:, s, :], 128, H), ALU.mult), r=[K("nwkT"), "mfm"], w=["mw%d" % b])
                    dve(TT(mg[b][:, :, :n], qdTg[:, :, :n], bc_mid(mfm[:, s, :], 128, H), ALU.mult), r=[K("qdTg"), "mfm"], w=["mg%d" % b])
                    for h in range(H):
                        hs = slice(h * 128, (h + 1) * 128)
                        first = (s == 0)
                        if first:
                            pe(MM(ps[5][:n, hs], scT[:n, h, :n], vr[:n, hs], start=(h == 0), stop=False, skip=True), r=[K("scT"), K("vr")], w=[PK(5)])
                            pe(MM(ps[1][:n, hs], Pm[:n, h, :n], vb[:n, h, :], start=(h == 0), stop=False, skip=True), r=[K("Pm"), K("vb")], w=[PK(1)])
                        pe(MM(ps[5][:n, hs], mq[b][:, h, :n], Sb[b][:, h, :], start=False, stop=(s == NS - 1), skip=True),
                           r=["mq%d" % b, "Sb%d" % b], w=[PK(5)])
                        pe(MM(ps[1][:n, hs], mw[b][:, h, :n], Sb[b][:, H + h, :], start=False, stop=(s == NS - 1), skip=True),
                           r=["mw%d" % b, "Sb%d" % b], w=[PK(1)])
                        pe(MM(ps[2][:n, hs], mg[b][:, h, :n], Sb[b][:, H + h, :], start=(first and h == 0), stop=False, skip=True),
                           r=["mg%d" % b, "Sb%d" % b], w=[PK(2)])
                act(ACT(w_sb[:n], ps[1][:n, :].rearrange("p (h d) -> p h d", h=H), AF.Copy), r=[PK(1)], w=["w_sb"])
                for h in range(H):
                    hs = slice(h * 128, (h + 1) * 128)
                    pe(MM(ps[2][:n, hs], qkT[:n, h, :n], w_sb[:n, h, :], start=False, stop=True, skip=True), r=[K("qkT"), "w_sb"], w=[PK(2)])
                def pass2(par):
                    for s in range(par, NS, 2):
                        b = s % 2
                        kr_ = kg_ = "Sf%d" % b
                        P.dma("sp", DMA(Sf[b][:, :, :], sall_in[s].rearrange("h d e -> d h e")), "Sfl%d" % b, writes=[kr_])
                        dve(TS(mk[b][:n], kr[:n], mtm[:n, s:s + 1], 0.0, ALU.mult, ALU.add), r=[K("kr"), "mtm"], w=["mk%d" % b])
                        dve(TS(md[b][:n], kdec[:n], mtm[:n, s:s + 1], 0.0, ALU.mult, ALU.add), r=[K("kdec"), "mtm"], w=["md%d" % b])
                        bank = 3 + (s % 2)
                        bank2 = 6 if (s % 2) else 7
                        for h in range(H):
                            hs = slice(h * 128, (h + 1) * 128)
                            pe(MM(ps[bank][:, hs], mk[b][:n, h, :], vk[:n, h, :]), r=["mk%d" % b, K("vk")], w=[PK(bank)])
                            pe(MM(ps[bank2][:, hs], md[b][:n, h, :], w_sb[:n, h, :]), r=["md%d" % b, "w_sb"], w=[PK(bank2)])
                        for h in range(H):
                            hs = slice(h * 128, (h + 1) * 128)
                            dve(STT(Sf[b][:, h, :], Sf[b][:, h, :], float(GAM[h] ** DS), ps[bank][:, hs], ALU.mult, ALU.add),
                                r=[kr_, PK(bank)], w=[kr_])
                            dve(STT(Sf[b][:, H + h, :], Sf[b][:, H + h, :], CDt[:, s * H + h:s * H + h + 1], ps[bank2][:, hs], ALU.mult, ALU.add),
                                r=[kg_, K("CDt"), PK(bank2)], w=[kg_])
                        P.dma("act", DMA(o_sall_s[s].rearrange("h d e -> d h e"), Sf[b][:, :, :]), "Sfs%d" % b, reads=[kr_])

                def rec_fn(f, *a_):
                    P.rec = []
                    f(*a_)
                    r_ = P.rec
                    P.rec = None
                    return r_

                P.merge([rec_fn(pass2, 0), rec_fn(pass2, 1)])
                ret_norm(n, 5, 1)
                gdn_norm(n, 2, 1)
                out_proj(r0, n, 1, 0, 3, 4)
                P.barrier()

        with contextlib.ExitStack() as st:
            def T(name, shape, dt=F32):
                return st.enter_context(nc.sbuf_tensor("sb_" + name, list(shape), dt))

            w_up_sb = T("w_up_sb", [128, 8, 2 * DFF], BF16)
            w_dn_sb = T("w_dn_sb", [128, 22, D], BF16)
            identB_f = T("identB_f", [128, 128])
            identB_b = T("identB_b", [128, 128], BF16)
            nfc = T("nfc", [128, 8])
            cwf = T("cwf", [128, 44, 3])
            nfinB = T("nfinB", [128, D])
            carryf = T("carryf", [128, 44, 2])
            xf = [T("xf0", [128, D]), T("xf1", [128, D])]
            xr = [T("xr0", [128, D]), T("xr1", [128, D])]
            h2n2 = [T("h2n0", [128, D], BF16), T("h2n1", [128, D], BF16)]
            h2T2 = [T("h2T0", [128, 8, 256], BF16), T("h2T1", [128, 8, 256], BF16)]
            actT2 = [T("actT0", [128, 22, 256], BF16), T("actT1", [128, 22, 256], BF16)]
            ubg2 = [T("ubg0", [128, 258]), T("ubg1", [128, 258])]
            ubv2 = [T("ubv0", [128, 258]), T("ubv1", [128, 258])]
            cg2 = [T("cg0", [128, 256]), T("cg1", [128, 256])]
            cvl2 = [T("cvl0", [128, 256]), T("cvl1", [128, 256])]
            ptmp2 = [[T("pta0", [128, 256]), T("ptb0", [128, 256])], [T("pta1", [128, 256]), T("ptb1", [128, 256])]]
            yt = T("yt", [128, D])
            smb = T("smb", [128, 8])
            sct = T("sct", [32, 256])
            stg = T("stg", [64, 512])

            P.dma("pool", DMA(identB_b[:], cd["ident"]), "idBb", writes=["identB_b"])
            for blk in (0, 5, 6, 1, 7, 2, 8, 3, 9, 4, 10):
                P.dma("pool", DMA(w_up_sb[:, :, blk * 512:(blk + 1) * 512],
                                  w_up[:, blk * 512:(blk + 1) * 512].rearrange("(k p) f -> p k f", p=128)),
                      "w_up%d" % blk, writes=[("w_up_sb", blk)])
            for hf, j0 in enumerate((0, 11)):
                P.dma("pool", DMA(w_dn_sb[:, j0:j0 + 11, :], w_down[j0 * 128:(j0 + 11) * 128, :].rearrange("(j p) m -> p j m", p=128)),
                      "w_dn%d" % hf, writes=[("w_dn_sb", hf)])
            for (o_, i_, key) in [(identB_f[:], cd["ident"], "identB_f"), (nfc[:], nf_col, "nfc"),
                                  (cwf[:].rearrange("p c i -> p (c i)"), cwf_in, "cwf"),
                                  (nfinB[:], nfin_in.partition_broadcast(128), "nfinB")]:
                P.dma("sp", DMA(o_, i_), "setupB", writes=[key])
            P.fix_group("setupB")
            dve(MSET(carryf[:], 0.0), w=["carryf"])

            def tiles_of(rg0, ncol):
                return [(rg0 + o, min(128, ncol - o), o) for o in range(0, ncol, 128)]

            glist = [(256 * gi, 256, False, False) for gi in range(LP // 256)]
            glist.append((256 * (LP // 256), LP - 256 * (LP // 256), False, True))
            glist.append((LP, NS * DS, True, False))

            def front_a(g):
                rg0, ncol, sample, last_prompt = glist[g]
                for ti, (rt0, nt, off) in enumerate(tiles_of(rg0, ncol)):
                    xk = "xf%d" % ti
                    P.dma("sp", DMA(xf[ti][:nt, :], x1s[rt0:rt0 + nt, :]), xk + "_ld", reads=["x1s"], writes=[xk])
                    dve(MSET(smb[:nt, 0:1], 0.0), w=["smbf"])
                    act(ACT(h2n2[ti][:nt, :], xf[ti][:nt, :], AF.Square, accum_out=smb[:nt, 0:1]), r=[xk], w=["h2n%d" % ti, "smbf"])
                    dve(TS(smb[:nt, 0:1], smb[:nt, 0:1], 1.0 / D, EPS, ALU.mult, ALU.add), r=["smbf"], w=["smbf"])
                    act(ACT(smb[:nt, 0:1], smb[:nt, 0:1], AF.Ln), r=["smbf"], w=["smbf"])
                    act(ACT(smb[:nt, 0:1], smb[:nt, 0:1], AF.Exp, scale=-0.5), r=["smbf"], w=["smbf"])
                    act(ACT(h2n2[ti][:nt, :], xf[ti][:nt, :], AF.Copy, scale=smb[:nt, 0:1]), r=[xk, "smbf"], w=["h2n%d" % ti])

            def front_b(g):
                rg0, ncol, sample, last_prompt = glist[g]
                h2T = h2T2[g % 2]
                for ti, (rt0, nt, off) in enumerate(tiles_of(rg0, ncol)):
                    for k in range(8):
                        pe(TR(psb[0][:, k * 128:k * 128 + nt], h2n2[ti][:nt, k * 128:(k + 1) * 128], identB_b[:nt, :nt]),
                           r=["h2n%d" % ti, "identB_b"], w=[PK(0)])
                    dve(TT(h2T[:, :, off:off + nt], psb[0][:].rearrange("p (k t) -> p k t", k=8)[:, :, :nt],
                           nfc[:, :].unsqueeze(2).to_broadcast([128, 8, nt]), ALU.mult), r=[PK(0), "nfc"], w=["h2T%d" % (g % 2)])

            def down_work(g):
                rg0, ncol, sample, last_prompt = glist[g]
                actT = actT2[g % 2]
                ak = "actT%d" % (g % 2)
                work = []
                for ti, (rt0, nt, off) in enumerate(tiles_of(rg0, ncol)):
                    xk = "xr%d" % ti

                    def ld(ti=ti, rt0=rt0, nt=nt, xk=xk):
                        P.dma("sp", DMA(xr[ti][:nt, :], x1s[rt0:rt0 + nt, :]), xk + "_ld", reads=["x1s"], writes=[xk])
                    work.append(ld)
                    for half in range(2):
                        bank = 5 + half
                        for j0 in range(0, 22, 4):
                            def mm(ti=ti, nt=nt, off=off, half=half, bank=bank, j0=j0):
                                for j in range(j0, min(j0 + 4, 22)):
                                    pe(MM(ps[bank][:nt, :], actT[:, j, off:off + nt], w_dn_sb[:, j, half * 512:(half + 1) * 512],
                                          start=(j == 0), stop=(j == 21)), r=[(ak, j), ("w_dn_sb", j // 11)], w=[PK(bank)])
                            work.append(mm)

                        def add(ti=ti, nt=nt, half=half, bank=bank, xk=xk):
                            dve(TT(xr[ti][:nt, half * 512:(half + 1) * 512], xr[ti][:nt, half * 512:(half + 1) * 512], ps[bank][:nt, :], ALU.add),
                                r=[xk, PK(bank)], w=[xk])
                        work.append(add)

                    def fin(ti=ti, rt0=rt0, nt=nt, xk=xk, sample=sample):
                        dve(MSET(smb[:nt, 1:2], 0.0), w=["smbd"])
                        act(ACT(yt[:nt, :], xr[ti][:nt, :], AF.Square, accum_out=smb[:nt, 1:2]), r=[xk], w=["yt", "smbd"])
                        dve(TS(smb[:nt, 1:2], smb[:nt, 1:2], 1.0 / D, EPS, ALU.mult, ALU.add), r=["smbd"], w=["smbd"])
                        act(ACT(smb[:nt, 1:2], smb[:nt, 1:2], AF.Ln), r=["smbd"], w=["smbd"])
                        act(ACT(smb[:nt, 1:2], smb[:nt, 1:2], AF.Exp, scale=-0.5), r=["smbd"], w=["smbd"])
                        act(ACT(yt[:nt, :], xr[ti][:nt, :], AF.Copy, scale=smb[:nt, 1:2]), r=[xk, "smbd"], w=["yt"])
                        dve(TT(yt[:nt, :], yt[:nt, :], nfinB[:nt, :], ALU.mult), r=["yt", "nfinB"], w=["yt"])
                        if sample:
                            P.dma("sp", DMA(y[SEQ:SEQ + nt, :], yt[:nt, :]), "yt_st", reads=["yt"])
                        else:
                            lo = max(rt0, NMETA)
                            hi = rt0 + nt
                            if hi > lo:
                                P.dma("sp", DMA(y[lo - NMETA:hi - NMETA, :], yt[lo - rt0:hi - rt0, :]), "yt_st", reads=["yt"])
                    work.append(fin)
                return work

            def up_group(g, what):
                rg0, ncol, sample, last_prompt = glist[g]
                h2T = h2T2[g % 2]
                h2k = "h2T%d" % (g % 2)
                actT = actT2[g % 2]
                ak = "actT%d" % (g % 2)
                if (sample or last_prompt) and what in ("tm", "all"):
                    for f in range(11):
                        for k in range(8):
                            pe(MM(ps[1][:ncol, :], h2T[:, k, :ncol], w_up_sb[:, k, f * 512:(f + 1) * 512], start=(k == 0), stop=(k == 7)),
                               r=[h2k, ("w_up_sb", f)], w=[PK(1)])
                        act(ACT(stg[:ncol, :], ps[1][:ncol, :], AF.Copy), r=[PK(1)], w=["stg"])
                        if sample:
                            for i in range(2):
                                P.dma("sp", DMA(o_cf_s[:, i, f * 512:(f + 1) * 512], stg[2 + i:64:4, :]), "stg_st", reads=["stg"])
                        else:
                            P.dma("sp", DMA(o_cf_p[:, f * 512:(f + 1) * 512], stg[ncol - 2:ncol, :]), "stg_st", reads=["stg"])

                def vw(ap2):
                    return ap2 if not sample else ap2.rearrange("p (s j) -> p s j", j=4)

                def stage1(j):
                    pj = j % 2
                    for (cidx, ub, bank, cdst, ubk, cdk, isg) in ((j, ubg2[pj], (2, 4)[pj], cg2[pj], "ubg%d" % pj, "cg%d" % pj, True),
                                                                  (22 + j, ubv2[pj], (3, 7)[pj], cvl2[pj], "ubv%d" % pj, "cvl%d" % pj, False)):
                        for k in range(8):
                            pe(MM(ps[bank][:, :ncol], w_up_sb[:, k, cidx * 128:(cidx + 1) * 128], h2T[:, k, :ncol], start=(k == 0), stop=(k == 7)),
                               r=[h2k, ("w_up_sb", cidx // 4)], w=[PK(bank)])
                        if not sample:
                            pool(CP(ub[:, 0:2], carryf[:, cidx, :]), r=[("carryf", cidx)], w=[(ubk, "c")])
                            act(ACT(ub[:, 2:2 + ncol], ps[bank][:, :ncol], AF.Copy), r=[PK(bank)], w=[(ubk, "m")])
                            pool(CP(carryf[:, cidx, :], ub[:, ncol:ncol + 2]), r=[(ubk, "m")], w=[("carryf", cidx)])
                            tp = [ub[:, i:i + ncol] for i in range(3)]
                        else:
                            ubv3 = ub[:, 0:96].rearrange("p (s j) -> p s j", j=6)
                            slot = pj * 2 + (0 if isg else 1)
                            sctv = h2n2[0].bitcast(F32)
                            sl_ = sctv[:32, slot * 128:(slot + 1) * 128]
                            skey = ("h2n0", slot)
                            P.dma("sp", DMA(sl_, scf_in[:, cidx * 128:(cidx + 1) * 128]), "sct_ld%d" % slot, writes=[skey])
                            pe(TR(ps[0][:, slot * 32:(slot + 1) * 32], sl_, identB_f[:32, :32]), r=[skey, "identB_f"], w=[PK(0)])
                            act(ACT(ubv3[:, :, 0:2], ps[0][:, slot * 32:(slot + 1) * 32].rearrange("p (s i) -> p s i", i=2), AF.Copy), r=[PK(0)], w=[ubk])
                            act(ACT(ubv3[:, :, 2:6], ps[bank][:, :ncol].rearrange("p (s j) -> p s j", j=4), AF.Copy), r=[PK(bank)], w=[ubk])
                            tp = [ubv3[:, :, i:i + 4] for i in range(3)]
                        o_ = vw(cdst[:, :ncol])
                        if isg:
                            act(ACT(o_, tp[0], AF.Copy, scale=cwf[:, cidx, 0:1]), r=[ubk, "cwf"], w=[cdk])
                            dve(STT(o_, tp[1], cwf[:, cidx, 1:2], o_, ALU.mult, ALU.add), r=[ubk, "cwf", cdk], w=[cdk])
                            dve(STT(o_, tp[2], cwf[:, cidx, 2:3], o_, ALU.mult, ALU.add), r=[ubk, "cwf", cdk], w=[cdk])
                        else:
                            pa, pb = vw(ptmp2[pj][0][:, :ncol]), vw(ptmp2[pj][1][:, :ncol])
                            act(ACT(pa, tp[0], AF.Copy, scale=cwf[:, cidx, 0:1]), r=[ubk, "cwf"], w=["pta%d" % pj])
                            pool(TS(pb, tp[1], cwf[:, cidx, 1:2], 0.0, ALU.mult, ALU.add), r=[ubk, "cwf"], w=["ptb%d" % pj])
                            dve(STT(o_, tp[2], cwf[:, cidx, 2:3], pa, ALU.mult, ALU.add), r=[ubk, "cwf", "pta%d" % pj], w=[cdk])
                            dve(TT(o_, o_, pb, ALU.add), r=[cdk, "ptb%d" % pj], w=[cdk])

                def stage2(j):
                    pj = j % 2
                    cg, cvl = cg2[pj], cvl2[pj]
                    act(ACT(cg[:, :ncol], cg[:, :ncol], AF.Silu), r=["cg%d" % pj], w=["cg%d" % pj])
                    dve(TT(actT[:, j, :ncol], cg[:, :ncol], cvl[:, :ncol], ALU.mult), r=["cg%d" % pj, "cvl%d" % pj], w=[(ak, j)])

                if what == "all":
                    for j in range(23):
                        if j < 22:
                            stage1(j)
                        if j >= 1:
                            stage2(j - 1)
                elif what in ("even", "odd"):
                    for j in range(0 if what == "even" else 1, 22, 2):
                        stage1(j)
                        stage2(j)

            def rec_call(f):
                P.rec = []
                f()
                r_ = P.rec
                P.rec = None
                return r_

            front_a(0)
            front_b(0)
            G_ = len(glist)
            for g in range(G_):
                lists = []
                if glist[g][2] or glist[g][3]:
                    lists.append(rec_call(lambda: up_group(g, "tm")))
                lists.append(rec_call(lambda: up_group(g, "even")))
                lists.append(rec_call(lambda: up_group(g, "odd")))
                if g > 0:
                    lists.append(rec_call(lambda: [w_() for w_ in down_work(g - 1)]))
                if g + 1 < G_:
                    lists.append(rec_call(lambda: (front_a(g + 1), front_b(g + 1))))
                P.merge(lists)
            for w_ in down_work(G_ - 1):
                w_()
            P.finalize()

        counts = P.emit(nc)
    return nc, consts, counts


_CACHE = {}


def kernel(x_prompt, x_sample, state_ret, state_gdn, state_conv_qkv, state_ffn_conv,
           meta_tokens, norm_mix, w_in, conv_gdn, gdn_a_log, gdn_dt_bias, norm_ret, norm_gdn, w_out,
           norm_ffn, w_up, conv_ffn, w_down, norm_final):
    f = lambda a: np.ascontiguousarray(np.asarray(a, dtype=np.float32))
    x_prompt, x_sample, state_ret, state_gdn = f(x_prompt), f(x_sample), f(state_ret), f(state_gdn)
    state_conv_qkv, state_ffn_conv, meta_tokens = f(state_conv_qkv), f(state_ffn_conv), f(meta_tokens)
    if "nc" not in _CACHE:
        _CACHE["nc"] = build_program()
    nc, consts, counts = _CACHE["nc"]
    shared = {
        "w_in": f(w_in)[0], "w_out": f(w_out)[0], "w_up": f(w_up)[0], "w_down": f(w_down)[0],
        "nm_col": f(f(norm_mix)[0].reshape(8, 128).T), "nf_col": f(f(norm_ffn)[0].reshape(8, 128).T),
        "cwg": f(f(conv_gdn)[0].reshape(4, 12, 128).transpose(2, 1, 0).reshape(128, 48)),
        "cwf": f(f(conv_ffn)[0].reshape(3, 44, 128).transpose(2, 1, 0).reshape(128, 132)),
        "alog": f(gdn_a_log)[0], "dtb": f(gdn_dt_bias)[0], "nret": f(norm_ret)[0], "ngdn": f(norm_gdn)[0],
        "nfin": f(norm_final),
    }
    for k, v in consts.items():
        shared["c_" + k] = v
    in_maps = []
    for c in range(8):
        m = dict(shared)
        m["x"] = np.concatenate([meta_tokens, x_prompt[c], x_sample[c * NS:(c + 1) * NS].reshape(NS * DS, D)], axis=0)
        m["sall"] = np.concatenate([state_ret[0, c * NS:(c + 1) * NS], state_gdn[0, c * NS:(c + 1) * NS]], axis=1)
        m["scq"] = state_conv_qkv[0, c * NS:(c + 1) * NS].reshape(NS * 3, 1536)
        m["scf"] = state_ffn_conv[0, c * NS:(c + 1) * NS].reshape(NS * 2, 2 * DFF)
        in_maps.append(m)
    res = run_bass_kernel_spmd(nc, in_maps, core_ids=list(range(8)))
    R = res.results
    y_prompt = np.stack([R[c]["y"][:SEQ] for c in range(8)])
    y_sample = np.concatenate([R[c]["y"][SEQ:].reshape(NS, DS, D) for c in range(8)])
    p_ret = np.stack([R[c]["o_sret_p"] for c in range(8)])[None]
    p_gdn = np.stack([R[c]["o_sgdn_p"] for c in range(8)])[None]
    p_cq = np.stack([R[c]["o_cq_p"] for c in range(8)])[None]
    p_cf = np.stack([R[c]["o_cf_p"] for c in range(8)])[None]
    s_ret = np.concatenate([R[c]["o_sall_s"][:, :H] for c in range(8)])[None]
    s_gdn = np.concatenate([R[c]["o_sall_s"][:, H:] for c in range(8)])[None]
    s_cq = np.concatenate([R[c]["o_cq_s"] for c in range(8)])[None]
    s_cf = np.concatenate([R[c]["o_cf_s"] for c in range(8)])[None]
    outs = (y_prompt, y_sample, p_ret, p_gdn, p_cq, p_cf, s_ret, s_gdn, s_cq, s_cf)
    return tuple(np.ascontiguousarray(o, dtype=np.float32) for o in outs)
```
